# Optimizing a Trainium2 kernel written in Bass

```python
import math
import jax, jax.numpy as jnp
from jax import lax
import numpy as np

D_MODEL = 1024
BATCH = 32
SEQ = 256
DEPTH = 2
DEC_BATCH = 8
DEC_SEQ = 1024
PAST_LEN = 256

F32 = jnp.float32
GRID_W = 64
POS_BASE = 10000.0
EPS = 1e-6
N_DIR = 2
D_MIX = D_MODEL
N_MIXERS = 4
GROUP_W = D_MIX // N_MIXERS
D_FF = -(-8 * D_MODEL // (3 * 256)) * 256

HY_W = GROUP_W
HY_IN = 3 * HY_W
HY_SHORT = 3
HY_BANDS = 16
HY_EMB = 1 + 2 * HY_BANDS
HY_HIDDEN = 64
HY_DECAY_TARGET = 1e-2
HY_FAST = 0.3
HY_SLOW = 1.5

SSD_HEADDIM = 64
SSD_INNER = GROUP_W
SSD_HEADS = SSD_INNER // SSD_HEADDIM
SSD_GROUPS = 2
SSD_STATE = 64
SSD_CONV = 4
SSD_CHUNK = 64
SSD_XBC = SSD_INNER + 2 * SSD_GROUPS * SSD_STATE
SSD_IN = SSD_INNER + SSD_XBC + N_DIR * SSD_HEADS

LRU_W = GROUP_W
LRU_HEADS = 4
LRU_HD = LRU_W // LRU_HEADS
LRU_CONV = 4
LRU_C = 8.0
LRU_IN = 2 * LRU_W

GDN_HEAD_DIM = 64
GDN_HEADS = GROUP_W // GDN_HEAD_DIM
GDN_CONV = 4
GDN_CHUNK = 64
GDN_IN = 4 * GROUP_W + 2 * N_DIR * GDN_HEADS

IN_W = HY_IN + SSD_IN + LRU_IN + GDN_IN
IN_SPLITS = (HY_IN, HY_IN + SSD_IN, HY_IN + SSD_IN + LRU_IN)

kernel_name = 'hybrid_bidir_diffusion_prefix_step'


def rmsnorm(x, g):
    xf = x.astype(F32)
    y = xf * lax.rsqrt(jnp.mean(xf * xf, axis=-1, keepdims=True) + EPS)
    return (y * g).astype(x.dtype)


def l2norm(x):
    return x * lax.rsqrt(jnp.sum(x * x, axis=-1, keepdims=True) + EPS)


def dwconv(x, w):
    K = w.shape[0]
    L = x.shape[1]
    left = K // 2
    xp = jnp.pad(x, ((0, 0), (left, K - 1 - left), (0, 0)))
    return sum(xp[:, j:j + L] * w[j] for j in range(K))


def grid_pos_embed(n_tokens):
    rows = n_tokens // GRID_W
    rr, cc = jnp.meshgrid(jnp.arange(rows, dtype=F32), jnp.arange(GRID_W, dtype=F32), indexing='ij')
    quarter = D_MODEL // 4
    omega = 1.0 / (POS_BASE ** (jnp.arange(quarter, dtype=F32) / quarter))
    def enc(pos):
        ang = pos.reshape(-1)[:, None] * omega[None, :]
        return jnp.concatenate([jnp.sin(ang), jnp.cos(ang)], axis=-1)
    return jnp.concatenate([enc(rr), enc(cc)], axis=-1)


def hyena_filters(L, p):
    t = jnp.linspace(0.0, 1.0, L, dtype=F32)[:, None]
    w = (2.0 * math.pi / L) * jnp.arange(L, dtype=F32)[:, None]
    bands = jnp.linspace(1e-4, HY_BANDS - 1, HY_BANDS, dtype=F32)[None, :]
    feats = jnp.concatenate([t, jnp.cos(bands * w), -jnp.sin(bands * w)], axis=-1)
    h = jnp.sin(p['hy_freq'][0] * (feats @ p['hy_w1'] + p['hy_b1']))
    h = jnp.sin(p['hy_freq'][1] * (h @ p['hy_w2'] + p['hy_b2']))
    h = (h @ p['hy_w3']).reshape(L, N_DIR, HY_W).astype(F32)
    max_decay = math.log(HY_DECAY_TARGET) / HY_FAST
    min_decay = math.log(HY_DECAY_TARGET) / HY_SLOW
    deltas = jnp.abs(jnp.linspace(min_decay, max_decay, HY_W, dtype=F32))
    h = h * jnp.exp(-t * deltas)[:, None, :]
    return h / jnp.sum(jnp.abs(h), axis=(0, 1), keepdims=True)


def hyena_mixer(u, p):
    L = u.shape[1]
    x0, x1, v = jnp.split(dwconv(u, p['hy_conv']), 3, axis=-1)
    filt = hyena_filters(L, p)
    z = v * x1
    n = 2 * L
    def conv_fft(s, h):
        spec = jnp.fft.rfft(s, n=n, axis=1) * jnp.fft.rfft(h, n=n, axis=0)[None]
        return jnp.fft.irfft(spec, n=n, axis=1)[:, :L]
    y = conv_fft(z, filt[:, 0]) + conv_fft(z[:, ::-1], filt[:, 1])[:, ::-1] + z * p['hy_bias']
    return x0 * y


def segsum(a):
    T = a.shape[-1]
    cs = jnp.cumsum(a, axis=-1)
    diff = cs[..., :, None] - cs[..., None, :]
    return jnp.where(jnp.tril(jnp.ones((T, T), dtype=bool)), diff, -jnp.inf)


def ssd_scan(x, a, b, c, s0):
    bsz, L, H, P = x.shape
    N = b.shape[-1]
    nc = L // SSD_CHUNK
    x = x.reshape(bsz, nc, SSD_CHUNK, H, P)
    b = b.reshape(bsz, nc, SSD_CHUNK, H, N)
    c = c.reshape(bsz, nc, SSD_CHUNK, H, N)
    a = a.reshape(bsz, nc, SSD_CHUNK, H).transpose(0, 3, 1, 2)
    acs = jnp.cumsum(a, axis=-1)
    y_diag = jnp.einsum('bclhn,bcshn,bhcls,bcshp->bclhp', c, b, jnp.exp(segsum(a)), x)
    decay_in = jnp.exp(acs[..., -1:] - acs)
    chunk_states = jnp.einsum('bclhn,bhcl,bclhp->bchpn', b, decay_in, x)
    chunk_states = jnp.concatenate([s0[:, None], chunk_states], axis=1)
    chunk_decay = jnp.exp(segsum(jnp.pad(acs[..., -1], ((0, 0), (0, 0), (1, 0)))))
    states = jnp.einsum('bhzc,bchpn->bzhpn', chunk_decay, chunk_states)
    y_off = jnp.einsum('bclhn,bchpn,bhcl->bclhp', c, states[:, :-1], jnp.exp(acs))
    return (y_diag + y_off).reshape(bsz, L, H, P), states[:, -1]


def ssd_mixer(u, s0, p):
    bsz, L, _ = u.shape
    s0 = s0.astype(F32)
    z, xbc, dt_raw = jnp.split(u, [SSD_INNER, SSD_INNER + SSD_XBC], axis=-1)
    xbc = jax.nn.silu(dwconv(xbc, p['ssd_conv']))
    xs, bm, cm = jnp.split(xbc, [SSD_INNER, SSD_INNER + SSD_GROUPS * SSD_STATE], axis=-1)
    xs = xs.reshape(bsz, L, SSD_HEADS, SSD_HEADDIM)
    rep = SSD_HEADS // SSD_GROUPS
    bm = jnp.repeat(bm.reshape(bsz, L, SSD_GROUPS, SSD_STATE), rep, axis=2)
    cm = jnp.repeat(cm.reshape(bsz, L, SSD_GROUPS, SSD_STATE), rep, axis=2)
    dt = jax.nn.softplus(dt_raw.reshape(bsz, L, N_DIR, SSD_HEADS) + p['ssd_dt_bias'])
    a = -jnp.exp(p['ssd_a_log']) * dt
    xdt = xs[:, :, None] * dt[..., None]
    y_f, s_f = ssd_scan(xdt[:, :, 0], a[:, :, 0], bm, cm, s0[:, 0])
    y_b, s_b = ssd_scan(xdt[:, ::-1, 1], a[:, ::-1, 1], bm[:, ::-1], cm[:, ::-1], s0[:, 1])
    y = y_f + y_b[:, ::-1] + xs * p['ssd_d'][:, None]
    y = y.reshape(bsz, L, SSD_INNER) * jax.nn.silu(z)
    return rmsnorm(y, p['ssd_norm']), jnp.stack([s_f, s_b], axis=1)


def linear_scan(a, b, h0):
    b = b.at[:, 0].add(a[:, 0] * h0)
    def combine(left, right):
        return left[0] * right[0], right[0] * left[1] + right[1]
    return lax.associative_scan(combine, (a, b), axis=1)[1]


def rglru_mixer(u, s0, p):
    bsz, L, _ = u.shape
    s0 = s0.astype(F32)
    xr, gate = jnp.split(u, 2, axis=-1)
    xc = dwconv(xr, p['lru_conv'])
    xh = xc.reshape(bsz, L, LRU_HEADS, LRU_HD)
    r = jax.nn.sigmoid(jnp.einsum('blhi,dhij->bldhj', xh, p['lru_w_r']).reshape(bsz, L, N_DIR, LRU_W) + p['lru_b_r'])
    i = jax.nn.sigmoid(jnp.einsum('blhi,dhij->bldhj', xh, p['lru_w_i']).reshape(bsz, L, N_DIR, LRU_W) + p['lru_b_i'])
    log_a = -LRU_C * r * jax.nn.softplus(-p['lru_lambda'])
    a = jnp.exp(log_a)
    bterm = jnp.sqrt(-jnp.expm1(2.0 * log_a)) * i * xc[:, :, None]
    h_f = linear_scan(a[:, :, 0], bterm[:, :, 0], s0[:, 0])
    h_b = linear_scan(a[:, ::-1, 1], bterm[:, ::-1, 1], s0[:, 1])
    y = (h_f + h_b[:, ::-1]) * jax.nn.gelu(gate)
    return y, jnp.stack([h_f[:, -1], h_b[:, -1]], axis=1)


def gated_delta_chunked(q, k, v, g, beta, s0):
    bsz, H, L, _ = q.shape
    Dv = v.shape[-1]
    C = GDN_CHUNK
    nc = L // C
    q, k, v = (t.reshape(bsz, H, nc, C, -1) for t in (q, k, v))
    gc = jnp.cumsum(g.reshape(bsz, H, nc, C), axis=-1)
    beta = beta.reshape(bsz, H, nc, C, 1)
    incl = jnp.tril(jnp.ones((C, C), dtype=bool))
    strict = jnp.tril(jnp.ones((C, C), dtype=bool), k=-1)
    diff = gc[..., :, None] - gc[..., None, :]
    decay = jnp.where(incl, jnp.exp(jnp.where(incl, diff, 0.0)), 0.0)
    kb = k * beta
    a_mat = jnp.where(strict, jnp.einsum('bhncd,bhnsd->bhncs', kb, k) * decay, 0.0)
    rhs = jnp.concatenate([v * beta, kb * jnp.exp(gc)[..., None]], axis=-1)
    sol = lax.linalg.triangular_solve(a_mat + jnp.eye(C, dtype=a_mat.dtype), rhs,
                                      left_side=True, lower=True, unit_diagonal=True)
    u, w = sol[..., :Dv], sol[..., Dv:]
    attn = jnp.where(incl, jnp.einsum('bhncd,bhnsd->bhncs', q, k) * decay, 0.0)
    def step(s, inp):
        q_i, k_i, u_i, w_i, gc_i, attn_i = inp
        v_new = u_i - jnp.einsum('bhcd,bhde->bhce', w_i, s)
        o = (jnp.einsum('bhcd,bhde->bhce', q_i * jnp.exp(gc_i)[..., None], s)
             + jnp.einsum('bhcs,bhse->bhce', attn_i, v_new))
        g_last = gc_i[..., -1:]
        s = (s * jnp.exp(g_last)[..., None]
             + jnp.einsum('bhcd,bhce->bhde', k_i * jnp.exp(g_last - gc_i)[..., None], v_new))
        return s, o
    xs = tuple(jnp.moveaxis(t, 2, 0) for t in (q, k, u, w, gc, attn))
    s_final, o = lax.scan(step, s0, xs)
    return jnp.moveaxis(o, 0, 2).reshape(bsz, H, L, Dv), s_final


def gdn_mixer(u, s0, p):
    bsz, L, _ = u.shape
    s0 = s0.astype(F32)
    qkv, zg, b_raw, a_raw = jnp.split(u, [3 * GROUP_W, 4 * GROUP_W, 4 * GROUP_W + N_DIR * GDN_HEADS], axis=-1)
    q, k, v = jnp.split(jax.nn.silu(dwconv(qkv, p['gdn_conv'])), 3, axis=-1)
    heads = lambda t: t.reshape(bsz, L, GDN_HEADS, GDN_HEAD_DIM).transpose(0, 2, 1, 3)
    q = l2norm(heads(q)) * (GDN_HEAD_DIM ** -0.5)
    k = l2norm(heads(k))
    v = heads(v)
    beta = jax.nn.sigmoid(b_raw.reshape(bsz, L, N_DIR, GDN_HEADS)).transpose(0, 2, 3, 1)
    g = (-jnp.exp(p['gdn_a_log'])
         * jax.nn.softplus(a_raw.reshape(bsz, L, N_DIR, GDN_HEADS) + p['gdn_dt_bias'])).transpose(0, 2, 3, 1)
    o_f, s_f = gated_delta_chunked(q, k, v, g[:, 0], beta[:, 0], s0[:, 0])
    rev = lambda t: t[:, :, ::-1]
    o_b, s_b = gated_delta_chunked(rev(q), rev(k), rev(v), rev(g[:, 1]), rev(beta[:, 1]), s0[:, 1])
    o = (o_f + rev(o_b)).transpose(0, 2, 1, 3)
    o = rmsnorm(o, p['gdn_norm']) * jax.nn.silu(zg.reshape(bsz, L, GDN_HEADS, GDN_HEAD_DIM))
    return o.reshape(bsz, L, GROUP_W), jnp.stack([s_f, s_b], axis=1)


def trunk_layer(x, mod, s_lru, s_ssd, s_gdn, p):
    sh_m, sc_m, ga_m, sh_f, sc_f, ga_f = jnp.split(mod, 6, axis=-1)
    h = rmsnorm(x, p['g_mix']) * (1 + sc_m) + sh_m
    proj = (h @ p['w_in']).astype(F32)
    u_hy, u_ssd, u_lru, u_gdn = jnp.split(proj, IN_SPLITS, axis=-1)
    o_hy = hyena_mixer(u_hy, p)
    o_ssd, s_ssd = ssd_mixer(u_ssd, s_ssd, p)
    o_lru, s_lru = rglru_mixer(u_lru, s_lru, p)
    o_gdn, s_gdn = gdn_mixer(u_gdn, s_gdn, p)
    mixed = jnp.concatenate([o_hy, o_ssd, o_lru, o_gdn], axis=-1).astype(x.dtype)
    x = x + ga_m * (mixed @ p['w_out'])
    h = rmsnorm(x, p['g_ffn']) * (1 + sc_f) + sh_f
    x = x + ga_f * ((jax.nn.silu(h @ p['w_gate']) * (h @ p['w_up'])) @ p['w_down'])
    return x, s_lru, s_ssd, s_gdn


def setup_inputs(seed: int = 0) -> dict:
    key = jax.random.key(seed)
    keys = jax.random.split(key, 64)
    ctr = [0]
    def nk():
        ctr[0] += 1
        return keys[ctr[0] - 1]
    def nrm(shape, scale=1.0):
        return scale * jax.random.normal(nk(), shape, F32)
    def unif(shape, lo, hi):
        return jax.random.uniform(nk(), shape, F32, lo, hi)
    def gain(shape):
        return 1.0 + nrm(shape, 0.01)
    def dt_bias(shape):
        dt = jnp.exp(unif(shape, math.log(1e-3), math.log(1e-1)))
        return dt + jnp.log(-jnp.expm1(-dt))
    a0 = unif((DEPTH, N_DIR, LRU_W), 0.9, 0.999)
    s_lam = a0 ** (1.0 / LRU_C)
    return {
        'x_prompt': nrm((BATCH, SEQ, D_MODEL)),
        'x_sample': nrm((DEC_BATCH, DEC_SEQ, D_MODEL)),
        'state_lru': nrm((DEC_BATCH, DEPTH, N_DIR, LRU_W), 0.5),
        'state_ssd': nrm((DEC_BATCH, DEPTH, N_DIR, SSD_HEADS, SSD_HEADDIM, SSD_STATE), 0.3),
        'state_gdn': nrm((DEC_BATCH, DEPTH, N_DIR, GDN_HEADS, GDN_HEAD_DIM, GDN_HEAD_DIM), 0.3),
        'c': nrm((DEC_BATCH, D_MODEL)),
        'c_ctx': nrm((D_MODEL,)),
        'w_mod': nrm((DEPTH, D_MODEL, 6 * D_MODEL), 0.5 * D_MODEL ** -0.5),
        'b_mod': nrm((DEPTH, 6 * D_MODEL), 0.01),
        'g_mix': gain((DEPTH, D_MODEL)),
        'g_ffn': gain((DEPTH, D_MODEL)),
        'g_final': gain((D_MODEL,)),
        'w_in': nrm((DEPTH, D_MODEL, IN_W), D_MODEL ** -0.5),
        'w_out': nrm((DEPTH, D_MIX, D_MODEL), D_MIX ** -0.5),
        'hy_conv': nrm((DEPTH, HY_SHORT, HY_IN), HY_SHORT ** -0.5),
        'hy_w1': nrm((DEPTH, HY_EMB, HY_HIDDEN), HY_EMB ** -0.5),
        'hy_b1': nrm((DEPTH, HY_HIDDEN), 0.02),
        'hy_w2': nrm((DEPTH, HY_HIDDEN, HY_HIDDEN), HY_HIDDEN ** -0.5),
        'hy_b2': nrm((DEPTH, HY_HIDDEN), 0.02),
        'hy_w3': nrm((DEPTH, HY_HIDDEN, N_DIR * HY_W), HY_HIDDEN ** -0.5),
        'hy_freq': gain((DEPTH, 2, HY_HIDDEN)),
        'hy_bias': nrm((DEPTH, HY_W)),
        'ssd_conv': nrm((DEPTH, SSD_CONV, SSD_XBC), SSD_CONV ** -0.5),
        'ssd_dt_bias': dt_bias((DEPTH, N_DIR, SSD_HEADS)),
        'ssd_a_log': jnp.log(unif((DEPTH, N_DIR, SSD_HEADS), 1.0, 16.0)),
        'ssd_d': gain((DEPTH, SSD_HEADS)),
        'ssd_norm': gain((DEPTH, SSD_INNER)),
        'lru_conv': nrm((DEPTH, LRU_CONV, LRU_W), LRU_CONV ** -0.5),
        'lru_w_r': nrm((DEPTH, N_DIR, LRU_HEADS, LRU_HD, LRU_HD), LRU_HD ** -0.5),
        'lru_b_r': nrm((DEPTH, N_DIR, LRU_W), 0.01),
        'lru_w_i': nrm((DEPTH, N_DIR, LRU_HEADS, LRU_HD, LRU_HD), LRU_HD ** -0.5),
        'lru_b_i': nrm((DEPTH, N_DIR, LRU_W), 0.01),
        'lru_lambda': jnp.log(s_lam) - jnp.log1p(-s_lam),
        'gdn_conv': nrm((DEPTH, GDN_CONV, 3 * GROUP_W), GDN_CONV ** -0.5),
        'gdn_dt_bias': dt_bias((DEPTH, N_DIR, GDN_HEADS)),
        'gdn_a_log': jnp.log(unif((DEPTH, N_DIR, GDN_HEADS), 1.0, 16.0)),
        'gdn_norm': gain((DEPTH, GDN_HEAD_DIM)),
        'w_gate': nrm((DEPTH, D_MODEL, D_FF), D_MODEL ** -0.5),
        'w_up': nrm((DEPTH, D_MODEL, D_FF), D_MODEL ** -0.5),
        'w_down': nrm((DEPTH, D_FF, D_MODEL), D_FF ** -0.5),
    }


def reference(x_prompt, x_sample, state_lru, state_ssd, state_gdn, c, c_ctx, w_mod, b_mod,
              g_mix, g_ffn, g_final, w_in, w_out, hy_conv, hy_w1, hy_b1, hy_w2, hy_b2, hy_w3,
              hy_freq, hy_bias, ssd_conv, ssd_dt_bias, ssd_a_log, ssd_d, ssd_norm, lru_conv,
              lru_w_r, lru_b_r, lru_w_i, lru_b_i, lru_lambda, gdn_conv, gdn_dt_bias, gdn_a_log,
              gdn_norm, w_gate, w_up, w_down):
    b_ctx = x_prompt.shape[0]
    n_lat = x_sample.shape[1]
    x_ctx = x_prompt
    x_lat = x_sample + grid_pos_embed(n_lat).astype(x_sample.dtype)
    zero_lru = jnp.zeros((b_ctx, N_DIR, LRU_W), F32)
    zero_ssd = jnp.zeros((b_ctx, N_DIR, SSD_HEADS, SSD_HEADDIM, SSD_STATE), F32)
    zero_gdn = jnp.zeros((b_ctx, N_DIR, GDN_HEADS, GDN_HEAD_DIM, GDN_HEAD_DIM), F32)
    lru_out, ssd_out, gdn_out = [], [], []
    for l in range(DEPTH):
        p = {
            'g_mix': g_mix[l], 'g_ffn': g_ffn[l], 'w_in': w_in[l], 'w_out': w_out[l],
            'hy_conv': hy_conv[l], 'hy_w1': hy_w1[l], 'hy_b1': hy_b1[l], 'hy_w2': hy_w2[l],
            'hy_b2': hy_b2[l], 'hy_w3': hy_w3[l], 'hy_freq': hy_freq[l], 'hy_bias': hy_bias[l],
            'ssd_conv': ssd_conv[l], 'ssd_dt_bias': ssd_dt_bias[l], 'ssd_a_log': ssd_a_log[l],
            'ssd_d': ssd_d[l], 'ssd_norm': ssd_norm[l],
            'lru_conv': lru_conv[l], 'lru_w_r': lru_w_r[l], 'lru_b_r': lru_b_r[l],
            'lru_w_i': lru_w_i[l], 'lru_b_i': lru_b_i[l], 'lru_lambda': lru_lambda[l],
            'gdn_conv': gdn_conv[l], 'gdn_dt_bias': gdn_dt_bias[l], 'gdn_a_log': gdn_a_log[l],
            'gdn_norm': gdn_norm[l],
            'w_gate': w_gate[l], 'w_up': w_up[l], 'w_down': w_down[l],
        }
        mod_ctx = (jax.nn.silu(c_ctx)[None] @ w_mod[l] + b_mod[l])[:, None].astype(x_ctx.dtype)
        mod_lat = (jax.nn.silu(c) @ w_mod[l] + b_mod[l])[:, None].astype(x_lat.dtype)
        x_ctx, s_lru, s_ssd, s_gdn = trunk_layer(x_ctx, mod_ctx, zero_lru, zero_ssd, zero_gdn, p)
        lru_out.append(s_lru)
        ssd_out.append(s_ssd)
        gdn_out.append(s_gdn)
        x_lat, _, _, _ = trunk_layer(x_lat, mod_lat, state_lru[:, l], state_ssd[:, l], state_gdn[:, l], p)
    y_prompt = rmsnorm(x_ctx, g_final)
    y_sample = rmsnorm(x_lat, g_final)
    new_state_lru = jnp.stack(lru_out, axis=1)
    new_state_ssd = jnp.stack(ssd_out, axis=1)
    new_state_gdn = jnp.stack(gdn_out, axis=1)
    return (y_prompt, y_sample, new_state_lru, new_state_ssd, new_state_gdn)
```

```python
import numpy as np
import concourse.bass as bass
import concourse.mybir as mybir
from concourse.bass_utils import run_bass_kernel_spmd
from contextlib import ExitStack

F32 = mybir.dt.float32
BF16 = mybir.dt.bfloat16
AF = mybir.ActivationFunctionType
ALU = mybir.AluOpType
AX = mybir.AxisListType

ENGS = ("pe", "act", "dve", "pool", "sp")
NDMA = 12


class Prog:
    def __init__(self, nc):
        self.nc = nc
        self.ops = {e: [] for e in ENGS}
        self.last_write = {}
        self.readers = {}
        self.known = {e: {} for e in ENGS}
        self.dma_val = [0] * NDMA
        self.dma_next = 0
        self.es = ExitStack()
        self.sb_bytes = 0

    def sb(self, name, shape, dt=F32):
        t = self.es.enter_context(self.nc.sbuf_tensor("sb_" + name, list(shape), dt))
        n = 1
        for s in shape[1:]:
            n *= s
        self.sb_bytes += n * (2 if dt == BF16 else 4)
        return t

    def ps(self, name, shape, dt=F32):
        return self.es.enter_context(self.nc.psum_tensor("pz_" + name, list(shape), dt))

    def op(self, eng, fn, reads=(), writes=(), dma=False):
        if eng != "pe":
            px = [kk for kk in reads if isinstance(kk, tuple) and kk[0] == "ps"]
            if px:
                writes = list(writes) + [kk for kk in px if kk not in writes]
        deps = []
        raw = set()
        for k in reads:
            t = self.last_write.get(k)
            if t is not None:
                deps.append(t)
                raw.add(t)
        for k in writes:
            t = self.last_write.get(k)
            if t is not None:
                deps.append(t)
            deps.extend(self.readers.get(k, ()))
        waits = []
        kn = self.known[eng]
        mx = {}
        for (stream, idx) in deps:
            if mx.get(stream, 0) < idx:
                mx[stream] = idx
        deps = list(mx.items())
        for t in deps:
            stream, idx = t
            if stream == eng:
                if eng == "pe":
                    continue
            if kn.get(stream, 0) >= idx:
                continue
            kn[stream] = idx
            waits.append(t)
        rec = {"waits": waits, "fn": fn, "signal": False, "dma": None}
        if dma:
            j = self.dma_next % NDMA
            self.dma_next += 1
            prev = self.dma_val[j]
            st = ("d", j)
            if prev > 0 and kn.get(st, 0) < prev:
                kn[st] = prev
                waits.append((st, prev))
            val = prev + 16
            self.dma_val[j] = val
            rec["dma"] = (j, val)
            tok = (st, val)
        else:
            tok = (eng, len(self.ops[eng]) + 1)
        self.ops[eng].append(rec)
        for (stream, idx) in waits:
            if not isinstance(stream, tuple):
                self.ops[stream][idx - 1]["signal"] = True
        for k in writes:
            self.last_write[k] = tok
            self.readers[k] = []
        for k in reads:
            self.readers.setdefault(k, []).append(tok)
        return tok

    def finish(self):
        waits = []
        for j in range(NDMA):
            if self.dma_val[j] > 0:
                waits.append((("d", j), self.dma_val[j]))
        self.ops["sp"].append({"waits": waits, "fn": None, "signal": False, "dma": None})

    def emit(self):
        nc = self.nc
        sems = {e: self.es.enter_context(nc.semaphore("s_" + e)) for e in ENGS}
        dsems = [self.es.enter_context(nc.semaphore("d_%d" % j)) for j in range(NDMA)]
        rank = {}
        for e in ENGS:
            r = [0]
            c = 0
            for o in self.ops[e]:
                if o["signal"]:
                    c += 1
                r.append(c)
            rank[e] = r

        def run(e, eng):
            for o in self.ops[e]:
                for (stream, idx) in o["waits"]:
                    if isinstance(stream, tuple):
                        eng.wait_ge(dsems[stream[1]], idx)
                    else:
                        eng.wait_ge(sems[stream], rank[stream][idx])
                if o["fn"] is None:
                    continue
                ins = o["fn"](eng)
                if o["dma"] is not None:
                    ins.then_inc(dsems[o["dma"][0]], 16)
                elif o["signal"]:
                    ins.then_inc(sems[e], 1)

        with nc.Block() as block:
            @block.tensor
            def _(eng):
                run("pe", eng)

            @block.scalar
            def _(eng):
                run("act", eng)

            @block.vector
            def _(eng):
                run("dve", eng)

            @block.gpsimd
            def _(eng):
                run("pool", eng)

            @block.sync
            def _(eng):
                run("sp", eng)
import math

T = 1024
EPS = 1e-6
D_FF = 2816
NHC = 22
C_HY = 0
C_SSD = 768
C_LRU = 1544
C_GDN = 2056


class K:
    pass


def build(debug=None, stop_after=None):
    nc = bass.Bass("TRN2", target_bir_lowering=False)
    P = Prog(nc)
    k = K()
    k.nc, k.P = nc, P
    k.debug = debug or []
    k.dbg_out = {}

    def din(name, shape, dt=F32):
        return nc.dram_tensor(name, list(shape), dt, kind="ExternalInput").ap()

    def dout(name, shape, dt=F32):
        return nc.dram_tensor(name, list(shape), dt, kind="ExternalOutput").ap()

    k.din, k.dout = din, dout
    D = {}
    D["x0"] = din("x_ctx", [T, 1024])
    D["x1"] = din("x_lat", [T, 1024])
    D["pos"] = din("pos", [T, 1024])
    D["cv"] = din("cv", [128, 2, 8])
    D["ident"] = din("ident", [128, 128])
    D["w_mod"] = din("w_mod", [2, 1024, 6144])
    D["b_mod_fm"] = din("b_mod_fm", [128, 2, 48])
    D["g_mix_fm"] = din("g_mix_fm", [128, 2, 8])
    D["g_ffn_fm"] = din("g_ffn_fm", [128, 2, 8])
    D["g_final"] = din("g_final", [1, 1024])
    D["w_in"] = din("w_in", [2, 1024, 3096])
    D["w_out"] = din("w_out", [2, 1024, 1024])
    D["w_gate"] = din("w_gate", [2, 1024, D_FF])
    D["w_up"] = din("w_up", [2, 1024, D_FF])
    D["w_down"] = din("w_down", [2, D_FF, 1024])
    D["lru_conv_fm"] = din("lru_conv_fm", [128, 2, 2, 4])
    D["lru_wbd"] = din("lru_wbd", [2, 2, 2, 2, 128, 128])
    D["lru_b_fm"] = din("lru_b_fm", [128, 2, 2, 2, 2])
    D["lru_lam_fm"] = din("lru_lam_fm", [128, 2, 2, 2])
    D["st_lru_fm"] = din("st_lru_fm", [128, 2, 2, 2])
    D["masks"] = din("masks", [128, 6, 128])
    D["masks2"] = din("masks2", [128, 2, 256])
    D["hy_conv_fm"] = din("hy_conv_fm", [128, 2, 6, 3])
    D["hy_w1"] = din("hy_w1", [2, 33, 64])
    D["hy_w2"] = din("hy_w2", [2, 64, 64])
    D["hy_w3"] = din("hy_w3", [2, 64, 512])
    D["hy_vec"] = din("hy_vec", [64, 2, 4])
    D["hy_bias_fm"] = din("hy_bias_fm", [128, 2, 2])
    for L_ in (256, 1024):
        D["hy_featsT_%d" % L_] = din("hy_featsT_%d" % L_, [33, L_])
        D["hy_env_%d" % L_] = din("hy_env_%d" % L_, [L_, 256])
        D["dftF_%d" % L_] = din("dftF_%d" % L_, [L_, L_ // 128, 2, 128], BF16)
        D["dftI_%d" % L_] = din("dftI_%d" % L_, [L_, 2, L_], BF16)
    D["masks4"] = din("masks4", [128, 4, 128])
    D["gdn_conv_fm"] = din("gdn_conv_fm", [128, 2, 6, 4])
    D["gdn_dtb_bc"] = din("gdn_dtb_bc", [128, 2, 8])
    D["gdn_alog_bc"] = din("gdn_alog_bc", [128, 2, 8])
    D["gdn_norm_bc"] = din("gdn_norm_bc", [128, 2, 64])
    D["st_gdn"] = din("st_gdn", [2, 2, 4, 64, 64])
    D["ns_gdn"] = dout("ns_gdn", [4, 2, 2, 4, 64, 64])
    D["ssd_conv_fm"] = din("ssd_conv_fm", [128, 2, 4, 4])
    D["ssd_dtb_bc"] = din("ssd_dtb_bc", [128, 2, 8])
    D["ssd_alog_bc"] = din("ssd_alog_bc", [128, 2, 8])
    D["ssd_d_fm"] = din("ssd_d_fm", [128, 2, 2])
    D["ssd_norm_fm"] = din("ssd_norm_fm", [128, 2, 2])
    D["st_ssd"] = din("st_ssd", [2, 2, 4, 64, 64])
    D["ns_ssd"] = dout("ns_ssd", [4, 2, 2, 4, 64, 64])
    for l_ in range(2):
        for L_ in (256, 1024):
            D["hyf_%d_%d" % (l_, L_)] = nc.dram_tensor("hyf_%d_%d" % (l_, L_), [2, 128, (L_ // 128) * 256], BF16, kind="Internal").ap()
    D["y0"] = dout("y_ctx", [T, 1024])
    D["y1"] = dout("y_lat", [T, 1024])
    D["ns_lru"] = dout("ns_lru", [32, 128])
    k.D = D

    k.xT = P.sb("xT", [128, 8, T])
    k.hT = P.sb("hT", [128, 8, T], BF16)
    k.mixT = P.sb("mixT", [128, 8, T], BF16)
    k.NBIG = 19
    k.BIG = P.sb("BIG", [128, k.NBIG, T])
    k.ident = P.sb("ident", [128, 128])
    k.ones_bf = P.sb("ones_bf", [128, 128], BF16)
    k.ident_bf = P.sb("ident_bf", [128, 128], BF16)
    k.masks = P.sb("masks", [128, 6, 128])
    k.masks2 = P.sb("masks2", [128, 2, 256])
    k.masks4 = P.sb("masks4", [128, 4, 128])
    k.modT = P.sb("modT", [128, 96, 2])
    k.MODS = P.sb("MODS", [128, 2, 2, 6, 8])
    k.cvs = P.sb("cvs", [128, 2, 8], BF16)
    k.cvf = P.sb("cvf", [128, 2, 8])
    k.small = P.sb("small", [128, 256])
    k.gfin = P.sb("gfin", [128, 1024])
    k.NWB = 3
    k.wb = [P.sb("wb%d" % i, [128, 2048], BF16) for i in range(k.NWB)]
    k.NSTG = 2
    k.stg = [P.sb("stg%d" % i, [128, 2048]) for i in range(k.NSTG)]
    k.psum = [P.ps("ps%d" % i, [128, 512]) for i in range(8)]
    k.psi = 0

    k.ps_held = set()

    def psn(hold=False):
        while (k.psi % 8) in k.ps_held:
            k.psi += 1
        i = k.psi % 8
        k.psi += 1
        if hold:
            k.ps_held.add(i)
        return k.psum[i], ("ps", i)

    k.psn = psn

    k.sched = []
    k.w_issued = 0
    k.w_consumed = 0

    def w_issue():
        i = k.w_issued
        b = i % k.NWB
        s = i % k.NSTG
        parts, nel = k.sched[i](k.stg[s])
        for pi, (o, a) in enumerate(parts):
            P.op("sp", lambda e, o=o, a=a: e.dma_start(out=o, in_=a), writes=[("stg", s, pi)], dma=True)
        ceng = "act"
        rk = [("stg", s, pi) for pi in range(len(parts))]
        if ceng == "act":
            P.op("act", lambda e, b=b, s=s, nel=nel: e.activation(out=k.wb[b][:, 0:nel], in_=k.stg[s][:, 0:nel], func=AF.Copy), reads=rk, writes=[("wb", b)])
        else:
            P.op(ceng, lambda e, b=b, s=s, nel=nel: e.tensor_copy(out=k.wb[b][:, 0:nel], in_=k.stg[s][:, 0:nel]), reads=rk, writes=[("wb", b)])
        k.w_keys[i] = [("wb", b)]
        k.w_issued += 1

    k.w_keys = {}

    def w_next():
        while k.w_issued < min(len(k.sched), k.w_consumed + k.NWB):
            w_issue()
        i = k.w_consumed
        k.w_consumed += 1
        return k.wb[i % k.NWB], k.w_keys[i]

    k.w_next = w_next

    def dbg(name, ap, shape, reads):
        if name not in k.debug:
            return
        o = dout("dbg_" + name, shape, ap.dtype)
        P.op("sp", lambda e: e.dma_start(out=o, in_=ap), reads=reads, dma=True)

    k.dbg = dbg

    build_schedule(k)
    prologue(k)
    for g in range(2):
        group(k, g, stop_after)
    P.finish()
    P.emit()
    print("SBUF bytes/partition:", P.sb_bytes, "ops:", {e: len(P.ops[e]) for e in ENGS})
    return nc, k


def gran_cols(wname, l, c0, n, nk=8):
    def loader(buf, k_=None):
        raise NotImplementedError
    return (wname, l, c0, n, nk)


def build_schedule(k):
    D = k.D
    sched = []

    def cols(wname, l, c0, n, nk=8, k0=0):
        def loader(buf):
            src = D[wname][l, k0 * 128:(k0 + nk) * 128, c0:c0 + n].rearrange("(kc p) c -> p kc c", p=128)
            dst = buf[:, 0:nk * n].rearrange("p (kc c) -> p kc c", c=n)
            return [(dst, src)], nk * n
        return loader

    def cols2(w1, w2, l, c0, n):
        def loader(buf):
            out = []
            for j, wn in enumerate((w1, w2)):
                src = D[wn][l, :, c0:c0 + n].rearrange("(kc p) c -> p kc c", p=128)
                dst = buf[:, j * 8 * n:(j + 1) * 8 * n].rearrange("p (kc c) -> p kc c", c=n)
                out.append((dst, src))
            return out, 16 * n
        return loader

    for l in range(2):
        for gi in range(24):
            sched.append(cols("w_mod", l, gi * 256, 256))
    for g in range(2):
        for l in range(2):
            for (c0, n) in win_granules():
                sched.append(cols("w_in", l, c0, n))
            for gi in range(4):
                sched.append(cols("w_out", l, gi * 256, 256))
            for gi in range(22):
                sched.append(cols2("w_gate", "w_up", l, gi * 128, 128))
            for mc in range(8):
                for kh in range(2):
                    sched.append(cols("w_down", l, mc * 128, 128, nk=11, k0=kh * 11))
    k.sched = sched


def win_granules():
    gr = []
    gr += [(0, 256), (256, 256), (512, 256)]
    gr += [(C_LRU, 256), (C_LRU + 256, 256)]
    gr += [(768, 256), (1024, 256), (1280, 256), (1536, 8)]
    gr += [(2056, 256), (2312, 256), (2568, 256), (2824, 256), (3080, 16)]
    return gr


def prologue(k):
    P, D = k.P, k.D
    P.op("sp", lambda e: e.dma_start(out=k.ident[:], in_=D["ident"][:, :]), writes=["ident"], dma=True)
    P.op("sp", lambda e: e.dma_start(out=k.cvf[:], in_=D["cv"][:, :, :]), writes=["cvf"], dma=True)
    P.op("sp", lambda e: e.dma_start(out=k.small[:, 0:96], in_=D["b_mod_fm"].rearrange("p l c -> p (l c)")), writes=["small"], dma=True)
    P.op("sp", lambda e: e.dma_start(out=k.small[:, 96:112], in_=D["g_mix_fm"].rearrange("p l c -> p (l c)")), writes=["small_g"], dma=True)
    P.op("sp", lambda e: e.dma_start(out=k.small[:, 112:128], in_=D["g_ffn_fm"].rearrange("p l c -> p (l c)")), writes=["small_g2"], dma=True)
    P.op("sp", lambda e: e.dma_start(out=k.gfin[:], in_=D["g_final"][0, :].partition_broadcast(128)), writes=["gfin"], dma=True)
    P.op("dve", lambda e: e.memset(k.ones_bf[:], 1.0), writes=["ones"])
    P.op("sp", lambda e: e.dma_start(out=k.masks[:], in_=D["masks"][:, :, :]), writes=["masks"], dma=True)
    P.op("sp", lambda e: e.dma_start(out=k.masks2[:], in_=D["masks2"][:, :, :]), writes=["masks2"], dma=True)
    P.op("sp", lambda e: e.dma_start(out=k.masks4[:], in_=D["masks4"][:, :, :]), writes=["masks4"], dma=True)
    P.op("dve", lambda e: e.tensor_copy(out=k.ident_bf[:], in_=k.ident[:]), reads=["ident"], writes=["ident_bf"])
    P.op("act", lambda e: e.activation(out=k.cvs[:], in_=k.cvf[:], func=AF.Silu), reads=["cvf"], writes=["cvs"])
    ps, pk = k.psn(hold=True)
    defer = []
    P.op = lambda *a, **kw: defer.append((a, kw))
    for l in range(2):
        for L_ in (256, 1024):
            hyena_filter(k, l, L_)
    del P.op
    per = (len(defer) + 47) // 48

    def replay(n):
        for _ in range(n):
            if defer:
                a, kw = defer.pop(0)
                P.op(*a, **kw)

    for l in range(2):
        for gi in range(24):
            wb, wk = k.w_next()
            w3 = wb[:, 0:2048].rearrange("p (kc c) -> p kc c", c=256)
            for mc in range(2):
                oc = l * 48 + gi * 2 + mc
                for kc in range(8):
                    P.op("pe", lambda e, oc=oc, kc=kc, mc=mc, w3=w3: e.matmul(
                        ps[:, oc * 2:oc * 2 + 2], lhsT=w3[:, kc, mc * 128:(mc + 1) * 128], rhs=k.cvs[:, :, kc],
                        start=(kc == 0), stop=(kc == 7)), reads=wk + ["cvs"], writes=[pk])
            replay(per)
    replay(len(defer))
    k.ps_held.discard(pk[1])
    P.op("dve", lambda e: e.tensor_tensor(out=k.modT[:], in0=ps[:, 0:192].rearrange("p (c g) -> p c g", g=2),
                                          in1=k.small[:, 0:96].unsqueeze(2).to_broadcast([128, 96, 2]), op=ALU.add),
         reads=[pk, "small"], writes=["modT"])
    for l in range(2):
        for g in range(2):
            P.op("dve", lambda e, l=l, g=g: e.tensor_copy(
                out=k.MODS[:, l, g, :, :], in_=k.modT[:, l * 48:(l + 1) * 48, g].rearrange("p (a b) -> p a b", b=8)),
                reads=["modT"], writes=[("MODS", l, g)])
            for (idx, off) in ((1, 96), (4, 112)):
                P.op("dve", lambda e, l=l, g=g, idx=idx, off=off: e.scalar_tensor_tensor(
                    out=k.MODS[:, l, g, idx, :], in0=k.MODS[:, l, g, idx, :], scalar=1.0,
                    in1=k.small[:, off + l * 8:off + l * 8 + 8], op0=ALU.add, op1=ALU.mult),
                    reads=[("MODS", l, g), "small_g", "small_g2"], writes=[("MODS", l, g)])
    k.dbg("mods", k.MODS[:].rearrange("p a b c d -> p (a b c d)"), [128, 192], [("MODS", l, g) for l in range(2) for g in range(2)])


def big(k, j):
    return k.BIG[:, j, :]


def bigkey(j):
    return ("BIG", j)


def load_x(k, g):
    P, D = k.P, k.D
    xin = D["x%d" % g]
    for tt in range(8):
        slot = tt % 2
        st = big(k, slot)
        P.op("sp", lambda e, tt=tt, st=st: e.dma_start(out=st, in_=xin[tt * 128:(tt + 1) * 128, :]), writes=[bigkey(slot)], dma=True)
        if g == 1:
            pslot = 2 + tt % 2
            pt = big(k, pslot)
            P.op("sp", lambda e, tt=tt, pt=pt: e.dma_start(out=pt, in_=D["pos"][tt * 128:(tt + 1) * 128, :]), writes=[bigkey(pslot)], dma=True)
            P.op("dve", lambda e, st=st, pt=pt: e.tensor_tensor(out=st, in0=st, in1=pt, op=ALU.add),
                 reads=[bigkey(slot), bigkey(pslot)], writes=[bigkey(slot)])
        for half in range(2):
            ps, pk = k.psn()
            for j in range(4):
                kc = half * 4 + j
                P.op("pe", lambda e, ps=ps, j=j, kc=kc, st=st: e.transpose(ps[:, j * 128:(j + 1) * 128], st[:, kc * 128:(kc + 1) * 128], k.ident[:]),
                     reads=[bigkey(slot), "ident"], writes=[pk])
            eng = "act" if half == 0 else "dve"
            dst = k.xT[:, half * 4:half * 4 + 4, tt * 128:(tt + 1) * 128]
            src = ps[:, :].rearrange("p (j t) -> p j t", t=128)
            if eng == "act":
                P.op("act", lambda e, dst=dst, src=src: e.activation(out=dst, in_=src, func=AF.Copy), reads=[pk], writes=[("xT", tt // 4)])
            else:
                P.op("dve", lambda e, dst=dst, src=src: e.tensor_copy(out=dst, in_=src), reads=[pk], writes=[("xT", tt // 4)])


def xkeys():
    return [("xT", 0), ("xT", 1)]


def norm_mod(k, l, g, ia, ib):
    P = k.P
    NS = k.NBIG
    sqv = k.BIG[:, NS - 2:NS, :].bitcast(BF16).rearrange("p a (b t) -> p (a b) t", b=4)
    sqk = [bigkey(NS - 2), bigkey(NS - 1)]
    rs = big(k, NS - 3)[:, 0:512]
    rsk = bigkey(NS - 3)
    for tg in range(2):
        ts = slice(tg * 512, (tg + 1) * 512)
        P.op("act", lambda e, ts=ts: e.activation(out=sqv, in_=k.xT[:, :, ts], func=AF.Square), reads=[("xT", tg)], writes=sqk)
        ps, pk = k.psn()
        for kc in range(8):
            P.op("pe", lambda e, kc=kc, ps=ps: e.matmul(ps[:, :], lhsT=k.ones_bf[:, :], rhs=sqv[:, kc, :], start=(kc == 0), stop=(kc == 7)),
                 reads=sqk + ["ones"], writes=[pk])
        P.op("act", lambda e, ps=ps: e.activation(out=rs, in_=ps[:, :], func=AF.Ln, scale=1.0 / 1024, bias=EPS), reads=[pk], writes=[rsk])
        P.op("act", lambda e: e.activation(out=rs, in_=rs, func=AF.Exp, scale=-0.5), reads=[rsk], writes=[rsk])
        for kc in range(8):
            tslot = NS - 5 + (kc % 2)
            tmp = big(k, tslot)[:, 0:512]
            P.op("dve", lambda e, kc=kc, tmp=tmp, ts=ts: e.scalar_tensor_tensor(
                out=tmp, in0=k.xT[:, kc, ts], scalar=k.MODS[:, l, g, ia, kc:kc + 1], in1=rs, op0=ALU.mult, op1=ALU.mult),
                reads=[("xT", tg), rsk, ("MODS", l, g)], writes=[bigkey(tslot)])
            P.op("act", lambda e, kc=kc, tmp=tmp, ts=ts: e.activation(
                out=k.hT[:, kc, ts], in_=tmp, func=AF.Identity, bias=k.MODS[:, l, g, ib, kc:kc + 1]),
                reads=[bigkey(tslot), ("MODS", l, g)], writes=[("hT", tg)])


def hkeys():
    return [("hT", 0), ("hT", 1)]


def proj_fm(k, w3, wk, mcols, consume):
    P = k.P
    for i, (m0, mw) in enumerate(mcols):
        pss = [k.psn() for _ in range(2)]
        for kc in range(8):
            for tg in range(2):
                ps, pk = pss[tg]
                P.op("pe", lambda e, ps=ps, kc=kc, tg=tg, m0=m0, mw=mw: e.matmul(
                    ps[0:mw, :], lhsT=w3[:, kc, m0:m0 + mw], rhs=k.hT[:, kc, tg * 512:(tg + 1) * 512],
                    start=(kc == 0), stop=(kc == 7)), reads=wk + [("hT", tg)], writes=[pk])
        for tg in range(2):
            ps, pk = pss[tg]
            consume(i, tg, ps, pk, mw)


def evac_to_big(k, slot_of):
    P = k.P
    cnt = [0]

    def consume(i, tg, ps, pk, mw):
        slot = slot_of(i)
        dst = k.BIG[0:mw, slot, tg * 512:(tg + 1) * 512]
        if cnt[0] % 2 == 0:
            P.op("act", lambda e: e.activation(out=dst, in_=ps[0:mw, :], func=AF.Copy), reads=[pk], writes=[bigkey(slot)])
        else:
            P.op("dve", lambda e: e.tensor_copy(out=dst, in_=ps[0:mw, :]), reads=[pk], writes=[bigkey(slot)])
        cnt[0] += 1
    return consume


def dwconv_fm(k, src_slot, dst_slot, wtile, wkey, ktaps, nseq, L, eng_first="act"):
    P = k.P
    left = ktaps // 2
    src3 = big(k, src_slot).rearrange("p (s t) -> p s t", t=L)
    dst3 = big(k, dst_slot).rearrange("p (s t) -> p s t", t=L)
    P.op("dve", lambda e: e.tensor_scalar(out=big(k, dst_slot), in0=big(k, src_slot), scalar1=wtile[:, left:left + 1], scalar2=None, op0=ALU.mult),
         reads=[bigkey(src_slot), wkey], writes=[bigkey(dst_slot)])
    for j in range(ktaps):
        if j == left:
            continue
        sh = j - left
        if sh < 0:
            o = dst3[:, :, -sh:L]
            i = src3[:, :, 0:L + sh]
        else:
            o = dst3[:, :, 0:L - sh]
            i = src3[:, :, sh:L]
        P.op("dve", lambda e, o=o, i=i, j=j: e.scalar_tensor_tensor(out=o, in0=i, scalar=wtile[:, j:j + 1], in1=o, op0=ALU.mult, op1=ALU.add),
             reads=[bigkey(src_slot), bigkey(dst_slot), wkey], writes=[bigkey(dst_slot)])


def lru_mixer(k, l, g, nseq, L):
    P, D = k.P, k.D
    cw = P_lru(k)
    P.op("sp", lambda e: e.dma_start(out=cw["conv"][:], in_=D["lru_conv_fm"][:, l, :, :]), writes=["lru_conv"], dma=True)
    P.op("sp", lambda e: e.dma_start(out=cw["b"][:], in_=D["lru_b_fm"][:, l, :, :, :]), writes=["lru_b"], dma=True)
    P.op("sp", lambda e: e.dma_start(out=cw["lam"][:], in_=D["lru_lam_fm"][:, l, :, :]), writes=["lru_lam"], dma=True)
    P.op("sp", lambda e: e.dma_start(out=cw["s0"][:], in_=D["st_lru_fm"][:, l, :, :]), writes=["lru_s0"], dma=True)
    for ri in range(2):
        for d in range(2):
            for q in range(2):
                P.op("sp", lambda e, ri=ri, d=d, q=q: e.dma_start(out=cw["w"][:, ri, d, q, :], in_=D["lru_wbd"][l, ri, d, q, :, :]),
                     writes=[("lru_w", ri, d, q)], dma=True)
    P.op("act", lambda e: e.activation(out=cw["c"][:], in_=cw["lam"][:], func=AF.Exp, scale=-1.0), reads=["lru_lam"], writes=["lru_c"])
    P.op("act", lambda e: e.activation(out=cw["c"][:], in_=cw["c"][:], func=AF.Ln, bias=1.0), reads=["lru_c"], writes=["lru_c"])
    P.op("dve", lambda e: e.tensor_scalar(out=cw["c"][:], in0=cw["c"][:], scalar1=-8.0, scalar2=None, op0=ALU.mult), reads=["lru_c"], writes=["lru_c"])
    for gi in range(2):
        wb, wk = k.w_next()
        w3 = wb[:, 0:2048].rearrange("p (kc c) -> p kc c", c=256)
        proj_fm(k, w3, wk, [(0, 128), (128, 128)], evac_to_big(k, lambda i, gi=gi: gi * 2 + i))
    k.dbg("lru_u%d%d" % (l, g), k.BIG[:, 0:4, :], [128, 4, T], [bigkey(j) for j in range(4)])
    S = 8
    for q in range(2):
        xc = S + 0
        dwconv_fm(k, q, xc, cw["conv"][:, q, :], "lru_conv", 4, nseq, L)
        hs = []
        for d in range(2):
            t1, t2, t3, hd = S + 1, S + 2, S + 3, S + 4 + d
            for ri, dst in ((0, t1), (1, t2)):
                for tg in range(2):
                    ps, pk = k.psn()
                    P.op("pe", lambda e, ps=ps, ri=ri, d=d, q=q, tg=tg: e.matmul(
                        ps[:, :], lhsT=cw["w"][:, ri, d, q, :], rhs=big(k, xc)[:, tg * 512:(tg + 1) * 512], start=True, stop=True),
                        reads=[bigkey(xc), ("lru_w", ri, d, q)], writes=[pk])
                    P.op("act", lambda e, ps=ps, ri=ri, d=d, q=q, tg=tg, dst=dst: e.activation(
                        out=big(k, dst)[:, tg * 512:(tg + 1) * 512], in_=ps[:, :], func=AF.Sigmoid, bias=cw["b"][:, ri, d, q:q + 1]),
                        reads=[pk, "lru_b"], writes=[bigkey(dst)])
            P.op("act", lambda e, d=d, q=q: e.activation(out=big(k, t1), in_=big(k, t1), func=AF.Exp, scale=cw["c"][:, d, q:q + 1]),
                 reads=[bigkey(t1), "lru_c"], writes=[bigkey(t1)])
            P.op("act", lambda e: e.activation(out=big(k, t3), in_=big(k, t1), func=AF.Square), reads=[bigkey(t1)], writes=[bigkey(t3)])
            P.op("dve", lambda e: e.tensor_scalar(out=big(k, t3), in0=big(k, t3), scalar1=1.0, scalar2=None, op0=ALU.min), reads=[bigkey(t3)], writes=[bigkey(t3)])
            P.op("act", lambda e: e.activation(out=big(k, t3), in_=big(k, t3), func=AF.Sqrt, scale=-1.0, bias=1.0), reads=[bigkey(t3)], writes=[bigkey(t3)])
            P.op("dve", lambda e: e.tensor_tensor(out=big(k, t3), in0=big(k, t3), in1=big(k, t2), op=ALU.mult),
                 reads=[bigkey(t3), bigkey(t2)], writes=[bigkey(t3)])
            P.op("dve", lambda e: e.tensor_tensor(out=big(k, t3), in0=big(k, t3), in1=big(k, xc), op=ALU.mult),
                 reads=[bigkey(t3), bigkey(xc)], writes=[bigkey(t3)])
            for s in range(nseq):
                sl = slice(s * L, (s + 1) * L)
                a_ap = big(k, t1)[:, sl]
                b_ap = big(k, t3)[:, sl]
                o_ap = big(k, hd)[:, sl]
                if d == 1:
                    a_ap, b_ap, o_ap = a_ap[:, ::-1], b_ap[:, ::-1], o_ap[:, ::-1]
                init = 0.0 if g == 0 else cw["s0"][:, d, q:q + 1]
                P.op("dve", lambda e, a_ap=a_ap, b_ap=b_ap, o_ap=o_ap, init=init: e.tensor_tensor_scan(
                    out=o_ap, data0=a_ap, data1=b_ap, initial=init, op0=ALU.mult, op1=ALU.add),
                    reads=[bigkey(t1), bigkey(t3), "lru_s0"], writes=[bigkey(hd)])
                if g == 0:
                    col = ((s * 2 + l) * 2 + d) * 2 + q
                    last = (s + 1) * L - 1 if d == 0 else s * L
                    P.op("dve", lambda e, col=col, last=last, hd=hd: e.tensor_copy(out=k.nsl[:, col:col + 1], in_=big(k, hd)[:, last:last + 1]),
                         reads=[bigkey(hd)], writes=["nsl"])
        h0, h1 = S + 4, S + 5
        P.op("dve", lambda e: e.tensor_tensor(out=big(k, h0), in0=big(k, h0), in1=big(k, h1), op=ALU.add),
             reads=[bigkey(h0), bigkey(h1)], writes=[bigkey(h0)])
        P.op("act", lambda e, q=q: e.activation(out=big(k, h1), in_=big(k, 2 + q), func=AF.Gelu_apprx_tanh), reads=[bigkey(2 + q)], writes=[bigkey(h1)])
        P.op("dve", lambda e, q=q: e.tensor_tensor(out=k.mixT[:, 4 + q, :], in0=big(k, h0), in1=big(k, h1), op=ALU.mult),
             reads=[bigkey(h0), bigkey(h1)], writes=[("mixT", 4 + q)])


def P_lru(k):
    if hasattr(k, "_lru"):
        return k._lru
    P = k.P
    k._lru = {
        "conv": P.sb("lru_conv", [128, 2, 4]),
        "b": P.sb("lru_b", [128, 2, 2, 2]),
        "lam": P.sb("lru_lam", [128, 2, 2]),
        "c": P.sb("lru_c", [128, 2, 2]),
        "s0": P.sb("lru_s0", [128, 2, 2]),
        "w": P.sb("lru_w", [128, 2, 2, 2, 128]),
    }
    k.nsl = P.sb("nsl", [128, 32])
    return k._lru


def w_out_res(k, l, g):
    P = k.P
    for gi in range(4):
        wb, wk = k.w_next()
        w3 = wb[:, 0:2048].rearrange("p (kc c) -> p kc c", c=256)
        for mcl in range(2):
            mc = gi * 2 + mcl
            pss = [k.psn() for _ in range(2)]
            for kc in range(8):
                for tg in range(2):
                    ps, pk = pss[tg]
                    P.op("pe", lambda e, ps=ps, kc=kc, tg=tg, mcl=mcl, w3=w3: e.matmul(
                        ps[:, :], lhsT=w3[:, kc, mcl * 128:(mcl + 1) * 128], rhs=k.mixT[:, kc, tg * 512:(tg + 1) * 512],
                        start=(kc == 0), stop=(kc == 7)), reads=wk + [("mixT", kc)], writes=[pk])
            for tg in range(2):
                ps, pk = pss[tg]
                xs = k.xT[:, mc, tg * 512:(tg + 1) * 512]
                P.op("dve", lambda e, ps=ps, xs=xs, mc=mc: e.scalar_tensor_tensor(
                    out=xs, in0=ps[:, :], scalar=k.MODS[:, l, g, 2, mc:mc + 1], in1=xs, op0=ALU.mult, op1=ALU.add),
                    reads=[pk, ("xT", tg), ("MODS", l, g)], writes=[("xT", tg)])


def ffn(k, l, g):
    P = k.P
    actT = k.BIG[:, 0:11, :].bitcast(BF16).rearrange("p a (b t) -> p (a b) t", b=2)
    akeys = [bigkey(j) for j in range(11)]
    sil = [big(k, 11)[:, 0:512], big(k, 12)[:, 0:512]]
    for gi in range(22):
        wb, wk = k.w_next()
        wg = wb[:, 0:1024].rearrange("p (kc c) -> p kc c", c=128)
        wu = wb[:, 1024:2048].rearrange("p (kc c) -> p kc c", c=128)
        for hcl in range(1):
            hc = gi
            for tg in range(2):
                (pg, pgk), (pu, puk) = k.psn(), k.psn()
                for kc in range(8):
                    P.op("pe", lambda e, pg=pg, kc=kc, tg=tg, hcl=hcl, wg=wg: e.matmul(
                        pg[:, :], lhsT=wg[:, kc, hcl * 128:(hcl + 1) * 128], rhs=k.hT[:, kc, tg * 512:(tg + 1) * 512],
                        start=(kc == 0), stop=(kc == 7)), reads=wk + [("hT", tg)], writes=[pgk])
                for kc in range(8):
                    P.op("pe", lambda e, pu=pu, kc=kc, tg=tg, hcl=hcl, wu=wu: e.matmul(
                        pu[:, :], lhsT=wu[:, kc, hcl * 128:(hcl + 1) * 128], rhs=k.hT[:, kc, tg * 512:(tg + 1) * 512],
                        start=(kc == 0), stop=(kc == 7)), reads=wk + [("hT", tg)], writes=[puk])
                sj = (hc * 2 + tg) % 2
                P.op("act", lambda e, pg=pg, sj=sj: e.activation(out=sil[sj], in_=pg[:, :], func=AF.Silu), reads=[pgk], writes=[bigkey(11 + sj)])
                P.op("dve", lambda e, pu=pu, sj=sj, hc=hc, tg=tg: e.tensor_tensor(
                    out=actT[:, hc, tg * 512:(tg + 1) * 512], in0=sil[sj], in1=pu[:, :], op=ALU.mult),
                    reads=[puk, bigkey(11 + sj)], writes=[("actT", tg)] + akeys)
    for mc in range(8):
        pss = [k.psn() for _ in range(2)]
        for kh in range(2):
            wb, wk = k.w_next()
            w3 = wb[:, 0:11 * 128].rearrange("p (kc c) -> p kc c", c=128)
            for hcl in range(11):
                hc = kh * 11 + hcl
                for tg in range(2):
                    ps, pk = pss[tg]
                    P.op("pe", lambda e, ps=ps, hc=hc, hcl=hcl, tg=tg, w3=w3: e.matmul(
                        ps[:, :], lhsT=w3[:, hcl, :], rhs=actT[:, hc, tg * 512:(tg + 1) * 512],
                        start=(hc == 0), stop=(hc == NHC - 1)), reads=wk + [("actT", tg)] + akeys, writes=[pk])
        for tg in range(2):
            ps, pk = pss[tg]
            xs = k.xT[:, mc, tg * 512:(tg + 1) * 512]
            P.op("dve", lambda e, ps=ps, xs=xs, mc=mc: e.scalar_tensor_tensor(
                out=xs, in0=ps[:, :], scalar=k.MODS[:, l, g, 5, mc:mc + 1], in1=xs, op0=ALU.mult, op1=ALU.add),
                reads=[pk, ("xT", tg), ("MODS", l, g)], writes=[("xT", tg)])


def final_out(k, g):
    P, D = k.P, k.D
    yout = D["y%d" % g]
    for tt in range(8):
        slot = tt % 2
        st = big(k, slot)
        for half in range(2):
            ps, pk = k.psn()
            for j in range(4):
                kc = half * 4 + j
                P.op("pe", lambda e, ps=ps, j=j, kc=kc, tt=tt: e.transpose(ps[:, j * 128:(j + 1) * 128], k.xT[:, kc, tt * 128:(tt + 1) * 128], k.ident[:]),
                     reads=[("xT", tt // 4), "ident"], writes=[pk])
            dst = st[:, half * 512:(half + 1) * 512]
            if half == 0:
                P.op("act", lambda e, dst=dst, ps=ps: e.activation(out=dst, in_=ps[:, :], func=AF.Copy), reads=[pk], writes=[bigkey(slot)])
            else:
                P.op("dve", lambda e, dst=dst, ps=ps: e.tensor_copy(out=dst, in_=ps[:, :]), reads=[pk], writes=[bigkey(slot)])
        sq = big(k, 2 + slot)
        ss = k.fin_ss[:, tt:tt + 1]
        P.op("act", lambda e, st=st, sq=sq, ss=ss: e.activation(out=sq, in_=st, func=AF.Square, accum_out=ss),
             reads=[bigkey(slot)], writes=[bigkey(2 + slot), ("fss", tt)])
        P.op("act", lambda e, ss=ss: e.activation(out=ss, in_=ss, func=AF.Ln, scale=1.0 / 1024, bias=EPS), reads=[("fss", tt)], writes=[("fss", tt)])
        P.op("act", lambda e, ss=ss: e.activation(out=ss, in_=ss, func=AF.Exp, scale=-0.5), reads=[("fss", tt)], writes=[("fss", tt)])
        P.op("dve", lambda e, st=st, sq=sq, ss=ss: e.scalar_tensor_tensor(out=sq, in0=st, scalar=ss, in1=k.gfin[:], op0=ALU.mult, op1=ALU.mult),
             reads=[bigkey(slot), ("fss", tt), "gfin", bigkey(2 + slot)], writes=[bigkey(2 + slot)])
        P.op("sp", lambda e, sq=sq, tt=tt: e.dma_start(out=yout[tt * 128:(tt + 1) * 128, :], in_=sq), reads=[bigkey(2 + slot)], dma=True)


def group(k, g, stop_after=None):
    P, D = k.P, k.D
    nseq, L = (4, 256) if g == 0 else (1, 1024)
    if not hasattr(k, "fin_ss"):
        k.fin_ss = P.sb("fin_ss", [128, 8])
    load_x(k, g)
    k.dbg("xT%d" % g, k.xT[:], [128, 8, T], xkeys())
    for l in range(2):
        norm_mod(k, l, g, 1, 0)
        k.dbg("hT%d%d" % (l, g), k.hT[:], [128, 8, T], hkeys())
        hyena_mixer(k, l, g, nseq, L)
        lru_mixer(k, l, g, nseq, L)
        ssd_mixer(k, l, g, nseq, L)
        gdn_mixer(k, l, g, nseq, L)
        k.dbg("mixT%d%d" % (l, g), k.mixT[:], [128, 8, T], [("mixT", j) for j in range(8)])
        w_out_res(k, l, g)
        k.dbg("xmid%d%d" % (l, g), k.xT[:], [128, 8, T], xkeys())
        norm_mod(k, l, g, 4, 3)
        ffn(k, l, g)
        k.dbg("xout%d%d" % (l, g), k.xT[:], [128, 8, T], xkeys())
    final_out(k, g)
    if g == 0:
        ps, pk = k.psn()
        P.op("pe", lambda e: e.transpose(ps[0:32, 0:128], k.nsl[:, 0:32], k.ident[:]), reads=["nsl", "ident"], writes=[pk])
        st = big(k, 4)[0:32, 0:128]
        P.op("act", lambda e: e.activation(out=st, in_=ps[0:32, 0:128], func=AF.Copy), reads=[pk], writes=[bigkey(4)])
        P.op("sp", lambda e: e.dma_start(out=D["ns_lru"][:, :], in_=st), reads=[bigkey(4)], dma=True)
def bfv(k, slot):
    return big(k, slot).bitcast(BF16)


def P_ssd(k):
    if hasattr(k, "_ssd"):
        return k._ssd
    P = k.P
    k._ssd = {
        "conv": P.sb("ssd_conv", [128, 4, 4]),
        "dtb": P.sb("ssd_dtb", [128, 8]),
        "nA": P.sb("ssd_nA", [128, 8]),
        "dfm": P.sb("ssd_dfm", [128, 2]),
        "nfm": P.sb("ssd_nfm", [128, 2]),
        "dt": P.sb("ssd_dt", [128, 8, 8]),
        "a": P.sb("ssd_a", [128, 8, 8]),
        "eacs": P.sb("ssd_eacs", [128, 8, 8]),
        "dAe": P.sb("ssd_dAe", [128, 8, 4]),
        "ST": P.sb("ssd_ST", [128, 256]),
        "s0": P.sb("ssd_s0", [64, 8, 64]),
        "fin": P.sb("ssd_fin", [64, 4, 64]),
    }
    return k._ssd


def ssd_mixer(k, l, g, nseq, L):
    P, D = k.P, k.D
    c = P_ssd(k)
    import os
    PH0 = int(os.environ.get("SSD_PHASE", "9"))
    if PH0 < -1:
        for _ in range(4):
            k.w_next()
        return
    M = k.masks
    LE, GE, GT, LT, ONES = (M[:, j, :] for j in range(5))
    P.op("sp", lambda e: e.dma_start(out=c["conv"][:], in_=D["ssd_conv_fm"][:, l, :, :]), writes=["ssd_conv"], dma=True)
    P.op("sp", lambda e: e.dma_start(out=c["dtb"][:], in_=D["ssd_dtb_bc"][:, l, :]), writes=["ssd_dtb"], dma=True)
    P.op("sp", lambda e: e.dma_start(out=c["nA"][:], in_=D["ssd_alog_bc"][:, l, :]), writes=["ssd_nA"], dma=True)
    P.op("sp", lambda e: e.dma_start(out=c["dfm"][:], in_=D["ssd_d_fm"][:, l, :]), writes=["ssd_dfm"], dma=True)
    P.op("sp", lambda e: e.dma_start(out=c["nfm"][:], in_=D["ssd_norm_fm"][:, l, :]), writes=["ssd_nfm"], dma=True)
    P.op("act", lambda e: e.activation(out=c["nA"][:], in_=c["nA"][:], func=AF.Exp), reads=["ssd_nA"], writes=["ssd_nA"])
    for gi in range(3):
        wb, wk = k.w_next()
        w3 = wb[:, 0:2048].rearrange("p (kc c) -> p kc c", c=256)
        proj_fm(k, w3, wk, [(0, 128), (128, 128)], evac_to_big(k, lambda i, gi=gi: gi * 2 + i))
    wb, wk = k.w_next()
    w3 = wb[:, 0:64].rearrange("p (kc c) -> p kc c", c=8)
    ps, pk = k.psn()
    for tt in range(8):
        for kc in range(8):
            P.op("pe", lambda e, tt=tt, kc=kc, ps=ps, w3=w3: e.matmul(
                ps[:, tt * 8:(tt + 1) * 8], lhsT=k.hT[:, kc, tt * 128:(tt + 1) * 128], rhs=w3[:, kc, :],
                start=(kc == 0), stop=(kc == 7)), reads=wk + [("hT", tt // 4)], writes=[pk])
    P.op("dve", lambda e, ps=ps: e.tensor_tensor(out=c["dt"][:], in0=ps[:, 0:64].rearrange("p (t u) -> p t u", u=8),
                                                 in1=c["dtb"][:].unsqueeze(1).to_broadcast([128, 8, 8]), op=ALU.add),
         reads=[pk, "ssd_dtb"], writes=["ssd_dt"])
    P.op("act", lambda e: e.activation(out=c["dt"][:], in_=c["dt"][:], func=AF.Exp), reads=["ssd_dt"], writes=["ssd_dt"])
    P.op("act", lambda e: e.activation(out=c["dt"][:], in_=c["dt"][:], func=AF.Ln, bias=1.0), reads=["ssd_dt"], writes=["ssd_dt"])
    P.op("dve", lambda e: e.scalar_tensor_tensor(out=c["a"][:], in0=c["dt"][:], scalar=-1.0,
                                                 in1=c["nA"][:].unsqueeze(1).to_broadcast([128, 8, 8]), op0=ALU.mult, op1=ALU.mult),
         reads=["ssd_dt", "ssd_nA"], writes=["ssd_a"])
    k.dbg("ssd_dt%d%d" % (l, g), c["dt"][:], [128, 8, 8], ["ssd_dt"])
    if PH0 < 0:
        return
    S = 8
    XS = (S + 0, S + 1)
    dwconv_fm(k, 2, XS[0], c["conv"][:, 0, :], "ssd_conv", 4, nseq, L)
    dwconv_fm(k, 3, XS[1], c["conv"][:, 1, :], "ssd_conv", 4, nseq, L)
    SUB = int(os.environ.get("SSD_SUB", "9"))
    if SUB < 2:
        return
    dwconv_fm(k, 4, S + 3, c["conv"][:, 2, :], "ssd_conv", 4, nseq, L)
    dwconv_fm(k, 5, S + 4, c["conv"][:, 3, :], "ssd_conv", 4, nseq, L)
    if SUB < 3:
        return
    for q in range(2):
        P.op("act", lambda e, q=q: e.activation(out=big(k, XS[q]), in_=big(k, XS[q]), func=AF.Silu), reads=[bigkey(XS[q])], writes=[bigkey(XS[q])])
    if SUB < 4:
        return
    BC = bfv(k, S + 2).rearrange("p (a t) -> p a t", a=2)
    Bf = S + 3
    P.op("act", lambda e: e.activation(out=big(k, Bf), in_=big(k, Bf), func=AF.Silu), reads=[bigkey(Bf)], writes=[bigkey(Bf)])
    P.op("dve", lambda e: e.tensor_copy(out=BC[:, 0, :], in_=big(k, Bf)), reads=[bigkey(Bf)], writes=[bigkey(S + 2)])
    P.op("act", lambda e: e.activation(out=BC[:, 1, :], in_=big(k, S + 4), func=AF.Silu), reads=[bigkey(S + 4)], writes=[bigkey(S + 2)])
    sE, sLH, sW, sG = 2, 3, S + 4, S + 5
    Et = big(k, sE).rearrange("p (u l) -> p u l", l=128)
    LH = big(k, sLH).rearrange("p (u l) -> p u l", l=128)
    Wb = bfv(k, sW)[:, 0:1024].rearrange("p (u l) -> p u l", l=128)
    Xdt = bfv(k, sW)[:, 1024:1536]
    Xdec = bfv(k, sW)[:, 1536:2048]
    Gm = big(k, sG)[:, 0:512].rearrange("p (d g l) -> p d g l", d=2, g=2)
    T1 = big(k, sG)[:, 512:1024]
    Btok = bfv(k, S + 6)[:, 0:128]
    INC = k.BIG[:, 6:8, :].rearrange("p a (c x) -> p (a c) x", x=256)
    YD = k.BIG[:, 17:19, :].rearrange("p a (c x) -> p (a c) x", x=256)
    STin = bfv(k, 16).rearrange("p (c x) -> p c x", x=256)
    kE, kLH, kW, kG, kB, kINC, kYD, kST = bigkey(sE), bigkey(sLH), bigkey(sW), bigkey(sG), bigkey(S + 6), [bigkey(6), bigkey(7)], [bigkey(17), bigkey(18)], bigkey(16)
    kBC = bigkey(S + 2)
    import os
    PH = int(os.environ.get("SSD_PHASE", "9"))
    if PH < 1:
        return
    for ch in range(8):
        ts = slice(ch * 128, (ch + 1) * 128)
        psT, pkT = k.psn()
        for q in range(2):
            P.op("pe", lambda e, q=q, psT=psT, ts=ts: e.transpose(psT[:, q * 128:(q + 1) * 128], big(k, XS[q])[:, ts], k.ident[:]),
                 reads=[bigkey(XS[q]), "ident"], writes=[pkT])
        P.op("pe", lambda e, psT=psT, ts=ts: e.transpose(psT[:, 256:384], big(k, Bf)[:, ts], k.ident[:]),
             reads=[bigkey(Bf), "ident"], writes=[pkT])
        SV = os.environ.get("SSD_V", "xb")
        for d in range(2):
            if "x" not in SV:
                continue
            P.op("dve", lambda e, psT=psT, ch=ch, d=d: e.tensor_tensor(
                out=Xdt[:, d * 256:(d + 1) * 256].rearrange("p (h x) -> p h x", h=4),
                in0=psT[:, 0:256].rearrange("p (h x) -> p h x", h=4),
                in1=c["dt"][:, ch, d * 4:(d + 1) * 4].unsqueeze(2).to_broadcast([128, 4, 64]), op=ALU.mult),
                reads=[pkT, "ssd_dt"], writes=[kW])
        if "b" in SV:
            P.op("act", lambda e, psT=psT: e.activation(out=Btok, in_=psT[:, 256:384], func=AF.Copy), reads=[pkT], writes=[kB])
        SA = int(os.environ.get("SSD_SA", "9"))
        if SA < 2:
            continue
        psGs = [k.psn() for _ in range(2)]
        for gg in range(2):
            psG, pkG = psGs[gg]
            P.op("pe", lambda e, gg=gg, psG=psG, ts=ts: e.matmul(
                psG[:, 0:128], lhsT=BC[gg * 64:(gg + 1) * 64, 0, ts], rhs=BC[gg * 64:(gg + 1) * 64, 1, ts], start=True, stop=True),
                reads=[kBC], writes=[pkG])
        for gg in range(2):
            psG, pkG = psGs[gg]
            for d in range(2):
                mk = LE if d == 0 else GE
                P.op("dve", lambda e, d=d, gg=gg, mk=mk, psG=psG: e.tensor_tensor(
                    out=Gm[:, d, gg, :], in0=psG[:, 0:128], in1=mk, op=ALU.mult), reads=[pkG, "masks"], writes=[kG])
        if SA < 3:
            continue
        for d in range(2):
            mu = GT if d == 0 else LT
            P.op("dve", lambda e, d=d, mu=mu, ch=ch: e.tensor_tensor(
                out=LH[:, d * 4:(d + 1) * 4, :], in0=mu.unsqueeze(1).to_broadcast([128, 4, 128]),
                in1=c["a"][:, ch, d * 4:(d + 1) * 4].unsqueeze(2).to_broadcast([128, 4, 128]), op=ALU.mult),
                reads=["masks", "ssd_a"], writes=[kLH])
        for d in range(2):
            psD, pkD = k.psn()
            ml = LE if d == 0 else GE
            for h in range(4):
                P.op("pe", lambda e, d=d, h=h, psD=psD, ml=ml: e.matmul(
                    psD[:, h * 128:(h + 1) * 128], lhsT=LH[:, d * 4 + h, :], rhs=ml, start=True, stop=True),
                    reads=[kLH, "masks"], writes=[pkD])
            P.op("act", lambda e, d=d, psD=psD: e.activation(out=Et[:, d * 4:(d + 1) * 4, :], in_=psD[:, :].rearrange("p (h l) -> p h l", h=4), func=AF.Exp),
                 reads=[pkD], writes=[kE])
            for gg in range(2):
                P.op("dve", lambda e, d=d, gg=gg: e.tensor_tensor(
                    out=Wb[:, d * 4 + gg * 2:d * 4 + gg * 2 + 2, :],
                    in0=Et[:, d * 4 + gg * 2:d * 4 + gg * 2 + 2, :],
                    in1=Gm[:, d, gg, :].unsqueeze(1).to_broadcast([128, 2, 128]), op=ALU.mult),
                    reads=[kE, kG], writes=[kW])
            col = 127 if d == 0 else 0
            P.op("dve", lambda e, d=d, col=col: e.tensor_tensor(
                out=Xdec[:, d * 256:(d + 1) * 256].rearrange("p (h x) -> p h x", h=4),
                in0=Xdt[:, d * 256:(d + 1) * 256].rearrange("p (h x) -> p h x", h=4),
                in1=Et[:, d * 4:(d + 1) * 4, col:col + 1].to_broadcast([128, 4, 64]), op=ALU.mult),
                reads=[kE, kW], writes=[kW])
        if SA < 4:
            continue
        psY, pkY = k.psn()
        for h in range(4):
            for d in range(2):
                u = d * 4 + h
                P.op("pe", lambda e, h=h, d=d, u=u, psY=psY: e.matmul(
                    psY[:, h * 64:(h + 1) * 64], lhsT=Wb[:, u, :], rhs=Xdt[:, u * 64:(u + 1) * 64], start=(d == 0), stop=(d == 1)),
                    reads=[kW], writes=[pkY])
        P.op("act", lambda e, psY=psY, ch=ch: e.activation(out=YD[:, ch, :], in_=psY[:, 0:256], func=AF.Copy), reads=[pkY], writes=kYD)
        if SA < 5:
            continue
        psI, pkI = k.psn()
        for gg in range(2):
            for d in range(2):
                c0 = (d * 4 + gg * 2) * 64
                P.op("pe", lambda e, gg=gg, d=d, c0=c0, psI=psI: e.matmul(
                    psI[gg * 64:(gg + 1) * 64, d * 128:(d + 1) * 128], lhsT=Btok[:, gg * 64:(gg + 1) * 64], rhs=Xdec[:, c0:c0 + 128], start=True, stop=True),
                    reads=[kB, kW], writes=[pkI])
        P.op("act", lambda e, psI=psI, ch=ch: e.activation(out=INC[:, ch, :], in_=psI[:, 0:256], func=AF.Copy), reads=[pkI], writes=kINC)
        if SA < 6:
            continue
        psA, pkA = k.psn()
        P.op("pe", lambda e, psA=psA, ch=ch: e.matmul(psA[:, 0:4], lhsT=LE, rhs=c["a"][:, ch, 0:4], start=True, stop=True), reads=["masks", "ssd_a"], writes=[pkA])
        P.op("pe", lambda e, psA=psA, ch=ch: e.matmul(psA[:, 4:8], lhsT=GE, rhs=c["a"][:, ch, 4:8], start=True, stop=True), reads=["masks", "ssd_a"], writes=[pkA])
        P.op("pe", lambda e, psA=psA, ch=ch: e.matmul(psA[:, 8:16], lhsT=ONES, rhs=c["a"][:, ch, :], start=True, stop=True), reads=["masks", "ssd_a"], writes=[pkA])
        P.op("act", lambda e, psA=psA, ch=ch: e.activation(out=c["eacs"][:, ch, :], in_=psA[:, 0:8], func=AF.Exp), reads=[pkA], writes=["ssd_eacs"])
        for gg in range(2):
            P.op("act", lambda e, psA=psA, ch=ch, gg=gg: e.activation(
                out=c["dAe"][gg * 64:(gg + 1) * 64, ch, :].rearrange("p (d h) -> p d h", d=2),
                in_=psA[gg * 64:(gg + 1) * 64, 8:16].rearrange("p (d g h) -> p d g h", d=2, g=2)[:, :, gg, :], func=AF.Exp),
                reads=[pkA], writes=["ssd_dAe"])
    k.dbg("ssd_yd%d%d" % (l, g), YD, [128, 8, 256], kYD)
    if PH < 2:
        return
    nchs = L // 128
    for s in range(nseq):
        if g == 0:
            P.op("dve", lambda e: e.memset(c["ST"][:], 0.0), writes=["ssd_ST"])
        else:
            P.op("sp", lambda e: e.dma_start(out=c["s0"][:], in_=D["st_ssd"][l].rearrange("d h p n -> p (d h) n")), writes=["ssd_s0"], dma=True)
            ps0, pk0 = k.psn()
            for d in range(2):
                for gg in range(2):
                    for hh in range(2):
                        u = d * 4 + gg * 2 + hh
                        P.op("pe", lambda e, d=d, gg=gg, hh=hh, u=u, ps0=ps0: e.transpose(
                            ps0[0:64, (d * 4 + gg * 2 + hh) * 64:(d * 4 + gg * 2 + hh + 1) * 64], c["s0"][:, u, :], k.ident[0:64, 0:64]),
                            reads=["ssd_s0", "ident"], writes=[pk0])
            for gg in range(2):
                src = ps0[0:64, :].rearrange("n (d g x) -> n d g x", d=2, g=2)[:, :, gg, :]
                tmp = big(k, sG)[0:64, 0:256].rearrange("n (d x) -> n d x", d=2)
                P.op("act", lambda e, src=src, tmp=tmp: e.activation(out=tmp, in_=src, func=AF.Copy), reads=[pk0], writes=[kG])
                if gg == 0:
                    P.op("dve", lambda e, tmp=tmp: e.tensor_copy(out=c["ST"][0:64, :].rearrange("n (d x) -> n d x", d=2), in_=tmp), reads=[kG], writes=["ssd_ST"])
                else:
                    P.op("sp", lambda e, tmp=tmp: e.dma_start(out=c["ST"][64:128, :].rearrange("n (d x) -> n d x", d=2), in_=tmp), reads=[kG], writes=["ssd_ST"], dma=True)
        for d in range(2):
            order = range(nchs) if d == 0 else range(nchs - 1, -1, -1)
            cs = slice(d * 128, (d + 1) * 128)
            for j in order:
                ch = s * nchs + j
                P.op("act", lambda e, ch=ch, cs=cs: e.activation(out=STin[:, ch, cs], in_=c["ST"][:, cs], func=AF.Copy), reads=["ssd_ST"], writes=[kST])
                P.op("dve", lambda e, ch=ch, cs=cs, d=d: e.tensor_tensor(
                    out=c["ST"][:, cs].rearrange("p (h x) -> p h x", h=2), in0=c["ST"][:, cs].rearrange("p (h x) -> p h x", h=2),
                    in1=c["dAe"][:, ch, d * 2:(d + 1) * 2].unsqueeze(2).to_broadcast([128, 2, 64]), op=ALU.mult),
                    reads=["ssd_ST", "ssd_dAe"], writes=["ssd_ST"])
                P.op("dve", lambda e, ch=ch, cs=cs: e.tensor_tensor(out=c["ST"][:, cs], in0=c["ST"][:, cs], in1=INC[:, ch, cs], op=ALU.add),
                     reads=["ssd_ST"] + kINC, writes=["ssd_ST"])
            if g == 0:
                for gg in range(2):
                    psF, pkF = k.psn()
                    for hh in range(2):
                        P.op("pe", lambda e, gg=gg, hh=hh, d=d, psF=psF: e.transpose(
                            psF[0:64, hh * 64:(hh + 1) * 64], c["ST"][gg * 64:(gg + 1) * 64, d * 128 + hh * 64:d * 128 + (hh + 1) * 64],
                            k.ident[gg * 64:(gg + 1) * 64, gg * 64:(gg + 1) * 64]),
                            reads=["ssd_ST", "ident"], writes=[pkF])
                    P.op("act", lambda e, psF=psF, gg=gg: e.activation(out=c["fin"][:, gg * 2:gg * 2 + 2, :].rearrange("p h n -> p (h n)"), in_=psF[0:64, 0:128], func=AF.Copy),
                         reads=[pkF], writes=["ssd_fin"])
                P.op("sp", lambda e, s=s, d=d: e.dma_start(out=D["ns_ssd"][s, l, d].rearrange("h p n -> p h n"), in_=c["fin"][:]), reads=["ssd_fin"], dma=True)
    if PH < 3:
        return
    YT = (4, 5)
    for ch in range(8):
        ts = slice(ch * 128, (ch + 1) * 128)
        psOs = [k.psn() for _ in range(2)]
        for gg in range(2):
            psO, pkO = psOs[gg]
            for d in range(2):
                P.op("pe", lambda e, d=d, gg=gg, psO=psO, ts=ts, ch=ch: e.matmul(
                    psO[:, d * 128:(d + 1) * 128], lhsT=BC[gg * 64:(gg + 1) * 64, 1, ts],
                    rhs=STin[gg * 64:(gg + 1) * 64, ch, d * 128:(d + 1) * 128], start=True, stop=True),
                    reads=[kBC, kST], writes=[pkO])
        for gg in range(2):
            psO, pkO = psOs[gg]
            for d in range(2):
                u0 = d * 4 + gg * 2
                P.op("dve", lambda e, psO=psO, ch=ch, d=d, u0=u0: e.tensor_tensor(
                    out=T1[:, u0 * 64:(u0 + 2) * 64].rearrange("p (h x) -> p h x", h=2),
                    in0=psO[:, d * 128:(d + 1) * 128].rearrange("p (h x) -> p h x", h=2),
                    in1=c["eacs"][:, ch, u0:u0 + 2].unsqueeze(2).to_broadcast([128, 2, 64]), op=ALU.mult),
                    reads=[pkO, "ssd_eacs"], writes=[kG])
        P.op("dve", lambda e: e.tensor_tensor(out=T1[:, 0:256], in0=T1[:, 0:256], in1=T1[:, 256:512], op=ALU.add), reads=[kG], writes=[kG])
        P.op("dve", lambda e, ch=ch: e.tensor_tensor(out=T1[:, 0:256], in0=T1[:, 0:256], in1=YD[:, ch, :], op=ALU.add), reads=[kG] + kYD, writes=[kG])
        psT, pkT = k.psn()
        for q in range(2):
            P.op("pe", lambda e, q=q, psT=psT: e.transpose(psT[:, q * 128:(q + 1) * 128], T1[:, q * 128:(q + 1) * 128], k.ident[:]), reads=[kG, "ident"], writes=[pkT])
        P.op("act", lambda e, psT=psT, ts=ts: e.activation(out=k.BIG[:, 4:6, ts], in_=psT[:, 0:256].rearrange("p (q t) -> p q t", q=2), func=AF.Copy),
             reads=[pkT], writes=[bigkey(4), bigkey(5)])
    if PH < 4:
        return
    for q in range(2):
        P.op("dve", lambda e, q=q: e.scalar_tensor_tensor(out=big(k, YT[q]), in0=big(k, XS[q]), scalar=c["dfm"][:, q:q + 1], in1=big(k, YT[q]), op0=ALU.mult, op1=ALU.add),
             reads=[bigkey(XS[q]), bigkey(YT[q]), "ssd_dfm"], writes=[bigkey(YT[q])])
        P.op("act", lambda e, q=q: e.activation(out=big(k, q), in_=big(k, q), func=AF.Silu), reads=[bigkey(q)], writes=[bigkey(q)])
        P.op("dve", lambda e, q=q: e.tensor_tensor(out=big(k, YT[q]), in0=big(k, YT[q]), in1=big(k, q), op=ALU.mult), reads=[bigkey(YT[q]), bigkey(q)], writes=[bigkey(YT[q])])
    k.dbg("ssd_y%d%d" % (l, g), k.BIG[:, 4:6, :], [128, 2, T], [bigkey(4), bigkey(5)])
    sq = bfv(k, sE).rearrange("p (q t) -> p q t", q=2)
    P.op("act", lambda e: e.activation(out=sq, in_=k.BIG[:, 4:6, :], func=AF.Square), reads=[bigkey(4), bigkey(5)], writes=[kE])
    rs = big(k, sLH)
    for tg in range(2):
        ps, pk = k.psn()
        for q in range(2):
            P.op("pe", lambda e, q=q, tg=tg, ps=ps: e.matmul(ps[:, :], lhsT=k.ones_bf[:, :], rhs=sq[:, q, tg * 512:(tg + 1) * 512], start=(q == 0), stop=(q == 1)),
                 reads=[kE, "ones"], writes=[pk])
        P.op("act", lambda e, tg=tg, ps=ps: e.activation(out=rs[:, tg * 512:(tg + 1) * 512], in_=ps[:, :], func=AF.Ln, scale=1.0 / 256, bias=EPS), reads=[pk], writes=[kLH])
    P.op("act", lambda e: e.activation(out=rs, in_=rs, func=AF.Exp, scale=-0.5), reads=[kLH], writes=[kLH])
    for q in range(2):
        P.op("dve", lambda e, q=q: e.scalar_tensor_tensor(out=k.mixT[:, 2 + q, :], in0=big(k, YT[q]), scalar=c["nfm"][:, q:q + 1], in1=rs, op0=ALU.mult, op1=ALU.mult),
             reads=[bigkey(YT[q]), kLH, "ssd_nfm"], writes=[("mixT", 2 + q)])
def P_gdn(k):
    if hasattr(k, "_gdn"):
        return k._gdn
    P = k.P
    k._gdn = {
        "conv": P.sb("gdn_conv", [128, 6, 4]),
        "dtb": P.sb("gdn_dtb", [128, 8]),
        "nA": P.sb("gdn_nA", [128, 8]),
        "gn": P.sb("gdn_gn", [128, 64]),
        "beta": P.sb("gdn_beta", [128, 8, 8]),
        "gt": P.sb("gdn_gt", [128, 8, 8]),
        "egc": P.sb("gdn_egc", [128, 4]),
        "gam": P.sb("gdn_gam", [128, 4]),
        "bg": P.sb("gdn_bg", [128, 4]),
        "S": P.sb("gdn_S", [128, 2, 64]),
        "ss": P.sb("gdn_ss", [128, 4]),
    }
    return k._gdn


def gdn_mixer(k, l, g, nseq, L):
    P, D = k.P, k.D
    c = P_gdn(k)
    M = k.masks
    LE, GE, GT, LT, ONES, OBD = (M[:, j, :] for j in range(6))
    M2 = k.masks2
    P.op("sp", lambda e: e.dma_start(out=c["conv"][:], in_=D["gdn_conv_fm"][:, l, :, :]), writes=["gdn_conv"], dma=True)
    P.op("sp", lambda e: e.dma_start(out=c["dtb"][:], in_=D["gdn_dtb_bc"][:, l, :]), writes=["gdn_dtb"], dma=True)
    P.op("sp", lambda e: e.dma_start(out=c["nA"][:], in_=D["gdn_alog_bc"][:, l, :]), writes=["gdn_nA"], dma=True)
    P.op("sp", lambda e: e.dma_start(out=c["gn"][:], in_=D["gdn_norm_bc"][:, l, :]), writes=["gdn_gn"], dma=True)
    P.op("act", lambda e: e.activation(out=c["nA"][:], in_=c["nA"][:], func=AF.Exp), reads=["gdn_nA"], writes=["gdn_nA"])
    for gi in range(4):
        wb, wk = k.w_next()
        w3 = wb[:, 0:2048].rearrange("p (kc c) -> p kc c", c=256)
        proj_fm(k, w3, wk, [(0, 128), (128, 128)], evac_to_big(k, lambda i, gi=gi: gi * 2 + i))
    wb, wk = k.w_next()
    w3 = wb[:, 0:128].rearrange("p (kc c) -> p kc c", c=16)
    ps, pk = k.psn()
    for tt in range(8):
        for kc in range(8):
            P.op("pe", lambda e, tt=tt, kc=kc, ps=ps, w3=w3: e.matmul(
                ps[:, tt * 16:(tt + 1) * 16], lhsT=k.hT[:, kc, tt * 128:(tt + 1) * 128], rhs=w3[:, kc, :],
                start=(kc == 0), stop=(kc == 7)), reads=wk + [("hT", tt // 4)], writes=[pk])
    ba = ps[:, 0:128].rearrange("p (t u) -> p t u", u=16)
    P.op("act", lambda e: e.activation(out=c["beta"][:], in_=ba[:, :, 0:8], func=AF.Sigmoid), reads=[pk], writes=["gdn_beta"])
    P.op("dve", lambda e: e.tensor_tensor(out=c["gt"][:], in0=ba[:, :, 8:16], in1=c["dtb"][:].unsqueeze(1).to_broadcast([128, 8, 8]), op=ALU.add),
         reads=[pk, "gdn_dtb"], writes=["gdn_gt"])
    P.op("act", lambda e: e.activation(out=c["gt"][:], in_=c["gt"][:], func=AF.Exp), reads=["gdn_gt"], writes=["gdn_gt"])
    P.op("act", lambda e: e.activation(out=c["gt"][:], in_=c["gt"][:], func=AF.Ln, bias=1.0), reads=["gdn_gt"], writes=["gdn_gt"])
    P.op("dve", lambda e: e.scalar_tensor_tensor(out=c["gt"][:], in0=c["gt"][:], scalar=-1.0,
                                                 in1=c["nA"][:].unsqueeze(1).to_broadcast([128, 8, 8]), op0=ALU.mult, op1=ALU.mult),
         reads=["gdn_gt", "gdn_nA"], writes=["gdn_gt"])
    S = 8
    for j in range(6):
        dwconv_fm(k, j, S + j, c["conv"][:, j, :], "gdn_conv", 4, nseq, L)
        P.op("act", lambda e, j=j: e.activation(out=big(k, S + j), in_=big(k, S + j), func=AF.Silu), reads=[bigkey(S + j)], writes=[bigkey(S + j)])
    for j in range(4):
        sq = big(k, 0)
        P.op("act", lambda e, j=j, sq=sq: e.activation(out=sq, in_=big(k, S + j), func=AF.Square), reads=[bigkey(S + j)], writes=[bigkey(0)])
        for tg in range(2):
            ps, pk = k.psn()
            P.op("pe", lambda e, ps=ps, tg=tg, sq=sq: e.matmul(ps[:, :], lhsT=OBD, rhs=sq[:, tg * 512:(tg + 1) * 512], start=True, stop=True),
                 reads=[bigkey(0), "masks"], writes=[pk])
            rs = big(k, 1)[:, tg * 512:(tg + 1) * 512]
            P.op("act", lambda e, ps=ps, rs=rs: e.activation(out=rs, in_=ps[:, :], func=AF.Ln, bias=EPS), reads=[pk], writes=[bigkey(1)])
        bias = math.log(0.125) if j < 2 else 0.0
        P.op("act", lambda e, bias=bias: e.activation(out=big(k, 1), in_=big(k, 1), func=AF.Exp, scale=-0.5, bias=bias), reads=[bigkey(1)], writes=[bigkey(1)])
        P.op("dve", lambda e, j=j: e.tensor_tensor(out=big(k, S + j), in0=big(k, S + j), in1=big(k, 1), op=ALU.mult),
             reads=[bigkey(S + j), bigkey(1)], writes=[bigkey(S + j)])
    k.dbg("gdn_qk%d%d" % (l, g), k.BIG[:, 8:14, :], [128, 6, T], [bigkey(j) for j in range(8, 14)])
    for j in range(4):
        P.op("act" if j % 2 == 0 else "pool", (lambda e, j=j: e.activation(out=k.hT[:, j, :], in_=big(k, S + j), func=AF.Copy)) if j % 2 == 0 else
             (lambda e, j=j: e.tensor_copy(out=k.hT[:, j, :], in_=big(k, S + j))), reads=[bigkey(S + j)], writes=[("hT", 0), ("hT", 1)])
    kHT = [("hT", 0), ("hT", 1)]
    LHg = big(k, 0)[:, 0:512].rearrange("p (h s) -> p h s", h=4)
    EE = big(k, 0)[:, 512:768].rearrange("p (a s) -> p a s", a=2)
    EE2 = big(k, 1)[:, 0:256].rearrange("p (a s) -> p a s", a=2)
    KV = big(k, 1)[:, 512:1024]
    VBKB = bfv(k, 2)[:, 0:512].rearrange("p (a h x) -> p a h x", a=2, h=4)
    b14 = bfv(k, 14)
    b15 = bfv(k, 15)
    attT = [b15[:, 1024 + 128 * i_:1024 + 128 * (i_ + 1)] for i_ in range(4)]
    TTb = [b15[:, 1536 + 128 * i_:1536 + 128 * (i_ + 1)] for i_ in range(4)]
    Sbf = b14[:, 1536:1664].rearrange("p (a x) -> p a x", a=2)
    AMb = [big(k, 3 + p_)[:, 0:512].rearrange("p (a s) -> p a s", a=4) for p_ in range(2)]
    ATMb = [big(k, 3 + p_)[:, 512:896].rearrange("p (a s) -> p a s", a=3) for p_ in range(2)]
    XRb = [[big(k, 5 if p_ == 0 else 18)[:, pp * 256:(pp + 1) * 256] for pp in range(2)] for p_ in range(2)]
    XPb2 = [[big(k, 5 if p_ == 0 else 18)[:, 512 + pp * 256:512 + (pp + 1) * 256] for pp in range(2)] for p_ in range(2)]
    Ybuf = [(big(k, 0)[:, 768:896], big(k, 0)[:, 896:1024]), (big(k, 1)[:, 256:384], big(k, 1)[:, 384:512])]
    uS = big(k, 14)[:, 0:256].rearrange("p (h x) -> p h x", h=4)
    wT = b14[:, 512:768].rearrange("p (a c) -> p a c", a=2)
    vn = b14[:, 768:1024].rearrange("p (h x) -> p h x", h=4)
    kt = b14[:, 1024:1280].rearrange("p (h x) -> p h x", h=4)
    otmp = big(k, 15)[:, 0:256].rearrange("p (h x) -> p h x", h=4)
    Otok = k.BIG[:, 16:18, :].rearrange("p a (c x) -> p (a c) x", x=256)
    kOt = [bigkey(16), bigkey(17)]
    nchs = L // 128
    for d in range(2):
        ml = LE if d == 0 else GE
        mu = GT if d == 0 else LT
        col = 127 if d == 0 else 0
        for s in range(nseq):
            if g == 0:
                P.op("dve", lambda e: e.memset(c["S"][:], 0.0), writes=[("gdn_S", hq) for hq in range(4)])
            else:
                for h in range(4):
                    b, pr = h % 2, h // 2
                    P.op("sp", lambda e, h=h, b=b, pr=pr, d=d: e.dma_start(out=c["S"][b * 64:(b + 1) * 64, pr, :], in_=D["st_gdn"][l, d, h, :, :]), writes=[("gdn_S", h)], dma=True)
            P.op("act", lambda e: e.activation(out=Sbf, in_=c["S"][:], func=AF.Copy), reads=[("gdn_S", hq) for hq in range(4)], writes=[("gdn_S", hq) for hq in range(4)])
            order = range(nchs) if d == 0 else range(nchs - 1, -1, -1)
            for jj in order:
                ch = s * nchs + jj
                ts = slice(ch * 128, (ch + 1) * 128)
                psT, pkT = k.psn()
                for q in range(2):
                    P.op("pe", lambda e, q=q, psT=psT, ts=ts: e.transpose(psT[:, q * 128:(q + 1) * 128], big(k, S + 2 + q)[:, ts], k.ident[:]),
                         reads=[bigkey(S + 2 + q), "ident"], writes=[pkT])
                    P.op("pe", lambda e, q=q, psT=psT, ts=ts: e.transpose(psT[:, 256 + q * 128:256 + (q + 1) * 128], big(k, S + 4 + q)[:, ts], k.ident[:]),
                         reads=[bigkey(S + 4 + q), "ident"], writes=[pkT])
                P.op("act", lambda e, psT=psT: e.activation(out=KV, in_=psT[:, :], func=AF.Copy), reads=[pkT], writes=[("gdn", "KV")])
                psA, pkA = k.psn()
                P.op("pe", lambda e, psA=psA, ch=ch, d=d, ml=ml: e.matmul(psA[:, 0:4], lhsT=ml, rhs=c["gt"][:, ch, d * 4:(d + 1) * 4], start=True, stop=True),
                     reads=["masks", "gdn_gt"], writes=[pkA])
                P.op("pe", lambda e, psA=psA, ch=ch, d=d: e.matmul(psA[:, 4:8], lhsT=ONES, rhs=c["gt"][:, ch, d * 4:(d + 1) * 4], start=True, stop=True),
                     reads=["masks", "gdn_gt"], writes=[pkA])
                P.op("act", lambda e, psA=psA: e.activation(out=c["egc"][:], in_=psA[:, 0:4], func=AF.Exp), reads=[pkA], writes=["gdn_egc"])
                P.op("act", lambda e, psA=psA: e.activation(out=c["gam"][:], in_=psA[:, 4:8], func=AF.Exp), reads=[pkA], writes=["gdn_gam"])
                P.op("dve", lambda e, ch=ch, d=d: e.tensor_tensor(out=c["bg"][:], in0=c["beta"][:, ch, d * 4:(d + 1) * 4], in1=c["egc"][:], op=ALU.mult),
                     reads=["gdn_beta", "gdn_egc"], writes=["gdn_bg"])
                P.op("dve", lambda e, ch=ch, d=d: e.tensor_tensor(
                    out=VBKB[:, 0, :, :], in0=KV[:, 256:512].rearrange("p (h x) -> p h x", h=4),
                    in1=c["beta"][:, ch, d * 4:(d + 1) * 4].unsqueeze(2).to_broadcast([128, 4, 64]), op=ALU.mult),
                    reads=[("gdn", "KV"), "gdn_beta"], writes=[("gdn", "VBKB")])
                P.op("dve", lambda e: e.tensor_tensor(
                    out=VBKB[:, 1, :, :], in0=KV[:, 0:256].rearrange("p (h x) -> p h x", h=4),
                    in1=c["bg"][:].unsqueeze(2).to_broadcast([128, 4, 64]), op=ALU.mult),
                    reads=[("gdn", "KV"), "gdn_bg"], writes=[("gdn", "VBKB")])
                P.op("dve", lambda e, ch=ch, d=d, mu=mu: e.tensor_tensor(
                    out=LHg, in0=mu.unsqueeze(1).to_broadcast([128, 4, 128]),
                    in1=c["gt"][:, ch, d * 4:(d + 1) * 4].unsqueeze(2).to_broadcast([128, 4, 128]), op=ALU.mult),
                    reads=["masks", "gdn_gt"], writes=[("gdn", "LHg")])
                def unit(h, d=d, s=s, jj=jj, ch=ch, ts=ts, ml=ml, mu=mu, col=col):
                    b, pr = h % 2, h // 2
                    bs = slice(b * 64, (b + 1) * 64)
                    u = d * 4 + h
                    ee = EE if h % 2 == 0 else EE2
                    kee = ("gdn", "EE", h % 2)
                    at = attT[h]
                    kat = ("gdn", "att", h)
                    kn = big(k, S + 2 + pr)
                    qn = big(k, S + 0 + pr)
                    psD, pkD = k.psn()
                    P.op("pe", lambda e, psD=psD, h=h, ml=ml: e.matmul(psD[:, 0:128], lhsT=ml, rhs=LHg[:, h, :], start=True, stop=True),
                         reads=["masks", ("gdn", "LHg")], writes=[pkD])
                    P.op("pe", lambda e, psD=psD, h=h, ml=ml: e.matmul(psD[:, 128:256], lhsT=LHg[:, h, :], rhs=ml, start=True, stop=True),
                         reads=["masks", ("gdn", "LHg")], writes=[pkD])
                    P.op("act", lambda e, psD=psD, ee=ee: e.activation(out=ee, in_=psD[:, 0:256].rearrange("p (a s) -> p a s", a=2), func=AF.Exp),
                         reads=[pkD], writes=[kee])
                    P.op("dve", lambda e, ee=ee, d=d: e.tensor_tensor(out=ee, in0=ee, in1=M2[:, d, :].rearrange("p (a s) -> p a s", a=2), op=ALU.mult),
                         reads=[kee, "masks2"], writes=[kee])
                    P.op("dve", lambda e, h=h, ee=ee, col=col: e.tensor_scalar(out=kt[:, h, :], in0=KV[:, h * 64:(h + 1) * 64], scalar1=ee[:, 1, col:col + 1], scalar2=None, op0=ALU.mult),
                         reads=[("gdn", "KV"), kee], writes=[("gdn", "kt", h)])
                    par = h % 2
                    AM, ATM = AMb[par], ATMb[par]
                    XR, XP = XRb[par], XPb2[par]
                    Yb, Y2b = Ybuf[par]
                    kAM, kATM, kY, kY2 = ("gdn", "AM", par), ("gdn", "ATM", par), ("gdn", "Y", par), ("gdn", "Y2", par)
                    kXR = [("gdn", "XR", par, 0), ("gdn", "XR", par, 1)]
                    kXP = [("gdn", "XP", par, 0), ("gdn", "XP", par, 1)]
                    yield
                    psK, pkK = k.psn()
                    P.op("pe", lambda e, psK=psK, kn=kn, bs=bs, ts=ts: e.matmul(psK[:, 0:128], lhsT=kn[bs, ts], rhs=kn[bs, ts], start=True, stop=True),
                         reads=[bigkey(S + 2 + pr)], writes=[pkK])
                    P.op("dve", lambda e, psK=psK, ee=ee, Yb=Yb, ch=ch, u=u: e.scalar_tensor_tensor(
                        out=Yb, in0=psK[:, 0:128], scalar=c["beta"][:, ch, u:u + 1], in1=ee[:, 0, :], op0=ALU.mult, op1=ALU.mult),
                        reads=[pkK, kee, "gdn_beta"], writes=[kY])
                    yield
                    psQ, pkQ = k.psn()
                    P.op("pe", lambda e, psQ=psQ, bs=bs, ts=ts, pr=pr: e.matmul(psQ[:, 0:128], lhsT=k.hT[bs, 2 + pr, ts], rhs=k.hT[bs, pr, ts], start=True, stop=True),
                         reads=kHT, writes=[pkQ])
                    P.op("dve", lambda e, psQ=psQ, ee=ee, at=at: e.tensor_tensor(out=at, in0=psQ[:, 0:128], in1=ee[:, 1, :], op=ALU.mult),
                         reads=[pkQ, kee], writes=[kat])
                    first = (d == 0 and s == 0 and jj == 0 and h == 0 and l == 0 and g == 0)
                    if first:
                        k.dbg("gdn_A", Yb, [128, 128], [kY])
                    yield
                    psX, pkX = k.psn()
                    P.op("pe", lambda e, psX=psX, Yb=Yb: e.transpose(psX[:, 0:128], Yb, k.ident[:]), reads=[kY, "ident"], writes=[pkX])
                    P.op("act", lambda e, psX=psX, Y2b=Y2b: e.activation(out=Y2b, in_=psX[:, 0:128], func=AF.Copy), reads=[pkX], writes=[kY2])
                    P.op("dve", lambda e, Yb=Yb, AM=AM: e.tensor_tensor(out=AM[:, 0:3, :], in0=Yb.unsqueeze(1).to_broadcast([128, 3, 128]), in1=k.masks4[:, 0:3, :], op=ALU.mult),
                         reads=[kY, "masks4"], writes=[kAM])
                    P.op("dve", lambda e, Y2b=Y2b, ATM=ATM: e.tensor_tensor(out=ATM[:, 0:2, :], in0=Y2b.unsqueeze(1).to_broadcast([128, 2, 128]), in1=k.masks4[:, 0:2, :], op=ALU.mult),
                         reads=[kY2, "masks4"], writes=[kATM])
                    P.op("dve", lambda e, XR=XR, AM=AM: e.tensor_tensor(out=XR[1][:, 128:256], in0=k.ident[:], in1=AM[:, 0, :], op=ALU.subtract),
                         reads=[kAM, "ident"], writes=[kXR[1]])
                    P.op("dve", lambda e, XP=XP, ATM=ATM: e.tensor_tensor(out=XP[1][:, 128:256], in0=k.ident[:], in1=ATM[:, 0, :], op=ALU.subtract),
                         reads=[kATM, "ident"], writes=[kXP[1]])
                    yield
                    ps1, pk1 = k.psn()
                    P.op("pe", lambda e, ps1=ps1, AM=AM, ATM=ATM: e.matmul(ps1[:, 0:128], lhsT=ATM[:, 0, :], rhs=AM[:, 0, :], start=True, stop=True),
                         reads=[kAM, kATM], writes=[pk1])
                    P.op("pe", lambda e, ps1=ps1, AM=AM, ATM=ATM: e.matmul(ps1[:, 128:256], lhsT=AM[:, 0, :], rhs=ATM[:, 0, :], start=True, stop=True),
                         reads=[kAM, kATM], writes=[pk1])
                    P.op("act", lambda e, ps1=ps1, XR=XR: e.activation(out=XR[1][:, 0:128], in_=ps1[:, 0:128], func=AF.Copy), reads=[pk1], writes=[kXR[1]])
                    P.op("act", lambda e, ps1=ps1, XP=XP: e.activation(out=XP[1][:, 0:128], in_=ps1[:, 128:256], func=AF.Copy), reads=[pk1], writes=[kXP[1]])
                    for j in range(1, 5):
                        cur, nxt = j % 2, (j + 1) % 2
                        yield
                        psA2, pkA2 = k.psn()
                        psB2, pkB2 = k.psn()
                        if j <= 3:
                            P.op("pe", lambda e, psA2=psA2, XR=XR, XP=XP, cur=cur: e.matmul(psA2[:, 0:256], lhsT=XR[cur][:, 0:128], rhs=XP[cur], start=True, stop=True),
                                 reads=[kXR[cur], kXP[cur]], writes=[pkA2])
                            P.op("pe", lambda e, psB2=psB2, XR=XR, XP=XP, cur=cur: e.matmul(psB2[:, 0:256], lhsT=XP[cur][:, 0:128], rhs=XR[cur], start=True, stop=True),
                                 reads=[kXR[cur], kXP[cur]], writes=[pkB2])
                            P.op("act", lambda e, psA2=psA2, XP=XP, nxt=nxt: e.activation(out=XP[nxt][:, 0:128], in_=psA2[:, 0:128], func=AF.Copy), reads=[pkA2], writes=[kXP[nxt]])
                            P.op("act", lambda e, psB2=psB2, XR=XR, nxt=nxt: e.activation(out=XR[nxt][:, 0:128], in_=psB2[:, 0:128], func=AF.Copy), reads=[pkB2], writes=[kXR[nxt]])
                        else:
                            P.op("pe", lambda e, psA2=psA2, XR=XR, XP=XP, cur=cur: e.matmul(psA2[:, 128:256], lhsT=XR[cur][:, 0:128], rhs=XP[cur][:, 128:256], start=True, stop=True),
                                 reads=[kXR[cur], kXP[cur]], writes=[pkA2])
                            P.op("pe", lambda e, psB2=psB2, XR=XR, XP=XP, cur=cur: e.matmul(psB2[:, 128:256], lhsT=XP[cur][:, 0:128], rhs=XR[cur][:, 128:256], start=True, stop=True),
                                 reads=[kXR[cur], kXP[cur]], writes=[pkB2])
                        P.op("dve", lambda e, psA2=psA2, XP=XP, cur=cur, nxt=nxt: e.tensor_tensor(out=XP[nxt][:, 128:256], in0=XP[cur][:, 128:256], in1=psA2[:, 128:256], op=ALU.add),
                             reads=[pkA2, kXP[cur]], writes=[kXP[nxt]])
                        P.op("dve", lambda e, psB2=psB2, XR=XR, cur=cur, nxt=nxt: e.tensor_tensor(out=XR[nxt][:, 128:256], in0=XR[cur][:, 128:256], in1=psB2[:, 128:256], op=ALU.add),
                             reads=[pkB2, kXR[cur]], writes=[kXR[nxt]])
                    Tc, Qc = XR[1][:, 128:256], XP[1][:, 128:256]
                    Tn, Qn = XR[0][:, 128:256], XP[0][:, 128:256]
                    kT, kQ = [kXR[1], kXR[0]], [kXP[1], kXP[0]]
                    tb = [(Tc, Qc), (Tn, Qn)]
                    LAST = 1
                    for lev in range(2):
                        cur, nxt = lev % 2, (lev + 1) % 2
                        Tcur, Qcur = tb[cur]
                        Tnxt, Qnxt = tb[nxt]
                        yield
                        psY, pkY = k.psn()
                        P.op("pe", lambda e, psY=psY, AM=AM, Qcur=Qcur, lev=lev: e.matmul(psY[:, 0:128], lhsT=AM[:, 1 + lev, :], rhs=Qcur, start=True, stop=True),
                             reads=[kAM, kQ[cur]], writes=[pkY])
                        if lev < LAST:
                            P.op("pe", lambda e, psY=psY, ATM=ATM, Tcur=Tcur, lev=lev: e.matmul(psY[:, 128:256], lhsT=ATM[:, 1 + lev, :], rhs=Tcur, start=True, stop=True),
                                 reads=[kATM, kT[cur]], writes=[pkY])
                        P.op("act", lambda e, psY=psY, Yb=Yb: e.activation(out=Yb, in_=psY[:, 0:128], func=AF.Copy), reads=[pkY], writes=[kY])
                        if lev < LAST:
                            P.op("act", lambda e, psY=psY, Y2b=Y2b: e.activation(out=Y2b, in_=psY[:, 128:256], func=AF.Copy), reads=[pkY], writes=[kY2])
                        yield
                        psZ, pkZ = k.psn()
                        P.op("pe", lambda e, psZ=psZ, Tcur=Tcur, Yb=Yb: e.matmul(psZ[:, 0:128], lhsT=Tcur, rhs=Yb, start=True, stop=True),
                             reads=[kT[cur], kY], writes=[pkZ])
                        if lev < LAST:
                            P.op("pe", lambda e, psZ=psZ, Qcur=Qcur, Y2b=Y2b: e.matmul(psZ[:, 128:256], lhsT=Qcur, rhs=Y2b, start=True, stop=True),
                                 reads=[kQ[cur], kY2], writes=[pkZ])
                        qdst = Qnxt if lev < LAST else TTb[h]
                        P.op("dve", lambda e, psZ=psZ, Qcur=Qcur, qdst=qdst: e.tensor_tensor(out=qdst, in0=Qcur, in1=psZ[:, 0:128], op=ALU.subtract),
                             reads=[pkZ, kQ[cur]], writes=[kQ[nxt] if lev < LAST else ("gdn", "TT", h)])
                        if lev < LAST:
                            P.op("dve", lambda e, psZ=psZ, Tcur=Tcur, Tnxt=Tnxt: e.tensor_tensor(out=Tnxt, in0=Tcur, in1=psZ[:, 128:256], op=ALU.subtract),
                                 reads=[pkZ, kT[cur]], writes=[kT[nxt]])
                    TT = TTb[h]
                    kTT = ("gdn", "TT", h)
                    if first:
                        k.dbg("gdn_TT", TT, [128, 128], [kTT])
                    yield "TAIL"
                    psU, pkU = k.psn()
                    P.op("pe", lambda e, psU=psU, TT=TT, h=h: e.matmul(psU[:, 0:64], lhsT=TT, rhs=VBKB[:, 0, h, :], start=True, stop=True),
                         reads=[kTT, ("gdn", "VBKB")], writes=[pkU])
                    P.op("pe", lambda e, psU=psU, TT=TT, h=h, bs=bs: e.matmul(psU[bs, 128:256], lhsT=VBKB[:, 1, h, :], rhs=TT, start=True, stop=True),
                         reads=[kTT, ("gdn", "VBKB")], writes=[pkU])
                    P.op("act", lambda e, psU=psU, h=h: e.activation(out=uS[:, h, :], in_=psU[:, 0:64], func=AF.Copy), reads=[pkU], writes=[("gdn", "u", h)])
                    P.op("act", lambda e, psU=psU, bs=bs, pr=pr: e.activation(out=wT[bs, pr, :], in_=psU[bs, 128:256], func=AF.Copy), reads=[pkU], writes=[("gdn", "wT", h)])
                    yield
                    psV, pkV = k.psn()
                    P.op("pe", lambda e, psV=psV, bs=bs, pr=pr: e.matmul(psV[:, 0:64], lhsT=wT[bs, pr, :], rhs=Sbf[bs, pr, :], start=True, stop=True),
                         reads=[("gdn", "wT", h), ("gdn_S", h)], writes=[pkV])
                    P.op("dve", lambda e, psV=psV, h=h: e.tensor_tensor(out=vn[:, h, :], in0=uS[:, h, :], in1=psV[:, 0:64], op=ALU.subtract),
                         reads=[pkV, ("gdn", "u", h)], writes=[("gdn", "vn", h)])
                    yield
                    psO1, pkO1 = k.psn()
                    P.op("pe", lambda e, psO1=psO1, qn=qn, bs=bs, pr=pr, ts=ts: e.matmul(psO1[:, 0:64], lhsT=k.hT[bs, pr, ts], rhs=Sbf[bs, pr, :], start=True, stop=True),
                         reads=kHT + [("gdn_S", h)], writes=[pkO1])
                    P.op("act", lambda e, psO1=psO1, h=h: e.activation(out=otmp[:, h, :], in_=psO1[:, 0:64], func=AF.Copy, scale=c["egc"][:, h:h + 1]),
                         reads=[pkO1, "gdn_egc"], writes=[("gdn", "otmp", h)])
                    yield
                    psO2, pkO2 = k.psn()
                    P.op("pe", lambda e, psO2=psO2, at=at, h=h: e.matmul(psO2[:, 0:64], lhsT=at, rhs=vn[:, h, :], start=True, stop=True),
                         reads=[kat, ("gdn", "vn", h)], writes=[pkO2])
                    oslice = Otok[:, ch, h * 64:(h + 1) * 64]
                    if d == 0:
                        P.op("dve", lambda e, psO2=psO2, h=h, oslice=oslice: e.tensor_tensor(out=oslice, in0=otmp[:, h, :], in1=psO2[:, 0:64], op=ALU.add),
                             reads=[pkO2, ("gdn", "otmp", h)], writes=kOt)
                    else:
                        P.op("dve", lambda e, psO2=psO2, h=h: e.tensor_tensor(out=otmp[:, h, :], in0=otmp[:, h, :], in1=psO2[:, 0:64], op=ALU.add),
                             reads=[pkO2, ("gdn", "otmp", h)], writes=[("gdn", "otmp", h)])
                        P.op("dve", lambda e, h=h, oslice=oslice: e.tensor_tensor(out=oslice, in0=oslice, in1=otmp[:, h, :], op=ALU.add),
                             reads=[("gdn", "otmp", h)] + kOt, writes=kOt)
                    yield
                    psS, pkS = k.psn()
                    P.op("pe", lambda e, psS=psS, h=h, bs=bs: e.matmul(psS[bs, 0:64], lhsT=kt[:, h, :], rhs=vn[:, h, :], start=True, stop=True),
                         reads=[("gdn", "kt", h), ("gdn", "vn", h)], writes=[pkS])
                    P.op("dve", lambda e, psS=psS, h=h, bs=bs, pr=pr: e.scalar_tensor_tensor(
                        out=c["S"][bs, pr, :], in0=c["S"][bs, pr, :], scalar=c["gam"][bs, h:h + 1], in1=psS[bs, 0:64], op0=ALU.mult, op1=ALU.add),
                        reads=[pkS, ("gdn_S", h), "gdn_gam"], writes=[("gdn_S", h)])
                    P.op("act", lambda e, bs=bs, pr=pr: e.activation(out=Sbf[bs, pr, :], in_=c["S"][bs, pr, :], func=AF.Copy), reads=[("gdn_S", h)], writes=[("gdn_S", h)])
                    if first:
                        k.dbg("gdn_S0", c["S"][0:64, 0, :], [64, 64], [("gdn_S", h)])
                        k.dbg("gdn_kt", kt[:, 0, :], [128, 64], [("gdn", "kt", 0)])
                        k.dbg("gdn_vn", vn[:, 0, :], [128, 64], [("gdn", "vn", 0)])
                        k.dbg("gdn_gam", c["gam"][:], [128, 4], ["gdn_gam"])

                def lockstep(gens_, stop_at_tail):
                    active = list(gens_)
                    paused = []
                    while active:
                        for gq in list(active):
                            try:
                                v = next(gq)
                            except StopIteration:
                                active.remove(gq)
                                continue
                            if v == "TAIL" and gq in stop_at_tail:
                                active.remove(gq)
                                paused.append(gq)
                    return paused

                gA = [unit(0), unit(1)]
                lockstep(gA, gA)
                gB = [unit(2), unit(3)]
                lockstep(gA + gB, gB)
                lockstep(gB, [])
            if g == 0:
                for h in range(4):
                    b, pr = h % 2, h // 2
                    P.op("sp", lambda e, h=h, b=b, pr=pr, s=s, d=d: e.dma_start(out=D["ns_gdn"][s, l, d, h, :, :], in_=c["S"][b * 64:(b + 1) * 64, pr, :]), reads=[("gdn_S", h)], dma=True)
    k.dbg("gdn_o%d%d" % (l, g), Otok, [128, 8, 256], kOt)
    for q in range(2):
        P.op("act", lambda e, q=q: e.activation(out=big(k, 6 + q), in_=big(k, 6 + q), func=AF.Silu), reads=[bigkey(6 + q)], writes=[bigkey(6 + q)])
    sqb = big(k, 15)[:, 256:512]
    for ch in range(8):
        ts = slice(ch * 128, (ch + 1) * 128)
        P.op("dve", lambda e, ch=ch: e.tensor_tensor(out=sqb, in0=Otok[:, ch, :], in1=Otok[:, ch, :], op=ALU.mult), reads=kOt, writes=[bigkey(15)])
        P.op("dve", lambda e: e.tensor_reduce(out=c["ss"][:], in_=sqb.rearrange("p (h x) -> p h x", h=4), axis=AX.X, op=ALU.add), reads=[bigkey(15)], writes=["gdn_ss"])
        P.op("act", lambda e: e.activation(out=c["ss"][:], in_=c["ss"][:], func=AF.Ln, scale=1.0 / 64, bias=EPS), reads=["gdn_ss"], writes=["gdn_ss"])
        P.op("act", lambda e: e.activation(out=c["ss"][:], in_=c["ss"][:], func=AF.Exp, scale=-0.5), reads=["gdn_ss"], writes=["gdn_ss"])
        P.op("dve", lambda e, ch=ch: e.tensor_tensor(out=sqb.rearrange("p (h x) -> p h x", h=4), in0=Otok[:, ch, :].rearrange("p (h x) -> p h x", h=4),
                                                     in1=c["ss"][:].unsqueeze(2).to_broadcast([128, 4, 64]), op=ALU.mult), reads=kOt + ["gdn_ss"], writes=[bigkey(15)])
        P.op("dve", lambda e: e.tensor_tensor(out=sqb.rearrange("p (h x) -> p h x", h=4), in0=sqb.rearrange("p (h x) -> p h x", h=4),
                                              in1=c["gn"][:].unsqueeze(1).to_broadcast([128, 4, 64]), op=ALU.mult), reads=[bigkey(15), "gdn_gn"], writes=[bigkey(15)])
        psT, pkT = k.psn()
        for q in range(2):
            P.op("pe", lambda e, q=q, psT=psT: e.transpose(psT[:, q * 128:(q + 1) * 128], sqb[:, q * 128:(q + 1) * 128], k.ident[:]), reads=[bigkey(15), "ident"], writes=[pkT])
        P.op("dve", lambda e, psT=psT, ts=ts: e.tensor_tensor(out=k.mixT[:, 6:8, ts], in0=psT[:, 0:256].rearrange("p (q t) -> p q t", q=2),
                                                             in1=k.BIG[:, 6:8, ts], op=ALU.mult), reads=[pkT, bigkey(6), bigkey(7)], writes=[("mixT", 6), ("mixT", 7)])
I32 = mybir.dt.int32
TWO_PI = 6.283185307179586


def P_hy(k):
    if hasattr(k, "_hy"):
        return k._hy
    P = k.P
    k._hy = {
        "conv": P.sb("hy_conv", [128, 6, 3]),
        "w1": P.sb("hy_w1", [33, 64]),
        "w2": P.sb("hy_w2", [64, 64]),
        "w3": P.sb("hy_w3", [64, 512]),
        "vec": P.sb("hy_vec", [64, 6]),
        "bias": P.sb("hy_bias", [128, 2]),
        "ki": P.sb("hy_ki", [64, 1024], I32),
        "rn": P.sb("hy_rn", [128, 256]),
    }
    return k._hy


def hy_sin(k, dst, src_ps, pk, scale_ap, bias_ap, n, dkey):
    P = k.P
    c = P_hy(k)
    ki = c["ki"][:, 0:n]
    tmp = big(k, 7)[0:64, 0:n]
    kt = bigkey(7)
    P.op("act", lambda e: e.activation(out=dst, in_=src_ps, func=AF.Identity, scale=scale_ap, bias=bias_ap), reads=[pk, "hy_vec"], writes=[dkey])
    P.op("dve", lambda e: e.tensor_scalar(out=ki, in0=dst, scalar1=1.0 / TWO_PI, scalar2=None, op0=ALU.mult), reads=[dkey], writes=["hy_ki"])
    P.op("dve", lambda e: e.tensor_copy(out=tmp, in_=ki), reads=["hy_ki"], writes=[kt])
    P.op("dve", lambda e: e.scalar_tensor_tensor(out=dst, in0=tmp, scalar=-TWO_PI, in1=dst, op0=ALU.mult, op1=ALU.add), reads=[kt, dkey], writes=[dkey])
    P.op("dve", lambda e: e.tensor_single_scalar(out=tmp, in_=dst, scalar=math.pi, op=ALU.is_gt), reads=[dkey], writes=[kt])
    P.op("dve", lambda e: e.scalar_tensor_tensor(out=dst, in0=tmp, scalar=-TWO_PI, in1=dst, op0=ALU.mult, op1=ALU.add), reads=[kt, dkey], writes=[dkey])
    P.op("dve", lambda e: e.tensor_single_scalar(out=tmp, in_=dst, scalar=-math.pi, op=ALU.is_lt), reads=[dkey], writes=[kt])
    P.op("dve", lambda e: e.scalar_tensor_tensor(out=dst, in0=tmp, scalar=TWO_PI, in1=dst, op0=ALU.mult, op1=ALU.add), reads=[kt, dkey], writes=[dkey])
    P.op("act", lambda e: e.activation(out=dst, in_=dst, func=AF.Sin), reads=[dkey], writes=[dkey])


def hyena_filter(k, l, L):
    P, D = k.P, k.D
    c = P_hy(k)
    ONES = k.masks[:, 4, :]
    nb = L // 128
    sfx = "_%d" % L
    P.op("sp", lambda e: e.dma_start(out=c["w1"][:], in_=D["hy_w1"][l, :, :]), writes=["hy_w1"], dma=True)
    P.op("sp", lambda e: e.dma_start(out=c["w2"][:], in_=D["hy_w2"][l, :, :]), writes=["hy_w2"], dma=True)
    P.op("sp", lambda e: e.dma_start(out=c["w3"][:], in_=D["hy_w3"][l, :, :]), writes=["hy_w3"], dma=True)
    P.op("sp", lambda e: e.dma_start(out=c["vec"][:, 0:4], in_=D["hy_vec"][:, l, :]), writes=["hy_vec"], dma=True)
    P.op("dve", lambda e: e.tensor_tensor(out=c["vec"][:, 4:5], in0=c["vec"][:, 0:1], in1=c["vec"][:, 1:2], op=ALU.mult), reads=["hy_vec"], writes=["hy_vec"])
    P.op("dve", lambda e: e.tensor_tensor(out=c["vec"][:, 5:6], in0=c["vec"][:, 2:3], in1=c["vec"][:, 3:4], op=ALU.mult), reads=["hy_vec"], writes=["hy_vec"])
    feats = big(k, 4)[0:33, 0:L]
    h1 = big(k, 5)[0:64, 0:L]
    h2 = big(k, 6)[0:64, 0:L]
    P.op("sp", lambda e: e.dma_start(out=feats, in_=D["hy_featsT" + sfx][:, :]), writes=[bigkey(4)], dma=True)
    nt = max(1, L // 512)
    tw = min(L, 512)
    for (src, wkey, wt, dst, dslot, sc, bi) in ((feats, "hy_w1", c["w1"], h1, 5, 1, 4), (h1, "hy_w2", c["w2"], h2, 6, 3, 5)):
        for t in range(nt):
            ps, pk = k.psn()
            kin = 33 if dslot == 5 else 64
            P.op("pe", lambda e, ps=ps, wt=wt, src=src, t=t, kin=kin: e.matmul(ps[0:64, 0:tw], lhsT=wt[0:kin, :], rhs=src[:, t * tw:(t + 1) * tw], start=True, stop=True),
                 reads=[wkey, bigkey(4), bigkey(5)], writes=[pk])
            hy_sin(k, dst[:, t * tw:(t + 1) * tw], ps[0:64, 0:tw], pk, c["vec"][:, sc:sc + 1], c["vec"][:, bi:bi + 1], tw, bigkey(dslot))
    hraw = k.BIG[:, 8:12, :].rearrange("p a (b x) -> p (a b) x", x=512)
    kraw = [bigkey(8), bigkey(9), bigkey(10), bigkey(11)]
    env = k.BIG[:, 12:14, :].rearrange("p a (b x) -> p (a b) x", x=256)
    P.op("sp", lambda e: e.dma_start(out=env[:, 0:nb, :], in_=D["hy_env" + sfx].rearrange("(b p) c -> p b c", p=128)), writes=[bigkey(12), bigkey(13)], dma=True)
    habs = big(k, 2)[:, 0:512]
    psN, pkN = k.psn(hold=True)
    for b in range(nb):
        ps, pk = k.psn()
        P.op("pe", lambda e, ps=ps, b=b: e.matmul(ps[:, :], lhsT=h2[:, b * 128:(b + 1) * 128], rhs=c["w3"][:, :], start=True, stop=True),
             reads=[bigkey(6), "hy_w3"], writes=[pk])
        P.op("dve", lambda e, ps=ps, b=b: e.tensor_tensor(out=hraw[:, b, :].rearrange("p (a x) -> p a x", a=2), in0=ps[:, :].rearrange("p (a x) -> p a x", a=2),
                                                          in1=env[:, b, :].unsqueeze(1).to_broadcast([128, 2, 256]), op=ALU.mult),
             reads=[pk, bigkey(12), bigkey(13)], writes=kraw)
        P.op("act", lambda e, b=b: e.activation(out=habs, in_=hraw[:, b, :], func=AF.Abs), reads=kraw, writes=[bigkey(2)])
        P.op("pe", lambda e, psN=psN, b=b: e.matmul(psN[:, :], lhsT=ONES, rhs=habs, start=(b == 0), stop=(b == nb - 1)), reads=[bigkey(2), "masks"], writes=[pkN])
    k.ps_held.discard(pkN[1])
    P.op("act", lambda e, psN=psN: e.activation(out=c["rn"][:], in_=psN[:, 0:256], func=AF.Copy), reads=[pkN], writes=["hy_rn"])
    P.op("dve", lambda e, psN=psN: e.tensor_tensor(out=c["rn"][:], in0=c["rn"][:], in1=psN[:, 256:512], op=ALU.add), reads=[pkN, "hy_rn"], writes=["hy_rn"])
    P.op("dve", lambda e: e.reciprocal(out=c["rn"][:], in_=c["rn"][:]), reads=["hy_rn"], writes=["hy_rn"])
    hs = bfv(k, 14).rearrange("p (b x) -> p b x", x=256)
    hd = bfv(k, 15).rearrange("p (b x) -> p b x", x=256)
    tmpn = big(k, 2)[:, 512:768]
    for b in range(nb):
        P.op("dve", lambda e, b=b: e.tensor_tensor(out=tmpn, in0=hraw[:, b, 0:256], in1=hraw[:, b, 256:512], op=ALU.add), reads=kraw, writes=[bigkey(2)])
        P.op("dve", lambda e, b=b: e.tensor_tensor(out=hs[:, b, :], in0=tmpn, in1=c["rn"][:], op=ALU.mult), reads=[bigkey(2), "hy_rn"], writes=[bigkey(14)])
        P.op("dve", lambda e, b=b: e.tensor_tensor(out=tmpn, in0=hraw[:, b, 256:512], in1=hraw[:, b, 0:256], op=ALU.subtract), reads=kraw, writes=[bigkey(2)])
        P.op("dve", lambda e, b=b: e.tensor_tensor(out=hd[:, b, :], in0=tmpn, in1=c["rn"][:], op=ALU.mult), reads=[bigkey(2), "hy_rn"], writes=[bigkey(15)])
    dst = D["hyf_%d_%d" % (l, L)]
    fkey = ("hyf", l, L)
    P.op("sp", lambda e: e.dma_start(out=dst[0].rearrange("p (b x) -> p b x", x=256), in_=hs[:, 0:nb, :]), reads=[bigkey(14)], writes=[fkey], dma=True)
    P.op("sp", lambda e: e.dma_start(out=dst[1].rearrange("p (b x) -> p b x", x=256), in_=hd[:, 0:nb, :]), reads=[bigkey(15)], writes=[(fkey, 1)], dma=True)


def hyena_mixer(k, l, g, nseq, L):
    P, D = k.P, k.D
    c = P_hy(k)
    nb = L // 128
    sfx = "_%d" % L
    P.op("sp", lambda e: e.dma_start(out=c["conv"][:], in_=D["hy_conv_fm"][:, l, :, :]), writes=["hy_conv"], dma=True)
    P.op("sp", lambda e: e.dma_start(out=c["bias"][:], in_=D["hy_bias_fm"][:, l, :]), writes=["hy_bias"], dma=True)
    hs = bfv(k, 14).rearrange("p (b x) -> p b x", x=256)
    hd = bfv(k, 15).rearrange("p (b x) -> p b x", x=256)
    src = D["hyf_%d_%d" % (l, L)]
    fkey = ("hyf", l, L)
    P.op("sp", lambda e: e.dma_start(out=hs[:, 0:nb, :], in_=src[0].rearrange("p (b x) -> p b x", x=256)), reads=[fkey], writes=[bigkey(14)], dma=True)
    P.op("sp", lambda e: e.dma_start(out=hd[:, 0:nb, :], in_=src[1].rearrange("p (b x) -> p b x", x=256)), reads=[(fkey, 1)], writes=[bigkey(15)], dma=True)
    for gi in range(3):
        wb, wk = k.w_next()
        w3v = wb[:, 0:2048].rearrange("p (kc c) -> p kc c", c=256)
        proj_fm(k, w3v, wk, [(0, 128), (128, 128)], evac_to_big(k, lambda i, gi=gi: gi * 2 + i))
    S = 8
    for j in range(6):
        dwconv_fm(k, j, S + j, c["conv"][:, j, :], "hy_conv", 3, nseq, L)
    for q in range(2):
        P.op("dve", lambda e, q=q: e.tensor_tensor(out=big(k, S + 4 + q), in0=big(k, S + 4 + q), in1=big(k, S + 2 + q), op=ALU.mult),
             reads=[bigkey(S + 4 + q), bigkey(S + 2 + q)], writes=[bigkey(S + 4 + q)])
    ztok = bfv(k, 18).rearrange("p (b x) -> p b x", x=256)
    for b2 in range(4):
        ps, pk = k.psn()
        for bb in range(2):
            b = b2 * 2 + bb
            for q in range(2):
                P.op("pe", lambda e, ps=ps, b=b, bb=bb, q=q: e.transpose(ps[:, bb * 256 + q * 128:bb * 256 + (q + 1) * 128], big(k, S + 4 + q)[:, b * 128:(b + 1) * 128], k.ident[:]),
                     reads=[bigkey(S + 4 + q), "ident"], writes=[pk])
        P.op("act", lambda e, ps=ps, b2=b2: e.activation(out=ztok[:, b2 * 2:b2 * 2 + 2, :], in_=ps[:, :].rearrange("p (b x) -> p b x", b=2), func=AF.Copy),
             reads=[pk], writes=[bigkey(18)])
    Ysp = k.BIG[:, 0:2, :].bitcast(BF16).rearrange("p a (j x) -> p (a j) x", x=512)
    kY = [bigkey(0), bigkey(1)]
    G2 = big(k, 2)[:, 0:512]
    G3 = big(k, 2)[:, 512:1024]
    Atmp = big(k, 3)[:, 0:512]
    Btmp = big(k, 3)[:, 512:1024]
    for fb in range(nb):
        st = (16, 17, 6, 7)[fb % 4]
        CS = bfv(k, st)[:, 0:nb * 256].rearrange("p (b x) -> p b x", x=256)
        P.op("sp", lambda e, CS=CS, fb=fb: e.dma_start(out=CS, in_=D["dftF" + sfx][:, fb, :, :].rearrange("(b p) c f -> p b (c f)", p=128)), writes=[bigkey(st)], dma=True)
        psG, pkG = k.psn()
        for b in range(nb):
            P.op("pe", lambda e, psG=psG, CS=CS, b=b: e.matmul(psG[:, 0:256], lhsT=CS[:, b, 0:128], rhs=hs[:, b, :], start=(b == 0), stop=(b == nb - 1)),
                 reads=[bigkey(st), bigkey(14)], writes=[pkG])
        for b in range(nb):
            P.op("pe", lambda e, psG=psG, CS=CS, b=b: e.matmul(psG[:, 256:512], lhsT=CS[:, b, 128:256], rhs=hd[:, b, :], start=(b == 0), stop=(b == nb - 1)),
                 reads=[bigkey(st), bigkey(15)], writes=[pkG])
        P.op("act", lambda e, psG=psG: e.activation(out=G2.rearrange("p (a x) -> p a x", a=2), in_=psG[:, 0:256].unsqueeze(1).to_broadcast([128, 2, 256]), func=AF.Copy),
             reads=[pkG], writes=[bigkey(2)])
        P.op("act", lambda e, psG=psG: e.activation(out=G3.rearrange("p (a x) -> p a x", a=2), in_=psG[:, 256:512].unsqueeze(1).to_broadcast([128, 2, 256]), func=AF.Copy),
             reads=[pkG], writes=[bigkey(2)])
        for s in range(nseq):
            psZ, pkZ = k.psn()
            for b in range(nb):
                P.op("pe", lambda e, psZ=psZ, CS=CS, b=b, s=s: e.matmul(psZ[:, 0:256], lhsT=CS[:, b, 0:128], rhs=ztok[:, s * nb + b, :], start=(b == 0), stop=(b == nb - 1)),
                     reads=[bigkey(st), bigkey(18)], writes=[pkZ])
            for b in range(nb):
                P.op("pe", lambda e, psZ=psZ, CS=CS, b=b, s=s: e.matmul(psZ[:, 256:512], lhsT=CS[:, b, 128:256], rhs=ztok[:, s * nb + b, :], start=(b == 0), stop=(b == nb - 1)),
                     reads=[bigkey(st), bigkey(18)], writes=[pkZ])
            yi = fb * nseq + s
            P.op("dve", lambda e, psZ=psZ: e.tensor_tensor(out=Atmp, in0=psZ[:, :], in1=G2, op=ALU.mult), reads=[pkZ, bigkey(2)], writes=[bigkey(3)])
            P.op("dve", lambda e, psZ=psZ: e.tensor_tensor(out=Btmp, in0=psZ[:, :], in1=G3, op=ALU.mult), reads=[pkZ, bigkey(2)], writes=[bigkey(3)])
            P.op("dve", lambda e, yi=yi: e.tensor_tensor(out=Ysp[:, yi, 0:256], in0=Atmp[:, 0:256], in1=Btmp[:, 256:512], op=ALU.add), reads=[bigkey(3)], writes=kY)
            P.op("dve", lambda e, yi=yi: e.tensor_tensor(out=Ysp[:, yi, 256:512], in0=Atmp[:, 256:512], in1=Btmp[:, 0:256], op=ALU.subtract), reads=[bigkey(3)], writes=kY)
    tw = min(L, 512)
    tiles = [(s, cc, th) for s in range(nseq) for cc in range(2) for th in range(L // tw)]
    for bi0 in range(0, len(tiles), 4):
        batch = tiles[bi0:bi0 + 4]
        pss = [k.psn() for _ in batch]
        for fb in range(nb):
            st = (16, 17, 6, 7)[fb % 4]
            CST = bfv(k, st)[:, 0:2 * L].rearrange("p (a t) -> p a t", a=2)
            P.op("sp", lambda e, CST=CST, fb=fb: e.dma_start(out=CST, in_=D["dftI" + sfx][fb * 128:(fb + 1) * 128, :, :]), writes=[bigkey(st)], dma=True)
            for ti, (s, cc, th) in enumerate(batch):
                ps, pk = pss[ti]
                yi = fb * nseq + s
                for a in range(2):
                    P.op("pe", lambda e, ps=ps, CST=CST, yi=yi, cc=cc, th=th, a=a, fb=fb: e.matmul(
                        ps[:, 0:tw], lhsT=Ysp[:, yi, a * 256 + cc * 128:a * 256 + (cc + 1) * 128], rhs=CST[:, a, th * tw:(th + 1) * tw],
                        start=(fb == 0 and a == 0), stop=(fb == nb - 1 and a == 1)), reads=kY + [bigkey(st)], writes=[pk])
        for ti, (s, cc, th) in enumerate(batch):
            ps, pk = pss[ti]
            tsl = slice(s * L + th * tw, s * L + (th + 1) * tw)
            tmp = big(k, 4 + ti % 2)[:, 0:tw]
            P.op("dve", lambda e, ps=ps, cc=cc, tsl=tsl, tmp=tmp: e.scalar_tensor_tensor(
                out=tmp, in0=big(k, S + 4 + cc)[:, tsl], scalar=c["bias"][:, cc:cc + 1], in1=ps[:, 0:tw], op0=ALU.mult, op1=ALU.add),
                reads=[pk, bigkey(S + 4 + cc), "hy_bias"], writes=[bigkey(4 + ti % 2)])
            P.op("dve", lambda e, cc=cc, tsl=tsl, tmp=tmp: e.tensor_tensor(out=k.mixT[:, cc, tsl], in0=tmp, in1=big(k, S + cc)[:, tsl], op=ALU.mult),
                 reads=[bigkey(4 + ti % 2), bigkey(S + cc)], writes=[("mixT", cc)])
import math
def _grid_pos_embed(n_tokens, d_model=1024, grid_w=64):
    rows = n_tokens // grid_w
    rr, cc = np.meshgrid(np.arange(rows, dtype=np.float32), np.arange(grid_w, dtype=np.float32), indexing='ij')
    quarter = d_model // 4
    omega = (1.0 / (np.float32(10000.0) ** (np.arange(quarter, dtype=np.float32) / np.float32(quarter)))).astype(np.float32)
    def enc(pos):
        ang = pos.reshape(-1)[:, None].astype(np.float32) * omega[None, :]
        return np.concatenate([np.sin(ang), np.cos(ang)], axis=-1)
    return np.concatenate([enc(rr), enc(cc)], axis=-1).astype(np.float32)


def _fm(v, nchunk):
    v = np.asarray(v, np.float32)
    lead = v.shape[:-1]
    v = v.reshape(lead + (nchunk, 128))
    v = np.moveaxis(v, -1, 0)
    return np.ascontiguousarray(v)


def make_inputs(inp):
    f = lambda a: np.ascontiguousarray(np.asarray(a, np.float32))
    shared = {}
    shared["pos"] = _grid_pos_embed(1024)
    shared["ident"] = np.eye(128, dtype=np.float32)
    for n in ("w_mod", "w_in", "w_out", "w_gate", "w_up", "w_down"):
        shared[n] = f(inp[n])
    shared["b_mod_fm"] = _fm(inp["b_mod"], 48)
    shared["g_mix_fm"] = _fm(inp["g_mix"], 8)
    shared["g_ffn_fm"] = _fm(inp["g_ffn"], 8)
    shared["g_final"] = f(inp["g_final"]).reshape(1, 1024)
    lc = np.asarray(inp["lru_conv"], np.float32)
    shared["lru_conv_fm"] = np.ascontiguousarray(lc.reshape(2, 4, 2, 128).transpose(3, 0, 2, 1))
    wbd = np.zeros((2, 2, 2, 2, 128, 128), np.float32)
    for ri, nm in enumerate(("lru_w_r", "lru_w_i")):
        w = np.asarray(inp[nm], np.float32)
        for q in range(2):
            for hh in range(2):
                wbd[:, ri, :, q, hh * 64:(hh + 1) * 64, hh * 64:(hh + 1) * 64] = w[:, :, 2 * q + hh]
    shared["lru_wbd"] = wbd
    b = np.stack([np.asarray(inp["lru_b_r"], np.float32), np.asarray(inp["lru_b_i"], np.float32)], axis=1)
    shared["lru_b_fm"] = _fm(b, 2)
    shared["lru_lam_fm"] = _fm(inp["lru_lambda"], 2)
    r = np.arange(128)
    LE = (r[:, None] <= r[None, :]); GE = (r[:, None] >= r[None, :]); GT = (r[:, None] > r[None, :]); LT = (r[:, None] < r[None, :])
    shared["masks"] = np.ascontiguousarray(np.stack([LE, GE, GT, LT, np.ones((128, 128), bool), (r[:, None] // 64 == r[None, :] // 64)], axis=1).astype(np.float32))
    sc = np.asarray(inp["ssd_conv"], np.float32)
    shared["ssd_conv_fm"] = np.ascontiguousarray(sc.reshape(2, 4, 4, 128).transpose(3, 0, 2, 1))
    bc = lambda v: np.ascontiguousarray(np.broadcast_to(np.asarray(v, np.float32).reshape(1, 2, 8), (128, 2, 8)))
    shared["ssd_dtb_bc"] = bc(inp["ssd_dt_bias"])
    shared["ssd_alog_bc"] = bc(inp["ssd_a_log"])
    shared["ssd_d_fm"] = _fm(np.repeat(np.asarray(inp["ssd_d"], np.float32), 64, axis=-1), 2)
    shared["ssd_norm_fm"] = _fm(inp["ssd_norm"], 2)
    shared["masks2"] = np.ascontiguousarray(np.stack([np.concatenate([GT, LE], axis=1), np.concatenate([LT, GE], axis=1)], axis=1).astype(np.float32))
    bd = lambda m: (r[:, None] // m == r[None, :] // m)
    shared["masks4"] = np.ascontiguousarray(np.stack([bd(32), bd(64) & ~bd(32), ~bd(64), ~bd(64)], axis=1).astype(np.float32))
    gc = np.asarray(inp["gdn_conv"], np.float32)
    shared["gdn_conv_fm"] = np.ascontiguousarray(gc.reshape(2, 4, 6, 128).transpose(3, 0, 2, 1))
    shared["gdn_dtb_bc"] = bc(inp["gdn_dt_bias"])
    shared["gdn_alog_bc"] = bc(inp["gdn_a_log"])
    shared["gdn_norm_bc"] = np.ascontiguousarray(np.broadcast_to(np.asarray(inp["gdn_norm"], np.float32).reshape(1, 2, 64), (128, 2, 64)))
    import ml_dtypes
    hc = np.asarray(inp["hy_conv"], np.float32)
    shared["hy_conv_fm"] = np.ascontiguousarray(hc.reshape(2, 3, 6, 128).transpose(3, 0, 2, 1))
    for n in ("hy_w1", "hy_w2", "hy_w3"):
        shared[n] = f(inp[n])
    hv = np.stack([np.asarray(inp["hy_b1"], np.float32), np.asarray(inp["hy_freq"], np.float32)[:, 0],
                   np.asarray(inp["hy_b2"], np.float32), np.asarray(inp["hy_freq"], np.float32)[:, 1]], axis=-1)
    shared["hy_vec"] = np.ascontiguousarray(hv.transpose(1, 0, 2))
    shared["hy_bias_fm"] = _fm(inp["hy_bias"], 2)
    for L_ in (256, 1024):
        t = np.linspace(0.0, 1.0, L_, dtype=np.float32)[:, None]
        w_ = (np.float32(2.0 * math.pi / L_) * np.arange(L_, dtype=np.float32))[:, None]
        bands = np.linspace(1e-4, 15, 16, dtype=np.float32)[None, :]
        feats = np.concatenate([t, np.cos(bands * w_), -np.sin(bands * w_)], axis=-1).astype(np.float32)
        shared["hy_featsT_%d" % L_] = np.ascontiguousarray(feats.T)
        max_decay = math.log(1e-2) / 0.3
        min_decay = math.log(1e-2) / 1.5
        deltas = np.abs(np.linspace(min_decay, max_decay, 256, dtype=np.float32))
        shared["hy_env_%d" % L_] = np.exp(-t * deltas).astype(np.float32)
        n_ = 2 * L_
        idx = np.arange(L_, dtype=np.float64)
        th = 2.0 * np.pi * np.outer(idx, idx + 0.5) / n_
        Cm, Sm = np.cos(th), np.sin(th)
        F = np.stack([Cm.reshape(L_, L_ // 128, 128), Sm.reshape(L_, L_ // 128, 128)], axis=2)
        shared["dftF_%d" % L_] = np.ascontiguousarray(F).astype(ml_dtypes.bfloat16)
        Iv = np.stack([Cm.T, Sm.T], axis=1) * (2.0 / n_)
        shared["dftI_%d" % L_] = np.ascontiguousarray(Iv).astype(ml_dtypes.bfloat16)
    maps = []
    for i in range(8):
        m = dict(shared)
        m["x_ctx"] = f(inp["x_prompt"][4 * i:4 * i + 4]).reshape(1024, 1024)
        m["x_lat"] = f(inp["x_sample"][i])
        cv = np.stack([np.asarray(inp["c_ctx"], np.float32), np.asarray(inp["c"][i], np.float32)], axis=0)
        m["cv"] = _fm(cv, 8)
        m["st_lru_fm"] = _fm(inp["state_lru"][i], 2)
        m["st_ssd"] = f(inp["state_ssd"][i])
        m["st_gdn"] = f(inp["state_gdn"][i])
        maps.append(m)
    return maps


def kernel(**inputs):
    maps = make_inputs(inputs)
    nc, _k = build()
    res = run_bass_kernel_spmd(nc, maps, core_ids=list(range(8)))
    R = res.results
    y_prompt = np.concatenate([np.asarray(R[i]["y_ctx"], np.float32).reshape(4, 256, 1024) for i in range(8)], axis=0)
    y_sample = np.stack([np.asarray(R[i]["y_lat"], np.float32) for i in range(8)], axis=0)
    ns_lru = np.concatenate([np.asarray(R[i]["ns_lru"], np.float32).reshape(4, 2, 2, 256) for i in range(8)], axis=0)
    ns_ssd = np.concatenate([np.asarray(R[i]["ns_ssd"], np.float32) for i in range(8)], axis=0)
    ns_gdn = np.concatenate([np.asarray(R[i]["ns_gdn"], np.float32) for i in range(8)], axis=0)
    return (y_prompt, y_sample, ns_lru, ns_ssd, ns_gdn)
```

```python
import numpy as np
import concourse.bass as bass
import concourse.mybir as mybir
from concourse.bass_utils import run_bass_kernel_spmd
from contextlib import ExitStack

F32 = mybir.dt.float32
BF16 = mybir.dt.bfloat16
AF = mybir.ActivationFunctionType
ALU = mybir.AluOpType
AX = mybir.AxisListType

ENGS = ("pe", "act", "dve", "pool", "sp")
NDMA = 12


class Prog:
    def __init__(self, nc):
        self.nc = nc
        self.ops = {e: [] for e in ENGS}
        self.last_write = {}
        self.readers = {}
        self.known = {e: {} for e in ENGS}
        self.dma_val = [0] * NDMA
        self.dma_next = 0
        self.es = ExitStack()
        self.sb_bytes = 0

    def sb(self, name, shape, dt=F32):
        t = self.es.enter_context(self.nc.sbuf_tensor("sb_" + name, list(shape), dt))
        n = 1
        for s in shape[1:]:
            n *= s
        self.sb_bytes += n * (2 if dt == BF16 else 4)
        return t

    def ps(self, name, shape, dt=F32):
        return self.es.enter_context(self.nc.psum_tensor("pz_" + name, list(shape), dt))

    def op(self, eng, fn, reads=(), writes=(), dma=False):
        if eng != "pe":
            px = [kk for kk in reads if isinstance(kk, tuple) and kk[0] == "ps"]
            if px:
                writes = list(writes) + [kk for kk in px if kk not in writes]
        deps = []
        raw = set()
        for k in reads:
            t = self.last_write.get(k)
            if t is not None:
                deps.append(t)
                raw.add(t)
        for k in writes:
            t = self.last_write.get(k)
            if t is not None:
                deps.append(t)
            deps.extend(self.readers.get(k, ()))
        waits = []
        kn = self.known[eng]
        mx = {}
        for (stream, idx) in deps:
            if mx.get(stream, 0) < idx:
                mx[stream] = idx
        deps = list(mx.items())
        for t in deps:
            stream, idx = t
            if stream == eng:
                if eng == "pe":
                    continue
            if kn.get(stream, 0) >= idx:
                continue
            kn[stream] = idx
            waits.append(t)
        rec = {"waits": waits, "fn": fn, "signal": False, "dma": None}
        if dma:
            j = self.dma_next % NDMA
            self.dma_next += 1
            prev = self.dma_val[j]
            st = ("d", j)
            if prev > 0 and kn.get(st, 0) < prev:
                kn[st] = prev
                waits.append((st, prev))
            val = prev + 16
            self.dma_val[j] = val
            rec["dma"] = (j, val)
            tok = (st, val)
        else:
            tok = (eng, len(self.ops[eng]) + 1)
        self.ops[eng].append(rec)
        for (stream, idx) in waits:
            if not isinstance(stream, tuple):
                self.ops[stream][idx - 1]["signal"] = True
        for k in writes:
            self.last_write[k] = tok
            self.readers[k] = []
        for k in reads:
            self.readers.setdefault(k, []).append(tok)
        return tok

    def finish(self):
        waits = []
        for j in range(NDMA):
            if self.dma_val[j] > 0:
                waits.append((("d", j), self.dma_val[j]))
        self.ops["sp"].append({"waits": waits, "fn": None, "signal": False, "dma": None})

    def emit(self):
        nc = self.nc
        sems = {e: self.es.enter_context(nc.semaphore("s_" + e)) for e in ENGS}
        dsems = [self.es.enter_context(nc.semaphore("d_%d" % j)) for j in range(NDMA)]
        rank = {}
        for e in ENGS:
            r = [0]
            c = 0
            for o in self.ops[e]:
                if o["signal"]:
                    c += 1
                r.append(c)
            rank[e] = r

        def run(e, eng):
            for o in self.ops[e]:
                for (stream, idx) in o["waits"]:
                    if isinstance(stream, tuple):
                        eng.wait_ge(dsems[stream[1]], idx)
                    else:
                        eng.wait_ge(sems[stream], rank[stream][idx])
                if o["fn"] is None:
                    continue
                ins = o["fn"](eng)
                if o["dma"] is not None:
                    ins.then_inc(dsems[o["dma"][0]], 16)
                elif o["signal"]:
                    ins.then_inc(sems[e], 1)

        with nc.Block() as block:
            @block.tensor
            def _(eng):
                run("pe", eng)

            @block.scalar
            def _(eng):
                run("act", eng)

            @block.vector
            def _(eng):
                run("dve", eng)

            @block.gpsimd
            def _(eng):
                run("pool", eng)

            @block.sync
            def _(eng):
                run("sp", eng)
import math

T = 1024
EPS = 1e-6
D_FF = 2816
NHC = 22
C_HY = 0
C_SSD = 768
C_LRU = 1544
C_GDN = 2056


class K:
    pass


def build(debug=None, stop_after=None):
    nc = bass.Bass("TRN2", target_bir_lowering=False)
    P = Prog(nc)
    k = K()
    k.nc, k.P = nc, P
    k.debug = debug or []
    k.dbg_out = {}

    def din(name, shape, dt=F32):
        return nc.dram_tensor(name, list(shape), dt, kind="ExternalInput").ap()

    def dout(name, shape, dt=F32):
        return nc.dram_tensor(name, list(shape), dt, kind="ExternalOutput").ap()

    k.din, k.dout = din, dout
    D = {}
    D["x0"] = din("x_ctx", [T, 1024])
    D["x1"] = din("x_lat", [T, 1024])
    D["pos"] = din("pos", [T, 1024])
    D["cv"] = din("cv", [128, 2, 8])
    D["ident"] = din("ident", [128, 128])
    D["w_mod"] = din("w_mod", [2, 1024, 6144])
    D["b_mod_fm"] = din("b_mod_fm", [128, 2, 48])
    D["g_mix_fm"] = din("g_mix_fm", [128, 2, 8])
    D["g_ffn_fm"] = din("g_ffn_fm", [128, 2, 8])
    D["g_final"] = din("g_final", [1, 1024])
    D["w_in"] = din("w_in", [2, 1024, 3096])
    D["w_out"] = din("w_out", [2, 1024, 1024])
    D["w_gate"] = din("w_gate", [2, 1024, D_FF])
    D["w_up"] = din("w_up", [2, 1024, D_FF])
    D["w_down"] = din("w_down", [2, D_FF, 1024])
    D["lru_conv_fm"] = din("lru_conv_fm", [128, 2, 2, 4])
    D["lru_wbd"] = din("lru_wbd", [2, 2, 2, 2, 128, 128])
    D["lru_b_fm"] = din("lru_b_fm", [128, 2, 2, 2, 2])
    D["lru_lam_fm"] = din("lru_lam_fm", [128, 2, 2, 2])
    D["st_lru_fm"] = din("st_lru_fm", [128, 2, 2, 2])
    D["masks"] = din("masks", [128, 6, 128])
    D["masks2"] = din("masks2", [128, 2, 256])
    D["hy_conv_fm"] = din("hy_conv_fm", [128, 2, 6, 3])
    D["hy_w1"] = din("hy_w1", [2, 33, 64])
    D["hy_w2"] = din("hy_w2", [2, 64, 64])
    D["hy_w3"] = din("hy_w3", [2, 64, 512])
    D["hy_vec"] = din("hy_vec", [64, 2, 4])
    D["hy_bias_fm"] = din("hy_bias_fm", [128, 2, 2])
    for L_ in (256, 1024):
        D["hy_featsT_%d" % L_] = din("hy_featsT_%d" % L_, [33, L_])
        D["hy_env_%d" % L_] = din("hy_env_%d" % L_, [L_, 256])
        D["dftF_%d" % L_] = din("dftF_%d" % L_, [L_, L_ // 128, 2, 128], BF16)
        D["dftI_%d" % L_] = din("dftI_%d" % L_, [L_, 2, L_], BF16)
    D["masks4"] = din("masks4", [128, 4, 128])
    D["gdn_conv_fm"] = din("gdn_conv_fm", [128, 2, 6, 4])
    D["gdn_dtb_bc"] = din("gdn_dtb_bc", [128, 2, 8])
    D["gdn_alog_bc"] = din("gdn_alog_bc", [128, 2, 8])
    D["gdn_norm_bc"] = din("gdn_norm_bc", [128, 2, 64])
    D["st_gdn"] = din("st_gdn", [2, 2, 4, 64, 64])
    D["ns_gdn"] = dout("ns_gdn", [4, 2, 2, 4, 64, 64])
    D["ssd_conv_fm"] = din("ssd_conv_fm", [128, 2, 4, 4])
    D["ssd_dtb_bc"] = din("ssd_dtb_bc", [128, 2, 8])
    D["ssd_alog_bc"] = din("ssd_alog_bc", [128, 2, 8])
    D["ssd_d_fm"] = din("ssd_d_fm", [128, 2, 2])
    D["ssd_norm_fm"] = din("ssd_norm_fm", [128, 2, 2])
    D["st_ssd"] = din("st_ssd", [2, 2, 4, 64, 64])
    D["ns_ssd"] = dout("ns_ssd", [4, 2, 2, 4, 64, 64])
    for l_ in range(2):
        for L_ in (256, 1024):
            D["hyf_%d_%d" % (l_, L_)] = nc.dram_tensor("hyf_%d_%d" % (l_, L_), [2, 128, (L_ // 128) * 256], BF16, kind="Internal").ap()
    D["y0"] = dout("y_ctx", [T, 1024])
    D["y1"] = dout("y_lat", [T, 1024])
    D["ns_lru"] = dout("ns_lru", [32, 128])
    k.D = D

    k.xT = P.sb("xT", [128, 8, T])
    k.hT = P.sb("hT", [128, 8, T], BF16)
    k.mixT = P.sb("mixT", [128, 8, T], BF16)
    k.NBIG = 19
    k.BIG = P.sb("BIG", [128, k.NBIG, T])
    k.ident = P.sb("ident", [128, 128])
    k.ones_bf = P.sb("ones_bf", [128, 128], BF16)
    k.ident_bf = P.sb("ident_bf", [128, 128], BF16)
    k.masks = P.sb("masks", [128, 6, 128])
    k.masks2 = P.sb("masks2", [128, 2, 256])
    k.masks4 = P.sb("masks4", [128, 4, 128])
    k.modT = P.sb("modT", [128, 96, 2])
    k.MODS = P.sb("MODS", [128, 2, 2, 6, 8])
    k.cvs = P.sb("cvs", [128, 2, 8], BF16)
    k.cvf = P.sb("cvf", [128, 2, 8])
    k.small = P.sb("small", [128, 256])
    k.gfin = P.sb("gfin", [128, 1024])
    k.NWB = 3
    k.wb = [P.sb("wb%d" % i, [128, 2048], BF16) for i in range(k.NWB)]
    k.NSTG = 2
    k.stg = [P.sb("stg%d" % i, [128, 2048]) for i in range(k.NSTG)]
    k.psum = [P.ps("ps%d" % i, [128, 512]) for i in range(8)]
    k.psi = 0

    k.ps_held = set()

    def psn(hold=False):
        while (k.psi % 8) in k.ps_held:
            k.psi += 1
        i = k.psi % 8
        k.psi += 1
        if hold:
            k.ps_held.add(i)
        return k.psum[i], ("ps", i)

    k.psn = psn

    k.sched = []
    k.w_issued = 0
    k.w_consumed = 0

    def w_issue():
        i = k.w_issued
        b = i % k.NWB
        s = i % k.NSTG
        parts, nel = k.sched[i](k.stg[s])
        for pi, (o, a) in enumerate(parts):
            P.op("sp", lambda e, o=o, a=a: e.dma_start(out=o, in_=a), writes=[("stg", s, pi)], dma=True)
        ceng = ("act", "dve")[i % 2]
        rk = [("stg", s, pi) for pi in range(len(parts))]
        if ceng == "act":
            P.op("act", lambda e, b=b, s=s, nel=nel: e.activation(out=k.wb[b][:, 0:nel], in_=k.stg[s][:, 0:nel], func=AF.Copy), reads=rk, writes=[("wb", b)])
        else:
            P.op(ceng, lambda e, b=b, s=s, nel=nel: e.tensor_copy(out=k.wb[b][:, 0:nel], in_=k.stg[s][:, 0:nel]), reads=rk, writes=[("wb", b)])
        k.w_keys[i] = [("wb", b)]
        k.w_issued += 1

    k.w_keys = {}

    def w_next():
        while k.w_issued < min(len(k.sched), k.w_consumed + k.NWB):
            w_issue()
        i = k.w_consumed
        k.w_consumed += 1
        return k.wb[i % k.NWB], k.w_keys[i]

    k.w_next = w_next

    def dbg(name, ap, shape, reads):
        if name not in k.debug:
            return
        o = dout("dbg_" + name, shape, ap.dtype)
        P.op("sp", lambda e: e.dma_start(out=o, in_=ap), reads=reads, dma=True)

    k.dbg = dbg

    build_schedule(k)
    prologue(k)
    for g in range(2):
        group(k, g, stop_after)
    P.finish()
    P.emit()
    print("SBUF bytes/partition:", P.sb_bytes, "ops:", {e: len(P.ops[e]) for e in ENGS})
    return nc, k


def gran_cols(wname, l, c0, n, nk=8):
    def loader(buf, k_=None):
        raise NotImplementedError
    return (wname, l, c0, n, nk)


def build_schedule(k):
    D = k.D
    sched = []

    def cols(wname, l, c0, n, nk=8, k0=0):
        def loader(buf):
            src = D[wname][l, k0 * 128:(k0 + nk) * 128, c0:c0 + n].rearrange("(kc p) c -> p kc c", p=128)
            dst = buf[:, 0:nk * n].rearrange("p (kc c) -> p kc c", c=n)
            return [(dst, src)], nk * n
        return loader

    def cols2(w1, w2, l, c0, n):
        def loader(buf):
            out = []
            for j, wn in enumerate((w1, w2)):
                src = D[wn][l, :, c0:c0 + n].rearrange("(kc p) c -> p kc c", p=128)
                dst = buf[:, j * 8 * n:(j + 1) * 8 * n].rearrange("p (kc c) -> p kc c", c=n)
                out.append((dst, src))
            return out, 16 * n
        return loader

    for l in range(2):
        for gi in range(24):
            sched.append(cols("w_mod", l, gi * 256, 256))
    for g in range(2):
        for l in range(2):
            for (c0, n) in win_granules():
                sched.append(cols("w_in", l, c0, n))
            for gi in range(4):
                sched.append(cols("w_out", l, gi * 256, 256))
            for gi in range(22):
                sched.append(cols2("w_gate", "w_up", l, gi * 128, 128))
            for mc in range(8):
                for kh in range(2):
                    sched.append(cols("w_down", l, mc * 128, 128, nk=11, k0=kh * 11))
    k.sched = sched


def win_granules():
    gr = []
    gr += [(0, 256), (256, 256), (512, 256)]
    gr += [(C_LRU, 256), (C_LRU + 256, 256)]
    gr += [(768, 256), (1024, 256), (1280, 256), (1536, 8)]
    gr += [(2056, 256), (2312, 256), (2568, 256), (2824, 256), (3080, 16)]
    return gr


def prologue(k):
    P, D = k.P, k.D
    P.op("sp", lambda e: e.dma_start(out=k.ident[:], in_=D["ident"][:, :]), writes=["ident"], dma=True)
    P.op("sp", lambda e: e.dma_start(out=k.cvf[:], in_=D["cv"][:, :, :]), writes=["cvf"], dma=True)
    P.op("sp", lambda e: e.dma_start(out=k.small[:, 0:96], in_=D["b_mod_fm"].rearrange("p l c -> p (l c)")), writes=["small"], dma=True)
    P.op("sp", lambda e: e.dma_start(out=k.small[:, 96:112], in_=D["g_mix_fm"].rearrange("p l c -> p (l c)")), writes=["small_g"], dma=True)
    P.op("sp", lambda e: e.dma_start(out=k.small[:, 112:128], in_=D["g_ffn_fm"].rearrange("p l c -> p (l c)")), writes=["small_g2"], dma=True)
    P.op("sp", lambda e: e.dma_start(out=k.gfin[:], in_=D["g_final"][0, :].partition_broadcast(128)), writes=["gfin"], dma=True)
    P.op("dve", lambda e: e.memset(k.ones_bf[:], 1.0), writes=["ones"])
    P.op("sp", lambda e: e.dma_start(out=k.masks[:], in_=D["masks"][:, :, :]), writes=["masks"], dma=True)
    P.op("sp", lambda e: e.dma_start(out=k.masks2[:], in_=D["masks2"][:, :, :]), writes=["masks2"], dma=True)
    P.op("sp", lambda e: e.dma_start(out=k.masks4[:], in_=D["masks4"][:, :, :]), writes=["masks4"], dma=True)
    P.op("dve", lambda e: e.tensor_copy(out=k.ident_bf[:], in_=k.ident[:]), reads=["ident"], writes=["ident_bf"])
    P.op("act", lambda e: e.activation(out=k.cvs[:], in_=k.cvf[:], func=AF.Silu), reads=["cvf"], writes=["cvs"])
    ps, pk = k.psn(hold=True)
    defer = []
    P.op = lambda *a, **kw: defer.append((a, kw))
    for l in range(2):
        for L_ in (256, 1024):
            hyena_filter(k, l, L_)
    del P.op
    per = (len(defer) + 47) // 48

    def replay(n):
        for _ in range(n):
            if defer:
                a, kw = defer.pop(0)
                P.op(*a, **kw)

    for l in range(2):
        for gi in range(24):
            wb, wk = k.w_next()
            w3 = wb[:, 0:2048].rearrange("p (kc c) -> p kc c", c=256)
            for mc in range(2):
                oc = l * 48 + gi * 2 + mc
                for kc in range(8):
                    P.op("pe", lambda e, oc=oc, kc=kc, mc=mc, w3=w3: e.matmul(
                        ps[:, oc * 2:oc * 2 + 2], lhsT=w3[:, kc, mc * 128:(mc + 1) * 128], rhs=k.cvs[:, :, kc],
                        start=(kc == 0), stop=(kc == 7)), reads=wk + ["cvs"], writes=[pk])
            replay(per)
    replay(len(defer))
    k.ps_held.discard(pk[1])
    P.op("dve", lambda e: e.tensor_tensor(out=k.modT[:], in0=ps[:, 0:192].rearrange("p (c g) -> p c g", g=2),
                                          in1=k.small[:, 0:96].unsqueeze(2).to_broadcast([128, 96, 2]), op=ALU.add),
         reads=[pk, "small"], writes=["modT"])
    for l in range(2):
        for g in range(2):
            P.op("dve", lambda e, l=l, g=g: e.tensor_copy(
                out=k.MODS[:, l, g, :, :], in_=k.modT[:, l * 48:(l + 1) * 48, g].rearrange("p (a b) -> p a b", b=8)),
                reads=["modT"], writes=[("MODS", l, g)])
            for (idx, off) in ((1, 96), (4, 112)):
                P.op("dve", lambda e, l=l, g=g, idx=idx, off=off: e.scalar_tensor_tensor(
                    out=k.MODS[:, l, g, idx, :], in0=k.MODS[:, l, g, idx, :], scalar=1.0,
                    in1=k.small[:, off + l * 8:off + l * 8 + 8], op0=ALU.add, op1=ALU.mult),
                    reads=[("MODS", l, g), "small_g", "small_g2"], writes=[("MODS", l, g)])
    k.dbg("mods", k.MODS[:].rearrange("p a b c d -> p (a b c d)"), [128, 192], [("MODS", l, g) for l in range(2) for g in range(2)])


def big(k, j):
    return k.BIG[:, j, :]


def bigkey(j):
    return ("BIG", j)


def load_x(k, g):
    P, D = k.P, k.D
    xin = D["x%d" % g]
    for tt in range(8):
        slot = tt % 2
        st = big(k, slot)
        P.op("sp", lambda e, tt=tt, st=st: e.dma_start(out=st, in_=xin[tt * 128:(tt + 1) * 128, :]), writes=[bigkey(slot)], dma=True)
        if g == 1:
            pslot = 2 + tt % 2
            pt = big(k, pslot)
            P.op("sp", lambda e, tt=tt, pt=pt: e.dma_start(out=pt, in_=D["pos"][tt * 128:(tt + 1) * 128, :]), writes=[bigkey(pslot)], dma=True)
            P.op("dve", lambda e, st=st, pt=pt: e.tensor_tensor(out=st, in0=st, in1=pt, op=ALU.add),
                 reads=[bigkey(slot), bigkey(pslot)], writes=[bigkey(slot)])
        for half in range(2):
            ps, pk = k.psn()
            for j in range(4):
                kc = half * 4 + j
                P.op("pe", lambda e, ps=ps, j=j, kc=kc, st=st: e.transpose(ps[:, j * 128:(j + 1) * 128], st[:, kc * 128:(kc + 1) * 128], k.ident[:]),
                     reads=[bigkey(slot), "ident"], writes=[pk])
            eng = "act" if half == 0 else "dve"
            dst = k.xT[:, half * 4:half * 4 + 4, tt * 128:(tt + 1) * 128]
            src = ps[:, :].rearrange("p (j t) -> p j t", t=128)
            if eng == "act":
                P.op("act", lambda e, dst=dst, src=src: e.activation(out=dst, in_=src, func=AF.Copy), reads=[pk], writes=[("xT", tt // 4)])
            else:
                P.op("dve", lambda e, dst=dst, src=src: e.tensor_copy(out=dst, in_=src), reads=[pk], writes=[("xT", tt // 4)])


def xkeys():
    return [("xT", 0), ("xT", 1)]


def norm_mod(k, l, g, ia, ib):
    P = k.P
    NS = k.NBIG
    sqv = k.BIG[:, NS - 2:NS, :].bitcast(BF16).rearrange("p a (b t) -> p (a b) t", b=4)
    sqk = [bigkey(NS - 2), bigkey(NS - 1)]
    rs = big(k, NS - 3)[:, 0:512]
    rsk = bigkey(NS - 3)
    for tg in range(2):
        ts = slice(tg * 512, (tg + 1) * 512)
        P.op("act", lambda e, ts=ts: e.activation(out=sqv, in_=k.xT[:, :, ts], func=AF.Square), reads=[("xT", tg)], writes=sqk)
        ps, pk = k.psn()
        for kc in range(8):
            P.op("pe", lambda e, kc=kc, ps=ps: e.matmul(ps[:, :], lhsT=k.ones_bf[:, :], rhs=sqv[:, kc, :], start=(kc == 0), stop=(kc == 7)),
                 reads=sqk + ["ones"], writes=[pk])
        P.op("act", lambda e, ps=ps: e.activation(out=rs, in_=ps[:, :], func=AF.Ln, scale=1.0 / 1024, bias=EPS), reads=[pk], writes=[rsk])
        P.op("act", lambda e: e.activation(out=rs, in_=rs, func=AF.Exp, scale=-0.5), reads=[rsk], writes=[rsk])
        for kc in range(8):
            tslot = NS - 5 + (kc % 2)
            tmp = big(k, tslot)[:, 0:512]
            P.op("dve", lambda e, kc=kc, tmp=tmp, ts=ts: e.scalar_tensor_tensor(
                out=tmp, in0=k.xT[:, kc, ts], scalar=k.MODS[:, l, g, ia, kc:kc + 1], in1=rs, op0=ALU.mult, op1=ALU.mult),
                reads=[("xT", tg), rsk, ("MODS", l, g)], writes=[bigkey(tslot)])
            P.op("act", lambda e, kc=kc, tmp=tmp, ts=ts: e.activation(
                out=k.hT[:, kc, ts], in_=tmp, func=AF.Identity, bias=k.MODS[:, l, g, ib, kc:kc + 1]),
                reads=[bigkey(tslot), ("MODS", l, g)], writes=[("hT", tg)])


def hkeys():
    return [("hT", 0), ("hT", 1)]


def proj_fm(k, w3, wk, mcols, consume):
    P = k.P
    for i, (m0, mw) in enumerate(mcols):
        pss = [k.psn() for _ in range(2)]
        for kc in range(8):
            for tg in range(2):
                ps, pk = pss[tg]
                P.op("pe", lambda e, ps=ps, kc=kc, tg=tg, m0=m0, mw=mw: e.matmul(
                    ps[0:mw, :], lhsT=w3[:, kc, m0:m0 + mw], rhs=k.hT[:, kc, tg * 512:(tg + 1) * 512],
                    start=(kc == 0), stop=(kc == 7)), reads=wk + [("hT", tg)], writes=[pk])
        for tg in range(2):
            ps, pk = pss[tg]
            consume(i, tg, ps, pk, mw)


def evac_to_big(k, slot_of):
    P = k.P
    cnt = [0]

    def consume(i, tg, ps, pk, mw):
        slot = slot_of(i)
        dst = k.BIG[0:mw, slot, tg * 512:(tg + 1) * 512]
        if cnt[0] % 2 == 0:
            P.op("act", lambda e: e.activation(out=dst, in_=ps[0:mw, :], func=AF.Copy), reads=[pk], writes=[bigkey(slot)])
        else:
            P.op("dve", lambda e: e.tensor_copy(out=dst, in_=ps[0:mw, :]), reads=[pk], writes=[bigkey(slot)])
        cnt[0] += 1
    return consume


def dwconv_fm(k, src_slot, dst_slot, wtile, wkey, ktaps, nseq, L, eng_first="act"):
    P = k.P
    left = ktaps // 2
    src3 = big(k, src_slot).rearrange("p (s t) -> p s t", t=L)
    dst3 = big(k, dst_slot).rearrange("p (s t) -> p s t", t=L)
    P.op("dve", lambda e: e.tensor_scalar(out=big(k, dst_slot), in0=big(k, src_slot), scalar1=wtile[:, left:left + 1], scalar2=None, op0=ALU.mult),
         reads=[bigkey(src_slot), wkey], writes=[bigkey(dst_slot)])
    for j in range(ktaps):
        if j == left:
            continue
        sh = j - left
        if sh < 0:
            o = dst3[:, :, -sh:L]
            i = src3[:, :, 0:L + sh]
        else:
            o = dst3[:, :, 0:L - sh]
            i = src3[:, :, sh:L]
        P.op("dve", lambda e, o=o, i=i, j=j: e.scalar_tensor_tensor(out=o, in0=i, scalar=wtile[:, j:j + 1], in1=o, op0=ALU.mult, op1=ALU.add),
             reads=[bigkey(src_slot), bigkey(dst_slot), wkey], writes=[bigkey(dst_slot)])


def lru_mixer(k, l, g, nseq, L):
    P, D = k.P, k.D
    cw = P_lru(k)
    P.op("sp", lambda e: e.dma_start(out=cw["conv"][:], in_=D["lru_conv_fm"][:, l, :, :]), writes=["lru_conv"], dma=True)
    P.op("sp", lambda e: e.dma_start(out=cw["b"][:], in_=D["lru_b_fm"][:, l, :, :, :]), writes=["lru_b"], dma=True)
    P.op("sp", lambda e: e.dma_start(out=cw["lam"][:], in_=D["lru_lam_fm"][:, l, :, :]), writes=["lru_lam"], dma=True)
    P.op("sp", lambda e: e.dma_start(out=cw["s0"][:], in_=D["st_lru_fm"][:, l, :, :]), writes=["lru_s0"], dma=True)
    for ri in range(2):
        for d in range(2):
            for q in range(2):
                P.op("sp", lambda e, ri=ri, d=d, q=q: e.dma_start(out=cw["w"][:, ri, d, q, :], in_=D["lru_wbd"][l, ri, d, q, :, :]),
                     writes=[("lru_w", ri, d, q)], dma=True)
    P.op("act", lambda e: e.activation(out=cw["c"][:], in_=cw["lam"][:], func=AF.Exp, scale=-1.0), reads=["lru_lam"], writes=["lru_c"])
    P.op("act", lambda e: e.activation(out=cw["c"][:], in_=cw["c"][:], func=AF.Ln, bias=1.0), reads=["lru_c"], writes=["lru_c"])
    P.op("dve", lambda e: e.tensor_scalar(out=cw["c"][:], in0=cw["c"][:], scalar1=-8.0, scalar2=None, op0=ALU.mult), reads=["lru_c"], writes=["lru_c"])
    for gi in range(2):
        wb, wk = k.w_next()
        w3 = wb[:, 0:2048].rearrange("p (kc c) -> p kc c", c=256)
        proj_fm(k, w3, wk, [(0, 128), (128, 128)], evac_to_big(k, lambda i, gi=gi: gi * 2 + i))
    k.dbg("lru_u%d%d" % (l, g), k.BIG[:, 0:4, :], [128, 4, T], [bigkey(j) for j in range(4)])
    S = 8
    for q in range(2):
        xc = S + 0
        dwconv_fm(k, q, xc, cw["conv"][:, q, :], "lru_conv", 4, nseq, L)
        hs = []
        for d in range(2):
            t1, t2, t3, hd = S + 1, S + 2, S + 3, S + 4 + d
            for ri, dst in ((0, t1), (1, t2)):
                for tg in range(2):
                    ps, pk = k.psn()
                    P.op("pe", lambda e, ps=ps, ri=ri, d=d, q=q, tg=tg: e.matmul(
                        ps[:, :], lhsT=cw["w"][:, ri, d, q, :], rhs=big(k, xc)[:, tg * 512:(tg + 1) * 512], start=True, stop=True),
                        reads=[bigkey(xc), ("lru_w", ri, d, q)], writes=[pk])
                    P.op("act", lambda e, ps=ps, ri=ri, d=d, q=q, tg=tg, dst=dst: e.activation(
                        out=big(k, dst)[:, tg * 512:(tg + 1) * 512], in_=ps[:, :], func=AF.Sigmoid, bias=cw["b"][:, ri, d, q:q + 1]),
                        reads=[pk, "lru_b"], writes=[bigkey(dst)])
            P.op("act", lambda e, d=d, q=q: e.activation(out=big(k, t1), in_=big(k, t1), func=AF.Exp, scale=cw["c"][:, d, q:q + 1]),
                 reads=[bigkey(t1), "lru_c"], writes=[bigkey(t1)])
            P.op("act", lambda e: e.activation(out=big(k, t3), in_=big(k, t1), func=AF.Square), reads=[bigkey(t1)], writes=[bigkey(t3)])
            P.op("dve", lambda e: e.tensor_scalar(out=big(k, t3), in0=big(k, t3), scalar1=1.0, scalar2=None, op0=ALU.min), reads=[bigkey(t3)], writes=[bigkey(t3)])
            P.op("act", lambda e: e.activation(out=big(k, t3), in_=big(k, t3), func=AF.Sqrt, scale=-1.0, bias=1.0), reads=[bigkey(t3)], writes=[bigkey(t3)])
            P.op("dve", lambda e: e.tensor_tensor(out=big(k, t3), in0=big(k, t3), in1=big(k, t2), op=ALU.mult),
                 reads=[bigkey(t3), bigkey(t2)], writes=[bigkey(t3)])
            P.op("dve", lambda e: e.tensor_tensor(out=big(k, t3), in0=big(k, t3), in1=big(k, xc), op=ALU.mult),
                 reads=[bigkey(t3), bigkey(xc)], writes=[bigkey(t3)])
            for s in range(nseq):
                sl = slice(s * L, (s + 1) * L)
                a_ap = big(k, t1)[:, sl]
                b_ap = big(k, t3)[:, sl]
                o_ap = big(k, hd)[:, sl]
                if d == 1:
                    a_ap, b_ap, o_ap = a_ap[:, ::-1], b_ap[:, ::-1], o_ap[:, ::-1]
                init = 0.0 if g == 0 else cw["s0"][:, d, q:q + 1]
                P.op("dve", lambda e, a_ap=a_ap, b_ap=b_ap, o_ap=o_ap, init=init: e.tensor_tensor_scan(
                    out=o_ap, data0=a_ap, data1=b_ap, initial=init, op0=ALU.mult, op1=ALU.add),
                    reads=[bigkey(t1), bigkey(t3), "lru_s0"], writes=[bigkey(hd)])
                if g == 0:
                    col = ((s * 2 + l) * 2 + d) * 2 + q
                    last = (s + 1) * L - 1 if d == 0 else s * L
                    P.op("dve", lambda e, col=col, last=last, hd=hd: e.tensor_copy(out=k.nsl[:, col:col + 1], in_=big(k, hd)[:, last:last + 1]),
                         reads=[bigkey(hd)], writes=["nsl"])
        h0, h1 = S + 4, S + 5
        P.op("dve", lambda e: e.tensor_tensor(out=big(k, h0), in0=big(k, h0), in1=big(k, h1), op=ALU.add),
             reads=[bigkey(h0), bigkey(h1)], writes=[bigkey(h0)])
        P.op("act", lambda e, q=q: e.activation(out=big(k, h1), in_=big(k, 2 + q), func=AF.Gelu_apprx_tanh), reads=[bigkey(2 + q)], writes=[bigkey(h1)])
        P.op("dve", lambda e, q=q: e.tensor_tensor(out=k.mixT[:, 4 + q, :], in0=big(k, h0), in1=big(k, h1), op=ALU.mult),
             reads=[bigkey(h0), bigkey(h1)], writes=[("mixT", 4 + q)])


def P_lru(k):
    if hasattr(k, "_lru"):
        return k._lru
    P = k.P
    k._lru = {
        "conv": P.sb("lru_conv", [128, 2, 4]),
        "b": P.sb("lru_b", [128, 2, 2, 2]),
        "lam": P.sb("lru_lam", [128, 2, 2]),
        "c": P.sb("lru_c", [128, 2, 2]),
        "s0": P.sb("lru_s0", [128, 2, 2]),
        "w": P.sb("lru_w", [128, 2, 2, 2, 128]),
    }
    k.nsl = P.sb("nsl", [128, 32])
    return k._lru


def w_out_res(k, l, g):
    P = k.P
    for gi in range(4):
        wb, wk = k.w_next()
        w3 = wb[:, 0:2048].rearrange("p (kc c) -> p kc c", c=256)
        for mcl in range(2):
            mc = gi * 2 + mcl
            pss = [k.psn() for _ in range(2)]
            for kc in range(8):
                for tg in range(2):
                    ps, pk = pss[tg]
                    P.op("pe", lambda e, ps=ps, kc=kc, tg=tg, mcl=mcl, w3=w3: e.matmul(
                        ps[:, :], lhsT=w3[:, kc, mcl * 128:(mcl + 1) * 128], rhs=k.mixT[:, kc, tg * 512:(tg + 1) * 512],
                        start=(kc == 0), stop=(kc == 7)), reads=wk + [("mixT", kc)], writes=[pk])
            for tg in range(2):
                ps, pk = pss[tg]
                xs = k.xT[:, mc, tg * 512:(tg + 1) * 512]
                P.op("dve", lambda e, ps=ps, xs=xs, mc=mc: e.scalar_tensor_tensor(
                    out=xs, in0=ps[:, :], scalar=k.MODS[:, l, g, 2, mc:mc + 1], in1=xs, op0=ALU.mult, op1=ALU.add),
                    reads=[pk, ("xT", tg), ("MODS", l, g)], writes=[("xT", tg)])


def ffn(k, l, g):
    P = k.P
    actT = k.BIG[:, 0:11, :].bitcast(BF16).rearrange("p a (b t) -> p (a b) t", b=2)
    akeys = [bigkey(j) for j in range(11)]
    sil = [big(k, 11)[:, 0:512], big(k, 12)[:, 0:512]]
    for gi in range(22):
        wb, wk = k.w_next()
        wg = wb[:, 0:1024].rearrange("p (kc c) -> p kc c", c=128)
        wu = wb[:, 1024:2048].rearrange("p (kc c) -> p kc c", c=128)
        for hcl in range(1):
            hc = gi
            for tg in range(2):
                (pg, pgk), (pu, puk) = k.psn(), k.psn()
                for kc in range(8):
                    P.op("pe", lambda e, pg=pg, kc=kc, tg=tg, hcl=hcl, wg=wg: e.matmul(
                        pg[:, :], lhsT=wg[:, kc, hcl * 128:(hcl + 1) * 128], rhs=k.hT[:, kc, tg * 512:(tg + 1) * 512],
                        start=(kc == 0), stop=(kc == 7)), reads=wk + [("hT", tg)], writes=[pgk])
                for kc in range(8):
                    P.op("pe", lambda e, pu=pu, kc=kc, tg=tg, hcl=hcl, wu=wu: e.matmul(
                        pu[:, :], lhsT=wu[:, kc, hcl * 128:(hcl + 1) * 128], rhs=k.hT[:, kc, tg * 512:(tg + 1) * 512],
                        start=(kc == 0), stop=(kc == 7)), reads=wk + [("hT", tg)], writes=[puk])
                sj = (hc * 2 + tg) % 2
                P.op("act", lambda e, pg=pg, sj=sj: e.activation(out=sil[sj], in_=pg[:, :], func=AF.Silu), reads=[pgk], writes=[bigkey(11 + sj)])
                P.op("dve", lambda e, pu=pu, sj=sj, hc=hc, tg=tg: e.tensor_tensor(
                    out=actT[:, hc, tg * 512:(tg + 1) * 512], in0=sil[sj], in1=pu[:, :], op=ALU.mult),
                    reads=[puk, bigkey(11 + sj)], writes=[("actT", tg)] + akeys)
    for mc in range(8):
        pss = [k.psn() for _ in range(2)]
        for kh in range(2):
            wb, wk = k.w_next()
            w3 = wb[:, 0:11 * 128].rearrange("p (kc c) -> p kc c", c=128)
            for hcl in range(11):
                hc = kh * 11 + hcl
                for tg in range(2):
                    ps, pk = pss[tg]
                    P.op("pe", lambda e, ps=ps, hc=hc, hcl=hcl, tg=tg, w3=w3: e.matmul(
                        ps[:, :], lhsT=w3[:, hcl, :], rhs=actT[:, hc, tg * 512:(tg + 1) * 512],
                        start=(hc == 0), stop=(hc == NHC - 1)), reads=wk + [("actT", tg)] + akeys, writes=[pk])
        for tg in range(2):
            ps, pk = pss[tg]
            xs = k.xT[:, mc, tg * 512:(tg + 1) * 512]
            P.op("dve", lambda e, ps=ps, xs=xs, mc=mc: e.scalar_tensor_tensor(
                out=xs, in0=ps[:, :], scalar=k.MODS[:, l, g, 5, mc:mc + 1], in1=xs, op0=ALU.mult, op1=ALU.add),
                reads=[pk, ("xT", tg), ("MODS", l, g)], writes=[("xT", tg)])


def final_out(k, g):
    P, D = k.P, k.D
    yout = D["y%d" % g]
    for tt in range(8):
        slot = tt % 2
        st = big(k, slot)
        for half in range(2):
            ps, pk = k.psn()
            for j in range(4):
                kc = half * 4 + j
                P.op("pe", lambda e, ps=ps, j=j, kc=kc, tt=tt: e.transpose(ps[:, j * 128:(j + 1) * 128], k.xT[:, kc, tt * 128:(tt + 1) * 128], k.ident[:]),
                     reads=[("xT", tt // 4), "ident"], writes=[pk])
            dst = st[:, half * 512:(half + 1) * 512]
            if half == 0:
                P.op("act", lambda e, dst=dst, ps=ps: e.activation(out=dst, in_=ps[:, :], func=AF.Copy), reads=[pk], writes=[bigkey(slot)])
            else:
                P.op("dve", lambda e, dst=dst, ps=ps: e.tensor_copy(out=dst, in_=ps[:, :]), reads=[pk], writes=[bigkey(slot)])
        sq = big(k, 2 + slot)
        ss = k.fin_ss[:, tt:tt + 1]
        P.op("act", lambda e, st=st, sq=sq, ss=ss: e.activation(out=sq, in_=st, func=AF.Square, accum_out=ss),
             reads=[bigkey(slot)], writes=[bigkey(2 + slot), ("fss", tt)])
        P.op("act", lambda e, ss=ss: e.activation(out=ss, in_=ss, func=AF.Ln, scale=1.0 / 1024, bias=EPS), reads=[("fss", tt)], writes=[("fss", tt)])
        P.op("act", lambda e, ss=ss: e.activation(out=ss, in_=ss, func=AF.Exp, scale=-0.5), reads=[("fss", tt)], writes=[("fss", tt)])
        P.op("dve", lambda e, st=st, sq=sq, ss=ss: e.scalar_tensor_tensor(out=sq, in0=st, scalar=ss, in1=k.gfin[:], op0=ALU.mult, op1=ALU.mult),
             reads=[bigkey(slot), ("fss", tt), "gfin", bigkey(2 + slot)], writes=[bigkey(2 + slot)])
        P.op("sp", lambda e, sq=sq, tt=tt: e.dma_start(out=yout[tt * 128:(tt + 1) * 128, :], in_=sq), reads=[bigkey(2 + slot)], dma=True)


def group(k, g, stop_after=None):
    P, D = k.P, k.D
    nseq, L = (4, 256) if g == 0 else (1, 1024)
    if not hasattr(k, "fin_ss"):
        k.fin_ss = P.sb("fin_ss", [128, 8])
    load_x(k, g)
    k.dbg("xT%d" % g, k.xT[:], [128, 8, T], xkeys())
    for l in range(2):
        norm_mod(k, l, g, 1, 0)
        k.dbg("hT%d%d" % (l, g), k.hT[:], [128, 8, T], hkeys())
        hyena_mixer(k, l, g, nseq, L)
        lru_mixer(k, l, g, nseq, L)
        ssd_mixer(k, l, g, nseq, L)
        gdn_mixer(k, l, g, nseq, L)
        k.dbg("mixT%d%d" % (l, g), k.mixT[:], [128, 8, T], [("mixT", j) for j in range(8)])
        w_out_res(k, l, g)
        k.dbg("xmid%d%d" % (l, g), k.xT[:], [128, 8, T], xkeys())
        norm_mod(k, l, g, 4, 3)
        ffn(k, l, g)
        k.dbg("xout%d%d" % (l, g), k.xT[:], [128, 8, T], xkeys())
    final_out(k, g)
    if g == 0:
        ps, pk = k.psn()
        P.op("pe", lambda e: e.transpose(ps[0:32, 0:128], k.nsl[:, 0:32], k.ident[:]), reads=["nsl", "ident"], writes=[pk])
        st = big(k, 4)[0:32, 0:128]
        P.op("act", lambda e: e.activation(out=st, in_=ps[0:32, 0:128], func=AF.Copy), reads=[pk], writes=[bigkey(4)])
        P.op("sp", lambda e: e.dma_start(out=D["ns_lru"][:, :], in_=st), reads=[bigkey(4)], dma=True)
def bfv(k, slot):
    return big(k, slot).bitcast(BF16)


def P_ssd(k):
    if hasattr(k, "_ssd"):
        return k._ssd
    P = k.P
    k._ssd = {
        "conv": P.sb("ssd_conv", [128, 4, 4]),
        "dtb": P.sb("ssd_dtb", [128, 8]),
        "nA": P.sb("ssd_nA", [128, 8]),
        "dfm": P.sb("ssd_dfm", [128, 2]),
        "nfm": P.sb("ssd_nfm", [128, 2]),
        "dt": P.sb("ssd_dt", [128, 8, 8]),
        "a": P.sb("ssd_a", [128, 8, 8]),
        "eacs": P.sb("ssd_eacs", [128, 8, 8]),
        "dAe": P.sb("ssd_dAe", [128, 8, 4]),
        "ST": P.sb("ssd_ST", [128, 256]),
        "s0": P.sb("ssd_s0", [64, 8, 64]),
        "fin": P.sb("ssd_fin", [64, 4, 64]),
    }
    return k._ssd


def ssd_mixer(k, l, g, nseq, L):
    P, D = k.P, k.D
    c = P_ssd(k)
    import os
    PH0 = int(os.environ.get("SSD_PHASE", "9"))
    if PH0 < -1:
        for _ in range(4):
            k.w_next()
        return
    M = k.masks
    LE, GE, GT, LT, ONES = (M[:, j, :] for j in range(5))
    P.op("sp", lambda e: e.dma_start(out=c["conv"][:], in_=D["ssd_conv_fm"][:, l, :, :]), writes=["ssd_conv"], dma=True)
    P.op("sp", lambda e: e.dma_start(out=c["dtb"][:], in_=D["ssd_dtb_bc"][:, l, :]), writes=["ssd_dtb"], dma=True)
    P.op("sp", lambda e: e.dma_start(out=c["nA"][:], in_=D["ssd_alog_bc"][:, l, :]), writes=["ssd_nA"], dma=True)
    P.op("sp", lambda e: e.dma_start(out=c["dfm"][:], in_=D["ssd_d_fm"][:, l, :]), writes=["ssd_dfm"], dma=True)
    P.op("sp", lambda e: e.dma_start(out=c["nfm"][:], in_=D["ssd_norm_fm"][:, l, :]), writes=["ssd_nfm"], dma=True)
    P.op("act", lambda e: e.activation(out=c["nA"][:], in_=c["nA"][:], func=AF.Exp), reads=["ssd_nA"], writes=["ssd_nA"])
    for gi in range(3):
        wb, wk = k.w_next()
        w3 = wb[:, 0:2048].rearrange("p (kc c) -> p kc c", c=256)
        proj_fm(k, w3, wk, [(0, 128), (128, 128)], evac_to_big(k, lambda i, gi=gi: gi * 2 + i))
    wb, wk = k.w_next()
    w3 = wb[:, 0:64].rearrange("p (kc c) -> p kc c", c=8)
    ps, pk = k.psn()
    for tt in range(8):
        for kc in range(8):
            P.op("pe", lambda e, tt=tt, kc=kc, ps=ps, w3=w3: e.matmul(
                ps[:, tt * 8:(tt + 1) * 8], lhsT=k.hT[:, kc, tt * 128:(tt + 1) * 128], rhs=w3[:, kc, :],
                start=(kc == 0), stop=(kc == 7)), reads=wk + [("hT", tt // 4)], writes=[pk])
    P.op("dve", lambda e, ps=ps: e.tensor_tensor(out=c["dt"][:], in0=ps[:, 0:64].rearrange("p (t u) -> p t u", u=8),
                                                 in1=c["dtb"][:].unsqueeze(1).to_broadcast([128, 8, 8]), op=ALU.add),
         reads=[pk, "ssd_dtb"], writes=["ssd_dt"])
    P.op("act", lambda e: e.activation(out=c["dt"][:], in_=c["dt"][:], func=AF.Exp), reads=["ssd_dt"], writes=["ssd_dt"])
    P.op("act", lambda e: e.activation(out=c["dt"][:], in_=c["dt"][:], func=AF.Ln, bias=1.0), reads=["ssd_dt"], writes=["ssd_dt"])
    P.op("dve", lambda e: e.scalar_tensor_tensor(out=c["a"][:], in0=c["dt"][:], scalar=-1.0,
                                                 in1=c["nA"][:].unsqueeze(1).to_broadcast([128, 8, 8]), op0=ALU.mult, op1=ALU.mult),
         reads=["ssd_dt", "ssd_nA"], writes=["ssd_a"])
    k.dbg("ssd_dt%d%d" % (l, g), c["dt"][:], [128, 8, 8], ["ssd_dt"])
    if PH0 < 0:
        return
    S = 8
    XS = (S + 0, S + 1)
    dwconv_fm(k, 2, XS[0], c["conv"][:, 0, :], "ssd_conv", 4, nseq, L)
    dwconv_fm(k, 3, XS[1], c["conv"][:, 1, :], "ssd_conv", 4, nseq, L)
    SUB = int(os.environ.get("SSD_SUB", "9"))
    if SUB < 2:
        return
    dwconv_fm(k, 4, S + 3, c["conv"][:, 2, :], "ssd_conv", 4, nseq, L)
    dwconv_fm(k, 5, S + 4, c["conv"][:, 3, :], "ssd_conv", 4, nseq, L)
    if SUB < 3:
        return
    for q in range(2):
        P.op("act", lambda e, q=q: e.activation(out=big(k, XS[q]), in_=big(k, XS[q]), func=AF.Silu), reads=[bigkey(XS[q])], writes=[bigkey(XS[q])])
    if SUB < 4:
        return
    BC = bfv(k, S + 2).rearrange("p (a t) -> p a t", a=2)
    Bf = S + 3
    P.op("act", lambda e: e.activation(out=big(k, Bf), in_=big(k, Bf), func=AF.Silu), reads=[bigkey(Bf)], writes=[bigkey(Bf)])
    P.op("dve", lambda e: e.tensor_copy(out=BC[:, 0, :], in_=big(k, Bf)), reads=[bigkey(Bf)], writes=[bigkey(S + 2)])
    P.op("act", lambda e: e.activation(out=BC[:, 1, :], in_=big(k, S + 4), func=AF.Silu), reads=[bigkey(S + 4)], writes=[bigkey(S + 2)])
    sE, sLH, sW, sG = 2, 3, S + 4, S + 5
    Et = big(k, sE).rearrange("p (u l) -> p u l", l=128)
    LH = big(k, sLH).rearrange("p (u l) -> p u l", l=128)
    Wb = bfv(k, sW)[:, 0:1024].rearrange("p (u l) -> p u l", l=128)
    Xdt = bfv(k, sW)[:, 1024:1536]
    Xdec = bfv(k, sW)[:, 1536:2048]
    Gm = big(k, sG)[:, 0:512].rearrange("p (d g l) -> p d g l", d=2, g=2)
    T1 = big(k, sG)[:, 512:1024]
    Btok = bfv(k, S + 6)[:, 0:128]
    INC = k.BIG[:, 6:8, :].rearrange("p a (c x) -> p (a c) x", x=256)
    YD = k.BIG[:, 17:19, :].rearrange("p a (c x) -> p (a c) x", x=256)
    STin = bfv(k, 16).rearrange("p (c x) -> p c x", x=256)
    kE, kLH, kW, kG, kB, kINC, kYD, kST = bigkey(sE), bigkey(sLH), bigkey(sW), bigkey(sG), bigkey(S + 6), [bigkey(6), bigkey(7)], [bigkey(17), bigkey(18)], bigkey(16)
    kBC = bigkey(S + 2)
    import os
    PH = int(os.environ.get("SSD_PHASE", "9"))
    if PH < 1:
        return
    for ch in range(8):
        ts = slice(ch * 128, (ch + 1) * 128)
        psT, pkT = k.psn()
        for q in range(2):
            P.op("pe", lambda e, q=q, psT=psT, ts=ts: e.transpose(psT[:, q * 128:(q + 1) * 128], big(k, XS[q])[:, ts], k.ident[:]),
                 reads=[bigkey(XS[q]), "ident"], writes=[pkT])
        P.op("pe", lambda e, psT=psT, ts=ts: e.transpose(psT[:, 256:384], big(k, Bf)[:, ts], k.ident[:]),
             reads=[bigkey(Bf), "ident"], writes=[pkT])
        SV = os.environ.get("SSD_V", "xb")
        for d in range(2):
            if "x" not in SV:
                continue
            P.op("dve", lambda e, psT=psT, ch=ch, d=d: e.tensor_tensor(
                out=Xdt[:, d * 256:(d + 1) * 256].rearrange("p (h x) -> p h x", h=4),
                in0=psT[:, 0:256].rearrange("p (h x) -> p h x", h=4),
                in1=c["dt"][:, ch, d * 4:(d + 1) * 4].unsqueeze(2).to_broadcast([128, 4, 64]), op=ALU.mult),
                reads=[pkT, "ssd_dt"], writes=[kW])
        if "b" in SV:
            P.op("act", lambda e, psT=psT: e.activation(out=Btok, in_=psT[:, 256:384], func=AF.Copy), reads=[pkT], writes=[kB])
        SA = int(os.environ.get("SSD_SA", "9"))
        if SA < 2:
            continue
        psGs = [k.psn() for _ in range(2)]
        for gg in range(2):
            psG, pkG = psGs[gg]
            P.op("pe", lambda e, gg=gg, psG=psG, ts=ts: e.matmul(
                psG[:, 0:128], lhsT=BC[gg * 64:(gg + 1) * 64, 0, ts], rhs=BC[gg * 64:(gg + 1) * 64, 1, ts], start=True, stop=True),
                reads=[kBC], writes=[pkG])
        for gg in range(2):
            psG, pkG = psGs[gg]
            for d in range(2):
                mk = LE if d == 0 else GE
                P.op("dve", lambda e, d=d, gg=gg, mk=mk, psG=psG: e.tensor_tensor(
                    out=Gm[:, d, gg, :], in0=psG[:, 0:128], in1=mk, op=ALU.mult), reads=[pkG, "masks"], writes=[kG])
        if SA < 3:
            continue
        for d in range(2):
            mu = GT if d == 0 else LT
            P.op("dve", lambda e, d=d, mu=mu, ch=ch: e.tensor_tensor(
                out=LH[:, d * 4:(d + 1) * 4, :], in0=mu.unsqueeze(1).to_broadcast([128, 4, 128]),
                in1=c["a"][:, ch, d * 4:(d + 1) * 4].unsqueeze(2).to_broadcast([128, 4, 128]), op=ALU.mult),
                reads=["masks", "ssd_a"], writes=[kLH])
        for d in range(2):
            psD, pkD = k.psn()
            ml = LE if d == 0 else GE
            for h in range(4):
                P.op("pe", lambda e, d=d, h=h, psD=psD, ml=ml: e.matmul(
                    psD[:, h * 128:(h + 1) * 128], lhsT=LH[:, d * 4 + h, :], rhs=ml, start=True, stop=True),
                    reads=[kLH, "masks"], writes=[pkD])
            P.op("act", lambda e, d=d, psD=psD: e.activation(out=Et[:, d * 4:(d + 1) * 4, :], in_=psD[:, :].rearrange("p (h l) -> p h l", h=4), func=AF.Exp),
                 reads=[pkD], writes=[kE])
            for gg in range(2):
                P.op("dve", lambda e, d=d, gg=gg: e.tensor_tensor(
                    out=Wb[:, d * 4 + gg * 2:d * 4 + gg * 2 + 2, :],
                    in0=Et[:, d * 4 + gg * 2:d * 4 + gg * 2 + 2, :],
                    in1=Gm[:, d, gg, :].unsqueeze(1).to_broadcast([128, 2, 128]), op=ALU.mult),
                    reads=[kE, kG], writes=[kW])
            col = 127 if d == 0 else 0
            P.op("dve", lambda e, d=d, col=col: e.tensor_tensor(
                out=Xdec[:, d * 256:(d + 1) * 256].rearrange("p (h x) -> p h x", h=4),
                in0=Xdt[:, d * 256:(d + 1) * 256].rearrange("p (h x) -> p h x", h=4),
                in1=Et[:, d * 4:(d + 1) * 4, col:col + 1].to_broadcast([128, 4, 64]), op=ALU.mult),
                reads=[kE, kW], writes=[kW])
        if SA < 4:
            continue
        psY, pkY = k.psn()
        for h in range(4):
            for d in range(2):
                u = d * 4 + h
                P.op("pe", lambda e, h=h, d=d, u=u, psY=psY: e.matmul(
                    psY[:, h * 64:(h + 1) * 64], lhsT=Wb[:, u, :], rhs=Xdt[:, u * 64:(u + 1) * 64], start=(d == 0), stop=(d == 1)),
                    reads=[kW], writes=[pkY])
        P.op("act", lambda e, psY=psY, ch=ch: e.activation(out=YD[:, ch, :], in_=psY[:, 0:256], func=AF.Copy), reads=[pkY], writes=kYD)
        if SA < 5:
            continue
        psI, pkI = k.psn()
        for gg in range(2):
            for d in range(2):
                c0 = (d * 4 + gg * 2) * 64
                P.op("pe", lambda e, gg=gg, d=d, c0=c0, psI=psI: e.matmul(
                    psI[gg * 64:(gg + 1) * 64, d * 128:(d + 1) * 128], lhsT=Btok[:, gg * 64:(gg + 1) * 64], rhs=Xdec[:, c0:c0 + 128], start=True, stop=True),
                    reads=[kB, kW], writes=[pkI])
        P.op("act", lambda e, psI=psI, ch=ch: e.activation(out=INC[:, ch, :], in_=psI[:, 0:256], func=AF.Copy), reads=[pkI], writes=kINC)
        if SA < 6:
            continue
        psA, pkA = k.psn()
        P.op("pe", lambda e, psA=psA, ch=ch: e.matmul(psA[:, 0:4], lhsT=LE, rhs=c["a"][:, ch, 0:4], start=True, stop=True), reads=["masks", "ssd_a"], writes=[pkA])
        P.op("pe", lambda e, psA=psA, ch=ch: e.matmul(psA[:, 4:8], lhsT=GE, rhs=c["a"][:, ch, 4:8], start=True, stop=True), reads=["masks", "ssd_a"], writes=[pkA])
        P.op("pe", lambda e, psA=psA, ch=ch: e.matmul(psA[:, 8:16], lhsT=ONES, rhs=c["a"][:, ch, :], start=True, stop=True), reads=["masks", "ssd_a"], writes=[pkA])
        P.op("act", lambda e, psA=psA, ch=ch: e.activation(out=c["eacs"][:, ch, :], in_=psA[:, 0:8], func=AF.Exp), reads=[pkA], writes=["ssd_eacs"])
        for gg in range(2):
            P.op("act", lambda e, psA=psA, ch=ch, gg=gg: e.activation(
                out=c["dAe"][gg * 64:(gg + 1) * 64, ch, :].rearrange("p (d h) -> p d h", d=2),
                in_=psA[gg * 64:(gg + 1) * 64, 8:16].rearrange("p (d g h) -> p d g h", d=2, g=2)[:, :, gg, :], func=AF.Exp),
                reads=[pkA], writes=["ssd_dAe"])
    k.dbg("ssd_yd%d%d" % (l, g), YD, [128, 8, 256], kYD)
    if PH < 2:
        return
    nchs = L // 128
    for s in range(nseq):
        if g == 0:
            P.op("dve", lambda e: e.memset(c["ST"][:], 0.0), writes=["ssd_ST"])
        else:
            P.op("sp", lambda e: e.dma_start(out=c["s0"][:], in_=D["st_ssd"][l].rearrange("d h p n -> p (d h) n")), writes=["ssd_s0"], dma=True)
            ps0, pk0 = k.psn()
            for d in range(2):
                for gg in range(2):
                    for hh in range(2):
                        u = d * 4 + gg * 2 + hh
                        P.op("pe", lambda e, d=d, gg=gg, hh=hh, u=u, ps0=ps0: e.transpose(
                            ps0[0:64, (d * 4 + gg * 2 + hh) * 64:(d * 4 + gg * 2 + hh + 1) * 64], c["s0"][:, u, :], k.ident[0:64, 0:64]),
                            reads=["ssd_s0", "ident"], writes=[pk0])
            for gg in range(2):
                src = ps0[0:64, :].rearrange("n (d g x) -> n d g x", d=2, g=2)[:, :, gg, :]
                tmp = big(k, sG)[0:64, 0:256].rearrange("n (d x) -> n d x", d=2)
                P.op("act", lambda e, src=src, tmp=tmp: e.activation(out=tmp, in_=src, func=AF.Copy), reads=[pk0], writes=[kG])
                if gg == 0:
                    P.op("dve", lambda e, tmp=tmp: e.tensor_copy(out=c["ST"][0:64, :].rearrange("n (d x) -> n d x", d=2), in_=tmp), reads=[kG], writes=["ssd_ST"])
                else:
                    P.op("sp", lambda e, tmp=tmp: e.dma_start(out=c["ST"][64:128, :].rearrange("n (d x) -> n d x", d=2), in_=tmp), reads=[kG], writes=["ssd_ST"], dma=True)
        for d in range(2):
            order = range(nchs) if d == 0 else range(nchs - 1, -1, -1)
            cs = slice(d * 128, (d + 1) * 128)
            for j in order:
                ch = s * nchs + j
                P.op("act", lambda e, ch=ch, cs=cs: e.activation(out=STin[:, ch, cs], in_=c["ST"][:, cs], func=AF.Copy), reads=["ssd_ST"], writes=[kST])
                P.op("dve", lambda e, ch=ch, cs=cs, d=d: e.tensor_tensor(
                    out=c["ST"][:, cs].rearrange("p (h x) -> p h x", h=2), in0=c["ST"][:, cs].rearrange("p (h x) -> p h x", h=2),
                    in1=c["dAe"][:, ch, d * 2:(d + 1) * 2].unsqueeze(2).to_broadcast([128, 2, 64]), op=ALU.mult),
                    reads=["ssd_ST", "ssd_dAe"], writes=["ssd_ST"])
                P.op("dve", lambda e, ch=ch, cs=cs: e.tensor_tensor(out=c["ST"][:, cs], in0=c["ST"][:, cs], in1=INC[:, ch, cs], op=ALU.add),
                     reads=["ssd_ST"] + kINC, writes=["ssd_ST"])
            if g == 0:
                for gg in range(2):
                    psF, pkF = k.psn()
                    for hh in range(2):
                        P.op("pe", lambda e, gg=gg, hh=hh, d=d, psF=psF: e.transpose(
                            psF[0:64, hh * 64:(hh + 1) * 64], c["ST"][gg * 64:(gg + 1) * 64, d * 128 + hh * 64:d * 128 + (hh + 1) * 64],
                            k.ident[gg * 64:(gg + 1) * 64, gg * 64:(gg + 1) * 64]),
                            reads=["ssd_ST", "ident"], writes=[pkF])
                    P.op("act", lambda e, psF=psF, gg=gg: e.activation(out=c["fin"][:, gg * 2:gg * 2 + 2, :].rearrange("p h n -> p (h n)"), in_=psF[0:64, 0:128], func=AF.Copy),
                         reads=[pkF], writes=["ssd_fin"])
                P.op("sp", lambda e, s=s, d=d: e.dma_start(out=D["ns_ssd"][s, l, d].rearrange("h p n -> p h n"), in_=c["fin"][:]), reads=["ssd_fin"], dma=True)
    if PH < 3:
        return
    YT = (4, 5)
    for ch in range(8):
        ts = slice(ch * 128, (ch + 1) * 128)
        psOs = [k.psn() for _ in range(2)]
        for gg in range(2):
            psO, pkO = psOs[gg]
            for d in range(2):
                P.op("pe", lambda e, d=d, gg=gg, psO=psO, ts=ts, ch=ch: e.matmul(
                    psO[:, d * 128:(d + 1) * 128], lhsT=BC[gg * 64:(gg + 1) * 64, 1, ts],
                    rhs=STin[gg * 64:(gg + 1) * 64, ch, d * 128:(d + 1) * 128], start=True, stop=True),
                    reads=[kBC, kST], writes=[pkO])
        for gg in range(2):
            psO, pkO = psOs[gg]
            for d in range(2):
                u0 = d * 4 + gg * 2
                P.op("dve", lambda e, psO=psO, ch=ch, d=d, u0=u0: e.tensor_tensor(
                    out=T1[:, u0 * 64:(u0 + 2) * 64].rearrange("p (h x) -> p h x", h=2),
                    in0=psO[:, d * 128:(d + 1) * 128].rearrange("p (h x) -> p h x", h=2),
                    in1=c["eacs"][:, ch, u0:u0 + 2].unsqueeze(2).to_broadcast([128, 2, 64]), op=ALU.mult),
                    reads=[pkO, "ssd_eacs"], writes=[kG])
        P.op("dve", lambda e: e.tensor_tensor(out=T1[:, 0:256], in0=T1[:, 0:256], in1=T1[:, 256:512], op=ALU.add), reads=[kG], writes=[kG])
        P.op("dve", lambda e, ch=ch: e.tensor_tensor(out=T1[:, 0:256], in0=T1[:, 0:256], in1=YD[:, ch, :], op=ALU.add), reads=[kG] + kYD, writes=[kG])
        psT, pkT = k.psn()
        for q in range(2):
            P.op("pe", lambda e, q=q, psT=psT: e.transpose(psT[:, q * 128:(q + 1) * 128], T1[:, q * 128:(q + 1) * 128], k.ident[:]), reads=[kG, "ident"], writes=[pkT])
        P.op("act", lambda e, psT=psT, ts=ts: e.activation(out=k.BIG[:, 4:6, ts], in_=psT[:, 0:256].rearrange("p (q t) -> p q t", q=2), func=AF.Copy),
             reads=[pkT], writes=[bigkey(4), bigkey(5)])
    if PH < 4:
        return
    for q in range(2):
        P.op("dve", lambda e, q=q: e.scalar_tensor_tensor(out=big(k, YT[q]), in0=big(k, XS[q]), scalar=c["dfm"][:, q:q + 1], in1=big(k, YT[q]), op0=ALU.mult, op1=ALU.add),
             reads=[bigkey(XS[q]), bigkey(YT[q]), "ssd_dfm"], writes=[bigkey(YT[q])])
        P.op("act", lambda e, q=q: e.activation(out=big(k, q), in_=big(k, q), func=AF.Silu), reads=[bigkey(q)], writes=[bigkey(q)])
        P.op("dve", lambda e, q=q: e.tensor_tensor(out=big(k, YT[q]), in0=big(k, YT[q]), in1=big(k, q), op=ALU.mult), reads=[bigkey(YT[q]), bigkey(q)], writes=[bigkey(YT[q])])
    k.dbg("ssd_y%d%d" % (l, g), k.BIG[:, 4:6, :], [128, 2, T], [bigkey(4), bigkey(5)])
    sq = bfv(k, sE).rearrange("p (q t) -> p q t", q=2)
    P.op("act", lambda e: e.activation(out=sq, in_=k.BIG[:, 4:6, :], func=AF.Square), reads=[bigkey(4), bigkey(5)], writes=[kE])
    rs = big(k, sLH)
    for tg in range(2):
        ps, pk = k.psn()
        for q in range(2):
            P.op("pe", lambda e, q=q, tg=tg, ps=ps: e.matmul(ps[:, :], lhsT=k.ones_bf[:, :], rhs=sq[:, q, tg * 512:(tg + 1) * 512], start=(q == 0), stop=(q == 1)),
                 reads=[kE, "ones"], writes=[pk])
        P.op("act", lambda e, tg=tg, ps=ps: e.activation(out=rs[:, tg * 512:(tg + 1) * 512], in_=ps[:, :], func=AF.Ln, scale=1.0 / 256, bias=EPS), reads=[pk], writes=[kLH])
    P.op("act", lambda e: e.activation(out=rs, in_=rs, func=AF.Exp, scale=-0.5), reads=[kLH], writes=[kLH])
    for q in range(2):
        P.op("dve", lambda e, q=q: e.scalar_tensor_tensor(out=k.mixT[:, 2 + q, :], in0=big(k, YT[q]), scalar=c["nfm"][:, q:q + 1], in1=rs, op0=ALU.mult, op1=ALU.mult),
             reads=[bigkey(YT[q]), kLH, "ssd_nfm"], writes=[("mixT", 2 + q)])
def P_gdn(k):
    if hasattr(k, "_gdn"):
        return k._gdn
    P = k.P
    k._gdn = {
        "conv": P.sb("gdn_conv", [128, 6, 4]),
        "dtb": P.sb("gdn_dtb", [128, 8]),
        "nA": P.sb("gdn_nA", [128, 8]),
        "gn": P.sb("gdn_gn", [128, 64]),
        "beta": P.sb("gdn_beta", [128, 8, 8]),
        "gt": P.sb("gdn_gt", [128, 8, 8]),
        "egc": P.sb("gdn_egc", [128, 4]),
        "gam": P.sb("gdn_gam", [128, 4]),
        "bg": P.sb("gdn_bg", [128, 4]),
        "S": P.sb("gdn_S", [128, 2, 64]),
        "ss": P.sb("gdn_ss", [128, 4]),
    }
    return k._gdn


def gdn_mixer(k, l, g, nseq, L):
    P, D = k.P, k.D
    c = P_gdn(k)
    M = k.masks
    LE, GE, GT, LT, ONES, OBD = (M[:, j, :] for j in range(6))
    M2 = k.masks2
    P.op("sp", lambda e: e.dma_start(out=c["conv"][:], in_=D["gdn_conv_fm"][:, l, :, :]), writes=["gdn_conv"], dma=True)
    P.op("sp", lambda e: e.dma_start(out=c["dtb"][:], in_=D["gdn_dtb_bc"][:, l, :]), writes=["gdn_dtb"], dma=True)
    P.op("sp", lambda e: e.dma_start(out=c["nA"][:], in_=D["gdn_alog_bc"][:, l, :]), writes=["gdn_nA"], dma=True)
    P.op("sp", lambda e: e.dma_start(out=c["gn"][:], in_=D["gdn_norm_bc"][:, l, :]), writes=["gdn_gn"], dma=True)
    P.op("act", lambda e: e.activation(out=c["nA"][:], in_=c["nA"][:], func=AF.Exp), reads=["gdn_nA"], writes=["gdn_nA"])
    for gi in range(4):
        wb, wk = k.w_next()
        w3 = wb[:, 0:2048].rearrange("p (kc c) -> p kc c", c=256)
        proj_fm(k, w3, wk, [(0, 128), (128, 128)], evac_to_big(k, lambda i, gi=gi: gi * 2 + i))
    wb, wk = k.w_next()
    w3 = wb[:, 0:128].rearrange("p (kc c) -> p kc c", c=16)
    ps, pk = k.psn()
    for tt in range(8):
        for kc in range(8):
            P.op("pe", lambda e, tt=tt, kc=kc, ps=ps, w3=w3: e.matmul(
                ps[:, tt * 16:(tt + 1) * 16], lhsT=k.hT[:, kc, tt * 128:(tt + 1) * 128], rhs=w3[:, kc, :],
                start=(kc == 0), stop=(kc == 7)), reads=wk + [("hT", tt // 4)], writes=[pk])
    ba = ps[:, 0:128].rearrange("p (t u) -> p t u", u=16)
    P.op("act", lambda e: e.activation(out=c["beta"][:], in_=ba[:, :, 0:8], func=AF.Sigmoid), reads=[pk], writes=["gdn_beta"])
    P.op("dve", lambda e: e.tensor_tensor(out=c["gt"][:], in0=ba[:, :, 8:16], in1=c["dtb"][:].unsqueeze(1).to_broadcast([128, 8, 8]), op=ALU.add),
         reads=[pk, "gdn_dtb"], writes=["gdn_gt"])
    P.op("act", lambda e: e.activation(out=c["gt"][:], in_=c["gt"][:], func=AF.Exp), reads=["gdn_gt"], writes=["gdn_gt"])
    P.op("act", lambda e: e.activation(out=c["gt"][:], in_=c["gt"][:], func=AF.Ln, bias=1.0), reads=["gdn_gt"], writes=["gdn_gt"])
    P.op("dve", lambda e: e.scalar_tensor_tensor(out=c["gt"][:], in0=c["gt"][:], scalar=-1.0,
                                                 in1=c["nA"][:].unsqueeze(1).to_broadcast([128, 8, 8]), op0=ALU.mult, op1=ALU.mult),
         reads=["gdn_gt", "gdn_nA"], writes=["gdn_gt"])
    S = 8
    for j in range(6):
        dwconv_fm(k, j, S + j, c["conv"][:, j, :], "gdn_conv", 4, nseq, L)
        P.op("act", lambda e, j=j: e.activation(out=big(k, S + j), in_=big(k, S + j), func=AF.Silu), reads=[bigkey(S + j)], writes=[bigkey(S + j)])
    for j in range(4):
        sq = big(k, 0)
        P.op("act", lambda e, j=j, sq=sq: e.activation(out=sq, in_=big(k, S + j), func=AF.Square), reads=[bigkey(S + j)], writes=[bigkey(0)])
        for tg in range(2):
            ps, pk = k.psn()
            P.op("pe", lambda e, ps=ps, tg=tg, sq=sq: e.matmul(ps[:, :], lhsT=OBD, rhs=sq[:, tg * 512:(tg + 1) * 512], start=True, stop=True),
                 reads=[bigkey(0), "masks"], writes=[pk])
            rs = big(k, 1)[:, tg * 512:(tg + 1) * 512]
            P.op("act", lambda e, ps=ps, rs=rs: e.activation(out=rs, in_=ps[:, :], func=AF.Ln, bias=EPS), reads=[pk], writes=[bigkey(1)])
        bias = math.log(0.125) if j < 2 else 0.0
        P.op("act", lambda e, bias=bias: e.activation(out=big(k, 1), in_=big(k, 1), func=AF.Exp, scale=-0.5, bias=bias), reads=[bigkey(1)], writes=[bigkey(1)])
        P.op("dve", lambda e, j=j: e.tensor_tensor(out=big(k, S + j), in0=big(k, S + j), in1=big(k, 1), op=ALU.mult),
             reads=[bigkey(S + j), bigkey(1)], writes=[bigkey(S + j)])
    k.dbg("gdn_qk%d%d" % (l, g), k.BIG[:, 8:14, :], [128, 6, T], [bigkey(j) for j in range(8, 14)])
    for j in range(4):
        P.op("act" if j % 2 == 0 else "dve", (lambda e, j=j: e.activation(out=k.hT[:, j, :], in_=big(k, S + j), func=AF.Copy)) if j % 2 == 0 else
             (lambda e, j=j: e.tensor_copy(out=k.hT[:, j, :], in_=big(k, S + j))), reads=[bigkey(S + j)], writes=[("hT", 0), ("hT", 1)])
    kHT = [("hT", 0), ("hT", 1)]
    LHg = big(k, 0)[:, 0:512].rearrange("p (h s) -> p h s", h=4)
    EE = big(k, 0)[:, 512:768].rearrange("p (a s) -> p a s", a=2)
    EE2 = big(k, 1)[:, 0:256].rearrange("p (a s) -> p a s", a=2)
    KV = big(k, 1)[:, 512:1024]
    VBKB = bfv(k, 2)[:, 0:512].rearrange("p (a h x) -> p a h x", a=2, h=4)
    b14 = bfv(k, 14)
    b15 = bfv(k, 15)
    attT = [b15[:, 1024 + 128 * i_:1024 + 128 * (i_ + 1)] for i_ in range(4)]
    TTb = [b15[:, 1536 + 128 * i_:1536 + 128 * (i_ + 1)] for i_ in range(4)]
    Sbf = b14[:, 1536:1664].rearrange("p (a x) -> p a x", a=2)
    AMb = [big(k, 3 + p_)[:, 0:512].rearrange("p (a s) -> p a s", a=4) for p_ in range(2)]
    ATMb = [big(k, 3 + p_)[:, 512:896].rearrange("p (a s) -> p a s", a=3) for p_ in range(2)]
    XRb = [[big(k, 5 if p_ == 0 else 18)[:, pp * 256:(pp + 1) * 256] for pp in range(2)] for p_ in range(2)]
    XPb2 = [[big(k, 5 if p_ == 0 else 18)[:, 512 + pp * 256:512 + (pp + 1) * 256] for pp in range(2)] for p_ in range(2)]
    Ybuf = [(big(k, 0)[:, 768:896], big(k, 0)[:, 896:1024]), (big(k, 1)[:, 256:384], big(k, 1)[:, 384:512])]
    uS = big(k, 14)[:, 0:256].rearrange("p (h x) -> p h x", h=4)
    wT = b14[:, 512:768].rearrange("p (a c) -> p a c", a=2)
    vn = b14[:, 768:1024].rearrange("p (h x) -> p h x", h=4)
    kt = b14[:, 1024:1280].rearrange("p (h x) -> p h x", h=4)
    otmp = big(k, 15)[:, 0:256].rearrange("p (h x) -> p h x", h=4)
    Otok = k.BIG[:, 16:18, :].rearrange("p a (c x) -> p (a c) x", x=256)
    kOt = [bigkey(16), bigkey(17)]
    nchs = L // 128
    for d in range(2):
        ml = LE if d == 0 else GE
        mu = GT if d == 0 else LT
        col = 127 if d == 0 else 0
        for s in range(nseq):
            if g == 0:
                P.op("dve", lambda e: e.memset(c["S"][:], 0.0), writes=[("gdn_S", hq) for hq in range(4)])
            else:
                for h in range(4):
                    b, pr = h % 2, h // 2
                    P.op("sp", lambda e, h=h, b=b, pr=pr, d=d: e.dma_start(out=c["S"][b * 64:(b + 1) * 64, pr, :], in_=D["st_gdn"][l, d, h, :, :]), writes=[("gdn_S", h)], dma=True)
            P.op("act", lambda e: e.activation(out=Sbf, in_=c["S"][:], func=AF.Copy), reads=[("gdn_S", hq) for hq in range(4)], writes=[("gdn_S", hq) for hq in range(4)])
            order = range(nchs) if d == 0 else range(nchs - 1, -1, -1)
            for jj in order:
                ch = s * nchs + jj
                ts = slice(ch * 128, (ch + 1) * 128)
                psT, pkT = k.psn()
                for q in range(2):
                    P.op("pe", lambda e, q=q, psT=psT, ts=ts: e.transpose(psT[:, q * 128:(q + 1) * 128], big(k, S + 2 + q)[:, ts], k.ident[:]),
                         reads=[bigkey(S + 2 + q), "ident"], writes=[pkT])
                    P.op("pe", lambda e, q=q, psT=psT, ts=ts: e.transpose(psT[:, 256 + q * 128:256 + (q + 1) * 128], big(k, S + 4 + q)[:, ts], k.ident[:]),
                         reads=[bigkey(S + 4 + q), "ident"], writes=[pkT])
                P.op("act", lambda e, psT=psT: e.activation(out=KV, in_=psT[:, :], func=AF.Copy), reads=[pkT], writes=[("gdn", "KV")])
                psA, pkA = k.psn()
                P.op("pe", lambda e, psA=psA, ch=ch, d=d, ml=ml: e.matmul(psA[:, 0:4], lhsT=ml, rhs=c["gt"][:, ch, d * 4:(d + 1) * 4], start=True, stop=True),
                     reads=["masks", "gdn_gt"], writes=[pkA])
                P.op("pe", lambda e, psA=psA, ch=ch, d=d: e.matmul(psA[:, 4:8], lhsT=ONES, rhs=c["gt"][:, ch, d * 4:(d + 1) * 4], start=True, stop=True),
                     reads=["masks", "gdn_gt"], writes=[pkA])
                P.op("act", lambda e, psA=psA: e.activation(out=c["egc"][:], in_=psA[:, 0:4], func=AF.Exp), reads=[pkA], writes=["gdn_egc"])
                P.op("act", lambda e, psA=psA: e.activation(out=c["gam"][:], in_=psA[:, 4:8], func=AF.Exp), reads=[pkA], writes=["gdn_gam"])
                P.op("dve", lambda e, ch=ch, d=d: e.tensor_tensor(out=c["bg"][:], in0=c["beta"][:, ch, d * 4:(d + 1) * 4], in1=c["egc"][:], op=ALU.mult),
                     reads=["gdn_beta", "gdn_egc"], writes=["gdn_bg"])
                P.op("dve", lambda e, ch=ch, d=d: e.tensor_tensor(
                    out=VBKB[:, 0, :, :], in0=KV[:, 256:512].rearrange("p (h x) -> p h x", h=4),
                    in1=c["beta"][:, ch, d * 4:(d + 1) * 4].unsqueeze(2).to_broadcast([128, 4, 64]), op=ALU.mult),
                    reads=[("gdn", "KV"), "gdn_beta"], writes=[("gdn", "VBKB")])
                P.op("dve", lambda e: e.tensor_tensor(
                    out=VBKB[:, 1, :, :], in0=KV[:, 0:256].rearrange("p (h x) -> p h x", h=4),
                    in1=c["bg"][:].unsqueeze(2).to_broadcast([128, 4, 64]), op=ALU.mult),
                    reads=[("gdn", "KV"), "gdn_bg"], writes=[("gdn", "VBKB")])
                P.op("dve", lambda e, ch=ch, d=d, mu=mu: e.tensor_tensor(
                    out=LHg, in0=mu.unsqueeze(1).to_broadcast([128, 4, 128]),
                    in1=c["gt"][:, ch, d * 4:(d + 1) * 4].unsqueeze(2).to_broadcast([128, 4, 128]), op=ALU.mult),
                    reads=["masks", "gdn_gt"], writes=[("gdn", "LHg")])
                def unit(h, d=d, s=s, jj=jj, ch=ch, ts=ts, ml=ml, mu=mu, col=col):
                    b, pr = h % 2, h // 2
                    bs = slice(b * 64, (b + 1) * 64)
                    u = d * 4 + h
                    ee = EE if h % 2 == 0 else EE2
                    kee = ("gdn", "EE", h % 2)
                    at = attT[h]
                    kat = ("gdn", "att", h)
                    kn = big(k, S + 2 + pr)
                    qn = big(k, S + 0 + pr)
                    psD, pkD = k.psn()
                    P.op("pe", lambda e, psD=psD, h=h, ml=ml: e.matmul(psD[:, 0:128], lhsT=ml, rhs=LHg[:, h, :], start=True, stop=True),
                         reads=["masks", ("gdn", "LHg")], writes=[pkD])
                    P.op("pe", lambda e, psD=psD, h=h, ml=ml: e.matmul(psD[:, 128:256], lhsT=LHg[:, h, :], rhs=ml, start=True, stop=True),
                         reads=["masks", ("gdn", "LHg")], writes=[pkD])
                    P.op("act", lambda e, psD=psD, ee=ee: e.activation(out=ee, in_=psD[:, 0:256].rearrange("p (a s) -> p a s", a=2), func=AF.Exp),
                         reads=[pkD], writes=[kee])
                    P.op("dve", lambda e, ee=ee, d=d: e.tensor_tensor(out=ee, in0=ee, in1=M2[:, d, :].rearrange("p (a s) -> p a s", a=2), op=ALU.mult),
                         reads=[kee, "masks2"], writes=[kee])
                    P.op("dve", lambda e, h=h, ee=ee, col=col: e.tensor_scalar(out=kt[:, h, :], in0=KV[:, h * 64:(h + 1) * 64], scalar1=ee[:, 1, col:col + 1], scalar2=None, op0=ALU.mult),
                         reads=[("gdn", "KV"), kee], writes=[("gdn", "kt", h)])
                    par = h % 2
                    AM, ATM = AMb[par], ATMb[par]
                    XR, XP = XRb[par], XPb2[par]
                    Yb, Y2b = Ybuf[par]
                    kAM, kATM, kY, kY2 = ("gdn", "AM", par), ("gdn", "ATM", par), ("gdn", "Y", par), ("gdn", "Y2", par)
                    kXR = [("gdn", "XR", par, 0), ("gdn", "XR", par, 1)]
                    kXP = [("gdn", "XP", par, 0), ("gdn", "XP", par, 1)]
                    yield
                    psK, pkK = k.psn()
                    P.op("pe", lambda e, psK=psK, kn=kn, bs=bs, ts=ts: e.matmul(psK[:, 0:128], lhsT=kn[bs, ts], rhs=kn[bs, ts], start=True, stop=True),
                         reads=[bigkey(S + 2 + pr)], writes=[pkK])
                    P.op("dve", lambda e, psK=psK, ee=ee, Yb=Yb, ch=ch, u=u: e.scalar_tensor_tensor(
                        out=Yb, in0=psK[:, 0:128], scalar=c["beta"][:, ch, u:u + 1], in1=ee[:, 0, :], op0=ALU.mult, op1=ALU.mult),
                        reads=[pkK, kee, "gdn_beta"], writes=[kY])
                    yield
                    psQ, pkQ = k.psn()
                    P.op("pe", lambda e, psQ=psQ, bs=bs, ts=ts, pr=pr: e.matmul(psQ[:, 0:128], lhsT=k.hT[bs, 2 + pr, ts], rhs=k.hT[bs, pr, ts], start=True, stop=True),
                         reads=kHT, writes=[pkQ])
                    P.op("dve", lambda e, psQ=psQ, ee=ee, at=at: e.tensor_tensor(out=at, in0=psQ[:, 0:128], in1=ee[:, 1, :], op=ALU.mult),
                         reads=[pkQ, kee], writes=[kat])
                    first = (d == 0 and s == 0 and jj == 0 and h == 0 and l == 0 and g == 0)
                    if first:
                        k.dbg("gdn_A", Yb, [128, 128], [kY])
                    yield
                    psX, pkX = k.psn()
                    P.op("pe", lambda e, psX=psX, Yb=Yb: e.transpose(psX[:, 0:128], Yb, k.ident[:]), reads=[kY, "ident"], writes=[pkX])
                    P.op("act", lambda e, psX=psX, Y2b=Y2b: e.activation(out=Y2b, in_=psX[:, 0:128], func=AF.Copy), reads=[pkX], writes=[kY2])
                    P.op("dve", lambda e, Yb=Yb, AM=AM: e.tensor_tensor(out=AM[:, 0:3, :], in0=Yb.unsqueeze(1).to_broadcast([128, 3, 128]), in1=k.masks4[:, 0:3, :], op=ALU.mult),
                         reads=[kY, "masks4"], writes=[kAM])
                    P.op("dve", lambda e, Y2b=Y2b, ATM=ATM: e.tensor_tensor(out=ATM[:, 0:2, :], in0=Y2b.unsqueeze(1).to_broadcast([128, 2, 128]), in1=k.masks4[:, 0:2, :], op=ALU.mult),
                         reads=[kY2, "masks4"], writes=[kATM])
                    P.op("dve", lambda e, XR=XR, AM=AM: e.tensor_tensor(out=XR[1][:, 128:256], in0=k.ident[:], in1=AM[:, 0, :], op=ALU.subtract),
                         reads=[kAM, "ident"], writes=[kXR[1]])
                    P.op("dve", lambda e, XP=XP, ATM=ATM: e.tensor_tensor(out=XP[1][:, 128:256], in0=k.ident[:], in1=ATM[:, 0, :], op=ALU.subtract),
                         reads=[kATM, "ident"], writes=[kXP[1]])
                    yield
                    ps1, pk1 = k.psn()
                    P.op("pe", lambda e, ps1=ps1, AM=AM, ATM=ATM: e.matmul(ps1[:, 0:128], lhsT=ATM[:, 0, :], rhs=AM[:, 0, :], start=True, stop=True),
                         reads=[kAM, kATM], writes=[pk1])
                    P.op("pe", lambda e, ps1=ps1, AM=AM, ATM=ATM: e.matmul(ps1[:, 128:256], lhsT=AM[:, 0, :], rhs=ATM[:, 0, :], start=True, stop=True),
                         reads=[kAM, kATM], writes=[pk1])
                    P.op("act", lambda e, ps1=ps1, XR=XR: e.activation(out=XR[1][:, 0:128], in_=ps1[:, 0:128], func=AF.Copy), reads=[pk1], writes=[kXR[1]])
                    P.op("act", lambda e, ps1=ps1, XP=XP: e.activation(out=XP[1][:, 0:128], in_=ps1[:, 128:256], func=AF.Copy), reads=[pk1], writes=[kXP[1]])
                    for j in range(1, 5):
                        cur, nxt = j % 2, (j + 1) % 2
                        yield
                        psA2, pkA2 = k.psn()
                        psB2, pkB2 = k.psn()
                        if j <= 3:
                            P.op("pe", lambda e, psA2=psA2, XR=XR, XP=XP, cur=cur: e.matmul(psA2[:, 0:256], lhsT=XR[cur][:, 0:128], rhs=XP[cur], start=True, stop=True),
                                 reads=[kXR[cur], kXP[cur]], writes=[pkA2])
                            P.op("pe", lambda e, psB2=psB2, XR=XR, XP=XP, cur=cur: e.matmul(psB2[:, 0:256], lhsT=XP[cur][:, 0:128], rhs=XR[cur], start=True, stop=True),
                                 reads=[kXR[cur], kXP[cur]], writes=[pkB2])
                            P.op("act", lambda e, psA2=psA2, XP=XP, nxt=nxt: e.activation(out=XP[nxt][:, 0:128], in_=psA2[:, 0:128], func=AF.Copy), reads=[pkA2], writes=[kXP[nxt]])
                            P.op("act", lambda e, psB2=psB2, XR=XR, nxt=nxt: e.activation(out=XR[nxt][:, 0:128], in_=psB2[:, 0:128], func=AF.Copy), reads=[pkB2], writes=[kXR[nxt]])
                        else:
                            P.op("pe", lambda e, psA2=psA2, XR=XR, XP=XP, cur=cur: e.matmul(psA2[:, 128:256], lhsT=XR[cur][:, 0:128], rhs=XP[cur][:, 128:256], start=True, stop=True),
                                 reads=[kXR[cur], kXP[cur]], writes=[pkA2])
                            P.op("pe", lambda e, psB2=psB2, XR=XR, XP=XP, cur=cur: e.matmul(psB2[:, 128:256], lhsT=XP[cur][:, 0:128], rhs=XR[cur][:, 128:256], start=True, stop=True),
                                 reads=[kXR[cur], kXP[cur]], writes=[pkB2])
                        P.op("dve", lambda e, psA2=psA2, XP=XP, cur=cur, nxt=nxt: e.tensor_tensor(out=XP[nxt][:, 128:256], in0=XP[cur][:, 128:256], in1=psA2[:, 128:256], op=ALU.add),
                             reads=[pkA2, kXP[cur]], writes=[kXP[nxt]])
                        P.op("dve", lambda e, psB2=psB2, XR=XR, cur=cur, nxt=nxt: e.tensor_tensor(out=XR[nxt][:, 128:256], in0=XR[cur][:, 128:256], in1=psB2[:, 128:256], op=ALU.add),
                             reads=[pkB2, kXR[cur]], writes=[kXR[nxt]])
                    Tc, Qc = XR[1][:, 128:256], XP[1][:, 128:256]
                    Tn, Qn = XR[0][:, 128:256], XP[0][:, 128:256]
                    kT, kQ = [kXR[1], kXR[0]], [kXP[1], kXP[0]]
                    tb = [(Tc, Qc), (Tn, Qn)]
                    LAST = 1
                    for lev in range(2):
                        cur, nxt = lev % 2, (lev + 1) % 2
                        Tcur, Qcur = tb[cur]
                        Tnxt, Qnxt = tb[nxt]
                        yield
                        psY, pkY = k.psn()
                        P.op("pe", lambda e, psY=psY, AM=AM, Qcur=Qcur, lev=lev: e.matmul(psY[:, 0:128], lhsT=AM[:, 1 + lev, :], rhs=Qcur, start=True, stop=True),
                             reads=[kAM, kQ[cur]], writes=[pkY])
                        if lev < LAST:
                            P.op("pe", lambda e, psY=psY, ATM=ATM, Tcur=Tcur, lev=lev: e.matmul(psY[:, 128:256], lhsT=ATM[:, 1 + lev, :], rhs=Tcur, start=True, stop=True),
                                 reads=[kATM, kT[cur]], writes=[pkY])
                        P.op("act", lambda e, psY=psY, Yb=Yb: e.activation(out=Yb, in_=psY[:, 0:128], func=AF.Copy), reads=[pkY], writes=[kY])
                        if lev < LAST:
                            P.op("act", lambda e, psY=psY, Y2b=Y2b: e.activation(out=Y2b, in_=psY[:, 128:256], func=AF.Copy), reads=[pkY], writes=[kY2])
                        yield
                        psZ, pkZ = k.psn()
                        P.op("pe", lambda e, psZ=psZ, Tcur=Tcur, Yb=Yb: e.matmul(psZ[:, 0:128], lhsT=Tcur, rhs=Yb, start=True, stop=True),
                             reads=[kT[cur], kY], writes=[pkZ])
                        if lev < LAST:
                            P.op("pe", lambda e, psZ=psZ, Qcur=Qcur, Y2b=Y2b: e.matmul(psZ[:, 128:256], lhsT=Qcur, rhs=Y2b, start=True, stop=True),
                                 reads=[kQ[cur], kY2], writes=[pkZ])
                        qdst = Qnxt if lev < LAST else TTb[h]
                        P.op("dve", lambda e, psZ=psZ, Qcur=Qcur, qdst=qdst: e.tensor_tensor(out=qdst, in0=Qcur, in1=psZ[:, 0:128], op=ALU.subtract),
                             reads=[pkZ, kQ[cur]], writes=[kQ[nxt] if lev < LAST else ("gdn", "TT", h)])
                        if lev < LAST:
                            P.op("dve", lambda e, psZ=psZ, Tcur=Tcur, Tnxt=Tnxt: e.tensor_tensor(out=Tnxt, in0=Tcur, in1=psZ[:, 128:256], op=ALU.subtract),
                                 reads=[pkZ, kT[cur]], writes=[kT[nxt]])
                    TT = TTb[h]
                    kTT = ("gdn", "TT", h)
                    if first:
                        k.dbg("gdn_TT", TT, [128, 128], [kTT])
                    yield "TAIL"
                    psU, pkU = k.psn()
                    P.op("pe", lambda e, psU=psU, TT=TT, h=h: e.matmul(psU[:, 0:64], lhsT=TT, rhs=VBKB[:, 0, h, :], start=True, stop=True),
                         reads=[kTT, ("gdn", "VBKB")], writes=[pkU])
                    P.op("pe", lambda e, psU=psU, TT=TT, h=h, bs=bs: e.matmul(psU[bs, 128:256], lhsT=VBKB[:, 1, h, :], rhs=TT, start=True, stop=True),
                         reads=[kTT, ("gdn", "VBKB")], writes=[pkU])
                    P.op("act", lambda e, psU=psU, h=h: e.activation(out=uS[:, h, :], in_=psU[:, 0:64], func=AF.Copy), reads=[pkU], writes=[("gdn", "u", h)])
                    P.op("act", lambda e, psU=psU, bs=bs, pr=pr: e.activation(out=wT[bs, pr, :], in_=psU[bs, 128:256], func=AF.Copy), reads=[pkU], writes=[("gdn", "wT", h)])
                    yield
                    psV, pkV = k.psn()
                    P.op("pe", lambda e, psV=psV, bs=bs, pr=pr: e.matmul(psV[:, 0:64], lhsT=wT[bs, pr, :], rhs=Sbf[bs, pr, :], start=True, stop=True),
                         reads=[("gdn", "wT", h), ("gdn_S", h)], writes=[pkV])
                    P.op("dve", lambda e, psV=psV, h=h: e.tensor_tensor(out=vn[:, h, :], in0=uS[:, h, :], in1=psV[:, 0:64], op=ALU.subtract),
                         reads=[pkV, ("gdn", "u", h)], writes=[("gdn", "vn", h)])
                    yield
                    psO1, pkO1 = k.psn()
                    P.op("pe", lambda e, psO1=psO1, qn=qn, bs=bs, pr=pr, ts=ts: e.matmul(psO1[:, 0:64], lhsT=k.hT[bs, pr, ts], rhs=Sbf[bs, pr, :], start=True, stop=True),
                         reads=kHT + [("gdn_S", h)], writes=[pkO1])
                    P.op("act", lambda e, psO1=psO1, h=h: e.activation(out=otmp[:, h, :], in_=psO1[:, 0:64], func=AF.Copy, scale=c["egc"][:, h:h + 1]),
                         reads=[pkO1, "gdn_egc"], writes=[("gdn", "otmp", h)])
                    yield
                    psO2, pkO2 = k.psn()
                    P.op("pe", lambda e, psO2=psO2, at=at, h=h: e.matmul(psO2[:, 0:64], lhsT=at, rhs=vn[:, h, :], start=True, stop=True),
                         reads=[kat, ("gdn", "vn", h)], writes=[pkO2])
                    oslice = Otok[:, ch, h * 64:(h + 1) * 64]
                    if d == 0:
                        P.op("dve", lambda e, psO2=psO2, h=h, oslice=oslice: e.tensor_tensor(out=oslice, in0=otmp[:, h, :], in1=psO2[:, 0:64], op=ALU.add),
                             reads=[pkO2, ("gdn", "otmp", h)], writes=kOt)
                    else:
                        P.op("dve", lambda e, psO2=psO2, h=h: e.tensor_tensor(out=otmp[:, h, :], in0=otmp[:, h, :], in1=psO2[:, 0:64], op=ALU.add),
                             reads=[pkO2, ("gdn", "otmp", h)], writes=[("gdn", "otmp", h)])
                        P.op("dve", lambda e, h=h, oslice=oslice: e.tensor_tensor(out=oslice, in0=oslice, in1=otmp[:, h, :], op=ALU.add),
                             reads=[("gdn", "otmp", h)] + kOt, writes=kOt)
                    yield
                    psS, pkS = k.psn()
                    P.op("pe", lambda e, psS=psS, h=h, bs=bs: e.matmul(psS[bs, 0:64], lhsT=kt[:, h, :], rhs=vn[:, h, :], start=True, stop=True),
                         reads=[("gdn", "kt", h), ("gdn", "vn", h)], writes=[pkS])
                    P.op("dve", lambda e, psS=psS, h=h, bs=bs, pr=pr: e.scalar_tensor_tensor(
                        out=c["S"][bs, pr, :], in0=c["S"][bs, pr, :], scalar=c["gam"][bs, h:h + 1], in1=psS[bs, 0:64], op0=ALU.mult, op1=ALU.add),
                        reads=[pkS, ("gdn_S", h), "gdn_gam"], writes=[("gdn_S", h)])
                    P.op("act", lambda e, bs=bs, pr=pr: e.activation(out=Sbf[bs, pr, :], in_=c["S"][bs, pr, :], func=AF.Copy), reads=[("gdn_S", h)], writes=[("gdn_S", h)])
                    if first:
                        k.dbg("gdn_S0", c["S"][0:64, 0, :], [64, 64], [("gdn_S", h)])
                        k.dbg("gdn_kt", kt[:, 0, :], [128, 64], [("gdn", "kt", 0)])
                        k.dbg("gdn_vn", vn[:, 0, :], [128, 64], [("gdn", "vn", 0)])
                        k.dbg("gdn_gam", c["gam"][:], [128, 4], ["gdn_gam"])

                def lockstep(gens_, stop_at_tail):
                    active = list(gens_)
                    paused = []
                    while active:
                        for gq in list(active):
                            try:
                                v = next(gq)
                            except StopIteration:
                                active.remove(gq)
                                continue
                            if v == "TAIL" and gq in stop_at_tail:
                                active.remove(gq)
                                paused.append(gq)
                    return paused

                gA = [unit(0), unit(1)]
                lockstep(gA, gA)
                gB = [unit(2), unit(3)]
                lockstep(gA + gB, gB)
                lockstep(gB, [])
            if g == 0:
                for h in range(4):
                    b, pr = h % 2, h // 2
                    P.op("sp", lambda e, h=h, b=b, pr=pr, s=s, d=d: e.dma_start(out=D["ns_gdn"][s, l, d, h, :, :], in_=c["S"][b * 64:(b + 1) * 64, pr, :]), reads=[("gdn_S", h)], dma=True)
    k.dbg("gdn_o%d%d" % (l, g), Otok, [128, 8, 256], kOt)
    for q in range(2):
        P.op("act", lambda e, q=q: e.activation(out=big(k, 6 + q), in_=big(k, 6 + q), func=AF.Silu), reads=[bigkey(6 + q)], writes=[bigkey(6 + q)])
    sqb = big(k, 15)[:, 256:512]
    for ch in range(8):
        ts = slice(ch * 128, (ch + 1) * 128)
        P.op("dve", lambda e, ch=ch: e.tensor_tensor(out=sqb, in0=Otok[:, ch, :], in1=Otok[:, ch, :], op=ALU.mult), reads=kOt, writes=[bigkey(15)])
        P.op("dve", lambda e: e.tensor_reduce(out=c["ss"][:], in_=sqb.rearrange("p (h x) -> p h x", h=4), axis=AX.X, op=ALU.add), reads=[bigkey(15)], writes=["gdn_ss"])
        P.op("act", lambda e: e.activation(out=c["ss"][:], in_=c["ss"][:], func=AF.Ln, scale=1.0 / 64, bias=EPS), reads=["gdn_ss"], writes=["gdn_ss"])
        P.op("act", lambda e: e.activation(out=c["ss"][:], in_=c["ss"][:], func=AF.Exp, scale=-0.5), reads=["gdn_ss"], writes=["gdn_ss"])
        P.op("dve", lambda e, ch=ch: e.tensor_tensor(out=sqb.rearrange("p (h x) -> p h x", h=4), in0=Otok[:, ch, :].rearrange("p (h x) -> p h x", h=4),
                                                     in1=c["ss"][:].unsqueeze(2).to_broadcast([128, 4, 64]), op=ALU.mult), reads=kOt + ["gdn_ss"], writes=[bigkey(15)])
        P.op("dve", lambda e: e.tensor_tensor(out=sqb.rearrange("p (h x) -> p h x", h=4), in0=sqb.rearrange("p (h x) -> p h x", h=4),
                                              in1=c["gn"][:].unsqueeze(1).to_broadcast([128, 4, 64]), op=ALU.mult), reads=[bigkey(15), "gdn_gn"], writes=[bigkey(15)])
        psT, pkT = k.psn()
        for q in range(2):
            P.op("pe", lambda e, q=q, psT=psT: e.transpose(psT[:, q * 128:(q + 1) * 128], sqb[:, q * 128:(q + 1) * 128], k.ident[:]), reads=[bigkey(15), "ident"], writes=[pkT])
        P.op("dve", lambda e, psT=psT, ts=ts: e.tensor_tensor(out=k.mixT[:, 6:8, ts], in0=psT[:, 0:256].rearrange("p (q t) -> p q t", q=2),
                                                             in1=k.BIG[:, 6:8, ts], op=ALU.mult), reads=[pkT, bigkey(6), bigkey(7)], writes=[("mixT", 6), ("mixT", 7)])
I32 = mybir.dt.int32
TWO_PI = 6.283185307179586


def P_hy(k):
    if hasattr(k, "_hy"):
        return k._hy
    P = k.P
    k._hy = {
        "conv": P.sb("hy_conv", [128, 6, 3]),
        "w1": P.sb("hy_w1", [33, 64]),
        "w2": P.sb("hy_w2", [64, 64]),
        "w3": P.sb("hy_w3", [64, 512]),
        "vec": P.sb("hy_vec", [64, 6]),
        "bias": P.sb("hy_bias", [128, 2]),
        "ki": P.sb("hy_ki", [64, 1024], I32),
        "rn": P.sb("hy_rn", [128, 256]),
    }
    return k._hy


def hy_sin(k, dst, src_ps, pk, scale_ap, bias_ap, n, dkey):
    P = k.P
    c = P_hy(k)
    ki = c["ki"][:, 0:n]
    tmp = big(k, 7)[0:64, 0:n]
    kt = bigkey(7)
    P.op("act", lambda e: e.activation(out=dst, in_=src_ps, func=AF.Identity, scale=scale_ap, bias=bias_ap), reads=[pk, "hy_vec"], writes=[dkey])
    P.op("dve", lambda e: e.tensor_scalar(out=ki, in0=dst, scalar1=1.0 / TWO_PI, scalar2=None, op0=ALU.mult), reads=[dkey], writes=["hy_ki"])
    P.op("dve", lambda e: e.tensor_copy(out=tmp, in_=ki), reads=["hy_ki"], writes=[kt])
    P.op("dve", lambda e: e.scalar_tensor_tensor(out=dst, in0=tmp, scalar=-TWO_PI, in1=dst, op0=ALU.mult, op1=ALU.add), reads=[kt, dkey], writes=[dkey])
    P.op("dve", lambda e: e.tensor_single_scalar(out=tmp, in_=dst, scalar=math.pi, op=ALU.is_gt), reads=[dkey], writes=[kt])
    P.op("dve", lambda e: e.scalar_tensor_tensor(out=dst, in0=tmp, scalar=-TWO_PI, in1=dst, op0=ALU.mult, op1=ALU.add), reads=[kt, dkey], writes=[dkey])
    P.op("dve", lambda e: e.tensor_single_scalar(out=tmp, in_=dst, scalar=-math.pi, op=ALU.is_lt), reads=[dkey], writes=[kt])
    P.op("dve", lambda e: e.scalar_tensor_tensor(out=dst, in0=tmp, scalar=TWO_PI, in1=dst, op0=ALU.mult, op1=ALU.add), reads=[kt, dkey], writes=[dkey])
    P.op("act", lambda e: e.activation(out=dst, in_=dst, func=AF.Sin), reads=[dkey], writes=[dkey])


def hyena_filter(k, l, L):
    P, D = k.P, k.D
    c = P_hy(k)
    ONES = k.masks[:, 4, :]
    nb = L // 128
    sfx = "_%d" % L
    P.op("sp", lambda e: e.dma_start(out=c["w1"][:], in_=D["hy_w1"][l, :, :]), writes=["hy_w1"], dma=True)
    P.op("sp", lambda e: e.dma_start(out=c["w2"][:], in_=D["hy_w2"][l, :, :]), writes=["hy_w2"], dma=True)
    P.op("sp", lambda e: e.dma_start(out=c["w3"][:], in_=D["hy_w3"][l, :, :]), writes=["hy_w3"], dma=True)
    P.op("sp", lambda e: e.dma_start(out=c["vec"][:, 0:4], in_=D["hy_vec"][:, l, :]), writes=["hy_vec"], dma=True)
    P.op("dve", lambda e: e.tensor_tensor(out=c["vec"][:, 4:5], in0=c["vec"][:, 0:1], in1=c["vec"][:, 1:2], op=ALU.mult), reads=["hy_vec"], writes=["hy_vec"])
    P.op("dve", lambda e: e.tensor_tensor(out=c["vec"][:, 5:6], in0=c["vec"][:, 2:3], in1=c["vec"][:, 3:4], op=ALU.mult), reads=["hy_vec"], writes=["hy_vec"])
    feats = big(k, 4)[0:33, 0:L]
    h1 = big(k, 5)[0:64, 0:L]
    h2 = big(k, 6)[0:64, 0:L]
    P.op("sp", lambda e: e.dma_start(out=feats, in_=D["hy_featsT" + sfx][:, :]), writes=[bigkey(4)], dma=True)
    nt = max(1, L // 512)
    tw = min(L, 512)
    for (src, wkey, wt, dst, dslot, sc, bi) in ((feats, "hy_w1", c["w1"], h1, 5, 1, 4), (h1, "hy_w2", c["w2"], h2, 6, 3, 5)):
        for t in range(nt):
            ps, pk = k.psn()
            kin = 33 if dslot == 5 else 64
            P.op("pe", lambda e, ps=ps, wt=wt, src=src, t=t, kin=kin: e.matmul(ps[0:64, 0:tw], lhsT=wt[0:kin, :], rhs=src[:, t * tw:(t + 1) * tw], start=True, stop=True),
                 reads=[wkey, bigkey(4), bigkey(5)], writes=[pk])
            hy_sin(k, dst[:, t * tw:(t + 1) * tw], ps[0:64, 0:tw], pk, c["vec"][:, sc:sc + 1], c["vec"][:, bi:bi + 1], tw, bigkey(dslot))
    hraw = k.BIG[:, 8:12, :].rearrange("p a (b x) -> p (a b) x", x=512)
    kraw = [bigkey(8), bigkey(9), bigkey(10), bigkey(11)]
    env = k.BIG[:, 12:14, :].rearrange("p a (b x) -> p (a b) x", x=256)
    P.op("sp", lambda e: e.dma_start(out=env[:, 0:nb, :], in_=D["hy_env" + sfx].rearrange("(b p) c -> p b c", p=128)), writes=[bigkey(12), bigkey(13)], dma=True)
    habs = big(k, 2)[:, 0:512]
    psN, pkN = k.psn(hold=True)
    for b in range(nb):
        ps, pk = k.psn()
        P.op("pe", lambda e, ps=ps, b=b: e.matmul(ps[:, :], lhsT=h2[:, b * 128:(b + 1) * 128], rhs=c["w3"][:, :], start=True, stop=True),
             reads=[bigkey(6), "hy_w3"], writes=[pk])
        P.op("dve", lambda e, ps=ps, b=b: e.tensor_tensor(out=hraw[:, b, :].rearrange("p (a x) -> p a x", a=2), in0=ps[:, :].rearrange("p (a x) -> p a x", a=2),
                                                          in1=env[:, b, :].unsqueeze(1).to_broadcast([128, 2, 256]), op=ALU.mult),
             reads=[pk, bigkey(12), bigkey(13)], writes=kraw)
        P.op("act", lambda e, b=b: e.activation(out=habs, in_=hraw[:, b, :], func=AF.Abs), reads=kraw, writes=[bigkey(2)])
        P.op("pe", lambda e, psN=psN, b=b: e.matmul(psN[:, :], lhsT=ONES, rhs=habs, start=(b == 0), stop=(b == nb - 1)), reads=[bigkey(2), "masks"], writes=[pkN])
    k.ps_held.discard(pkN[1])
    P.op("act", lambda e, psN=psN: e.activation(out=c["rn"][:], in_=psN[:, 0:256], func=AF.Copy), reads=[pkN], writes=["hy_rn"])
    P.op("dve", lambda e, psN=psN: e.tensor_tensor(out=c["rn"][:], in0=c["rn"][:], in1=psN[:, 256:512], op=ALU.add), reads=[pkN, "hy_rn"], writes=["hy_rn"])
    P.op("dve", lambda e: e.reciprocal(out=c["rn"][:], in_=c["rn"][:]), reads=["hy_rn"], writes=["hy_rn"])
    hs = bfv(k, 14).rearrange("p (b x) -> p b x", x=256)
    hd = bfv(k, 15).rearrange("p (b x) -> p b x", x=256)
    tmpn = big(k, 2)[:, 512:768]
    for b in range(nb):
        P.op("dve", lambda e, b=b: e.tensor_tensor(out=tmpn, in0=hraw[:, b, 0:256], in1=hraw[:, b, 256:512], op=ALU.add), reads=kraw, writes=[bigkey(2)])
        P.op("dve", lambda e, b=b: e.tensor_tensor(out=hs[:, b, :], in0=tmpn, in1=c["rn"][:], op=ALU.mult), reads=[bigkey(2), "hy_rn"], writes=[bigkey(14)])
        P.op("dve", lambda e, b=b: e.tensor_tensor(out=tmpn, in0=hraw[:, b, 256:512], in1=hraw[:, b, 0:256], op=ALU.subtract), reads=kraw, writes=[bigkey(2)])
        P.op("dve", lambda e, b=b: e.tensor_tensor(out=hd[:, b, :], in0=tmpn, in1=c["rn"][:], op=ALU.mult), reads=[bigkey(2), "hy_rn"], writes=[bigkey(15)])
    dst = D["hyf_%d_%d" % (l, L)]
    fkey = ("hyf", l, L)
    P.op("sp", lambda e: e.dma_start(out=dst[0].rearrange("p (b x) -> p b x", x=256), in_=hs[:, 0:nb, :]), reads=[bigkey(14)], writes=[fkey], dma=True)
    P.op("sp", lambda e: e.dma_start(out=dst[1].rearrange("p (b x) -> p b x", x=256), in_=hd[:, 0:nb, :]), reads=[bigkey(15)], writes=[(fkey, 1)], dma=True)


def hyena_mixer(k, l, g, nseq, L):
    P, D = k.P, k.D
    c = P_hy(k)
    nb = L // 128
    sfx = "_%d" % L
    P.op("sp", lambda e: e.dma_start(out=c["conv"][:], in_=D["hy_conv_fm"][:, l, :, :]), writes=["hy_conv"], dma=True)
    P.op("sp", lambda e: e.dma_start(out=c["bias"][:], in_=D["hy_bias_fm"][:, l, :]), writes=["hy_bias"], dma=True)
    hs = bfv(k, 14).rearrange("p (b x) -> p b x", x=256)
    hd = bfv(k, 15).rearrange("p (b x) -> p b x", x=256)
    src = D["hyf_%d_%d" % (l, L)]
    fkey = ("hyf", l, L)
    P.op("sp", lambda e: e.dma_start(out=hs[:, 0:nb, :], in_=src[0].rearrange("p (b x) -> p b x", x=256)), reads=[fkey], writes=[bigkey(14)], dma=True)
    P.op("sp", lambda e: e.dma_start(out=hd[:, 0:nb, :], in_=src[1].rearrange("p (b x) -> p b x", x=256)), reads=[(fkey, 1)], writes=[bigkey(15)], dma=True)
    for gi in range(3):
        wb, wk = k.w_next()
        w3v = wb[:, 0:2048].rearrange("p (kc c) -> p kc c", c=256)
        proj_fm(k, w3v, wk, [(0, 128), (128, 128)], evac_to_big(k, lambda i, gi=gi: gi * 2 + i))
    S = 8
    for j in range(6):
        dwconv_fm(k, j, S + j, c["conv"][:, j, :], "hy_conv", 3, nseq, L)
    for q in range(2):
        P.op("dve", lambda e, q=q: e.tensor_tensor(out=big(k, S + 4 + q), in0=big(k, S + 4 + q), in1=big(k, S + 2 + q), op=ALU.mult),
             reads=[bigkey(S + 4 + q), bigkey(S + 2 + q)], writes=[bigkey(S + 4 + q)])
    ztok = bfv(k, 18).rearrange("p (b x) -> p b x", x=256)
    for b2 in range(4):
        ps, pk = k.psn()
        for bb in range(2):
            b = b2 * 2 + bb
            for q in range(2):
                P.op("pe", lambda e, ps=ps, b=b, bb=bb, q=q: e.transpose(ps[:, bb * 256 + q * 128:bb * 256 + (q + 1) * 128], big(k, S + 4 + q)[:, b * 128:(b + 1) * 128], k.ident[:]),
                     reads=[bigkey(S + 4 + q), "ident"], writes=[pk])
        P.op("act", lambda e, ps=ps, b2=b2: e.activation(out=ztok[:, b2 * 2:b2 * 2 + 2, :], in_=ps[:, :].rearrange("p (b x) -> p b x", b=2), func=AF.Copy),
             reads=[pk], writes=[bigkey(18)])
    Ysp = k.BIG[:, 0:2, :].bitcast(BF16).rearrange("p a (j x) -> p (a j) x", x=512)
    kY = [bigkey(0), bigkey(1)]
    G2 = big(k, 2)[:, 0:512]
    G3 = big(k, 2)[:, 512:1024]
    Atmp = big(k, 3)[:, 0:512]
    Btmp = big(k, 3)[:, 512:1024]
    for fb in range(nb):
        st = (16, 17, 6, 7)[fb % 4]
        CS = bfv(k, st)[:, 0:nb * 256].rearrange("p (b x) -> p b x", x=256)
        P.op("sp", lambda e, CS=CS, fb=fb: e.dma_start(out=CS, in_=D["dftF" + sfx][:, fb, :, :].rearrange("(b p) c f -> p b (c f)", p=128)), writes=[bigkey(st)], dma=True)
        psG, pkG = k.psn()
        for b in range(nb):
            P.op("pe", lambda e, psG=psG, CS=CS, b=b: e.matmul(psG[:, 0:256], lhsT=CS[:, b, 0:128], rhs=hs[:, b, :], start=(b == 0), stop=(b == nb - 1)),
                 reads=[bigkey(st), bigkey(14)], writes=[pkG])
        for b in range(nb):
            P.op("pe", lambda e, psG=psG, CS=CS, b=b: e.matmul(psG[:, 256:512], lhsT=CS[:, b, 128:256], rhs=hd[:, b, :], start=(b == 0), stop=(b == nb - 1)),
                 reads=[bigkey(st), bigkey(15)], writes=[pkG])
        P.op("act", lambda e, psG=psG: e.activation(out=G2.rearrange("p (a x) -> p a x", a=2), in_=psG[:, 0:256].unsqueeze(1).to_broadcast([128, 2, 256]), func=AF.Copy),
             reads=[pkG], writes=[bigkey(2)])
        P.op("act", lambda e, psG=psG: e.activation(out=G3.rearrange("p (a x) -> p a x", a=2), in_=psG[:, 256:512].unsqueeze(1).to_broadcast([128, 2, 256]), func=AF.Copy),
             reads=[pkG], writes=[bigkey(2)])
        for s in range(nseq):
            psZ, pkZ = k.psn()
            for b in range(nb):
                P.op("pe", lambda e, psZ=psZ, CS=CS, b=b, s=s: e.matmul(psZ[:, 0:256], lhsT=CS[:, b, 0:128], rhs=ztok[:, s * nb + b, :], start=(b == 0), stop=(b == nb - 1)),
                     reads=[bigkey(st), bigkey(18)], writes=[pkZ])
            for b in range(nb):
                P.op("pe", lambda e, psZ=psZ, CS=CS, b=b, s=s: e.matmul(psZ[:, 256:512], lhsT=CS[:, b, 128:256], rhs=ztok[:, s * nb + b, :], start=(b == 0), stop=(b == nb - 1)),
                     reads=[bigkey(st), bigkey(18)], writes=[pkZ])
            yi = fb * nseq + s
            P.op("dve", lambda e, psZ=psZ: e.tensor_tensor(out=Atmp, in0=psZ[:, :], in1=G2, op=ALU.mult), reads=[pkZ, bigkey(2)], writes=[bigkey(3)])
            P.op("dve", lambda e, psZ=psZ: e.tensor_tensor(out=Btmp, in0=psZ[:, :], in1=G3, op=ALU.mult), reads=[pkZ, bigkey(2)], writes=[bigkey(3)])
            P.op("dve", lambda e, yi=yi: e.tensor_tensor(out=Ysp[:, yi, 0:256], in0=Atmp[:, 0:256], in1=Btmp[:, 256:512], op=ALU.add), reads=[bigkey(3)], writes=kY)
            P.op("dve", lambda e, yi=yi: e.tensor_tensor(out=Ysp[:, yi, 256:512], in0=Atmp[:, 256:512], in1=Btmp[:, 0:256], op=ALU.subtract), reads=[bigkey(3)], writes=kY)
    tw = min(L, 512)
    tiles = [(s, cc, th) for s in range(nseq) for cc in range(2) for th in range(L // tw)]
    for bi0 in range(0, len(tiles), 4):
        batch = tiles[bi0:bi0 + 4]
        pss = [k.psn() for _ in batch]
        for fb in range(nb):
            st = (16, 17, 6, 7)[fb % 4]
            CST = bfv(k, st)[:, 0:2 * L].rearrange("p (a t) -> p a t", a=2)
            P.op("sp", lambda e, CST=CST, fb=fb: e.dma_start(out=CST, in_=D["dftI" + sfx][fb * 128:(fb + 1) * 128, :, :]), writes=[bigkey(st)], dma=True)
            for ti, (s, cc, th) in enumerate(batch):
                ps, pk = pss[ti]
                yi = fb * nseq + s
                for a in range(2):
                    P.op("pe", lambda e, ps=ps, CST=CST, yi=yi, cc=cc, th=th, a=a, fb=fb: e.matmul(
                        ps[:, 0:tw], lhsT=Ysp[:, yi, a * 256 + cc * 128:a * 256 + (cc + 1) * 128], rhs=CST[:, a, th * tw:(th + 1) * tw],
                        start=(fb == 0 and a == 0), stop=(fb == nb - 1 and a == 1)), reads=kY + [bigkey(st)], writes=[pk])
        for ti, (s, cc, th) in enumerate(batch):
            ps, pk = pss[ti]
            tsl = slice(s * L + th * tw, s * L + (th + 1) * tw)
            tmp = big(k, 4 + ti % 2)[:, 0:tw]
            P.op("dve", lambda e, ps=ps, cc=cc, tsl=tsl, tmp=tmp: e.scalar_tensor_tensor(
                out=tmp, in0=big(k, S + 4 + cc)[:, tsl], scalar=c["bias"][:, cc:cc + 1], in1=ps[:, 0:tw], op0=ALU.mult, op1=ALU.add),
                reads=[pk, bigkey(S + 4 + cc), "hy_bias"], writes=[bigkey(4 + ti % 2)])
            P.op("dve", lambda e, cc=cc, tsl=tsl, tmp=tmp: e.tensor_tensor(out=k.mixT[:, cc, tsl], in0=tmp, in1=big(k, S + cc)[:, tsl], op=ALU.mult),
                 reads=[bigkey(4 + ti % 2), bigkey(S + cc)], writes=[("mixT", cc)])
import math
def _grid_pos_embed(n_tokens, d_model=1024, grid_w=64):
    rows = n_tokens // grid_w
    rr, cc = np.meshgrid(np.arange(rows, dtype=np.float32), np.arange(grid_w, dtype=np.float32), indexing='ij')
    quarter = d_model // 4
    omega = (1.0 / (np.float32(10000.0) ** (np.arange(quarter, dtype=np.float32) / np.float32(quarter)))).astype(np.float32)
    def enc(pos):
        ang = pos.reshape(-1)[:, None].astype(np.float32) * omega[None, :]
        return np.concatenate([np.sin(ang), np.cos(ang)], axis=-1)
    return np.concatenate([enc(rr), enc(cc)], axis=-1).astype(np.float32)


def _fm(v, nchunk):
    v = np.asarray(v, np.float32)
    lead = v.shape[:-1]
    v = v.reshape(lead + (nchunk, 128))
    v = np.moveaxis(v, -1, 0)
    return np.ascontiguousarray(v)


def make_inputs(inp):
    f = lambda a: np.ascontiguousarray(np.asarray(a, np.float32))
    shared = {}
    shared["pos"] = _grid_pos_embed(1024)
    shared["ident"] = np.eye(128, dtype=np.float32)
    for n in ("w_mod", "w_in", "w_out", "w_gate", "w_up", "w_down"):
        shared[n] = f(inp[n])
    shared["b_mod_fm"] = _fm(inp["b_mod"], 48)
    shared["g_mix_fm"] = _fm(inp["g_mix"], 8)
    shared["g_ffn_fm"] = _fm(inp["g_ffn"], 8)
    shared["g_final"] = f(inp["g_final"]).reshape(1, 1024)
    lc = np.asarray(inp["lru_conv"], np.float32)
    shared["lru_conv_fm"] = np.ascontiguousarray(lc.reshape(2, 4, 2, 128).transpose(3, 0, 2, 1))
    wbd = np.zeros((2, 2, 2, 2, 128, 128), np.float32)
    for ri, nm in enumerate(("lru_w_r", "lru_w_i")):
        w = np.asarray(inp[nm], np.float32)
        for q in range(2):
            for hh in range(2):
                wbd[:, ri, :, q, hh * 64:(hh + 1) * 64, hh * 64:(hh + 1) * 64] = w[:, :, 2 * q + hh]
    shared["lru_wbd"] = wbd
    b = np.stack([np.asarray(inp["lru_b_r"], np.float32), np.asarray(inp["lru_b_i"], np.float32)], axis=1)
    shared["lru_b_fm"] = _fm(b, 2)
    shared["lru_lam_fm"] = _fm(inp["lru_lambda"], 2)
    r = np.arange(128)
    LE = (r[:, None] <= r[None, :]); GE = (r[:, None] >= r[None, :]); GT = (r[:, None] > r[None, :]); LT = (r[:, None] < r[None, :])
    shared["masks"] = np.ascontiguousarray(np.stack([LE, GE, GT, LT, np.ones((128, 128), bool), (r[:, None] // 64 == r[None, :] // 64)], axis=1).astype(np.float32))
    sc = np.asarray(inp["ssd_conv"], np.float32)
    shared["ssd_conv_fm"] = np.ascontiguousarray(sc.reshape(2, 4, 4, 128).transpose(3, 0, 2, 1))
    bc = lambda v: np.ascontiguousarray(np.broadcast_to(np.asarray(v, np.float32).reshape(1, 2, 8), (128, 2, 8)))
    shared["ssd_dtb_bc"] = bc(inp["ssd_dt_bias"])
    shared["ssd_alog_bc"] = bc(inp["ssd_a_log"])
    shared["ssd_d_fm"] = _fm(np.repeat(np.asarray(inp["ssd_d"], np.float32), 64, axis=-1), 2)
    shared["ssd_norm_fm"] = _fm(inp["ssd_norm"], 2)
    shared["masks2"] = np.ascontiguousarray(np.stack([np.concatenate([GT, LE], axis=1), np.concatenate([LT, GE], axis=1)], axis=1).astype(np.float32))
    bd = lambda m: (r[:, None] // m == r[None, :] // m)
    shared["masks4"] = np.ascontiguousarray(np.stack([bd(32), bd(64) & ~bd(32), ~bd(64), ~bd(64)], axis=1).astype(np.float32))
    gc = np.asarray(inp["gdn_conv"], np.float32)
    shared["gdn_conv_fm"] = np.ascontiguousarray(gc.reshape(2, 4, 6, 128).transpose(3, 0, 2, 1))
    shared["gdn_dtb_bc"] = bc(inp["gdn_dt_bias"])
    shared["gdn_alog_bc"] = bc(inp["gdn_a_log"])
    shared["gdn_norm_bc"] = np.ascontiguousarray(np.broadcast_to(np.asarray(inp["gdn_norm"], np.float32).reshape(1, 2, 64), (128, 2, 64)))
    import ml_dtypes
    hc = np.asarray(inp["hy_conv"], np.float32)
    shared["hy_conv_fm"] = np.ascontiguousarray(hc.reshape(2, 3, 6, 128).transpose(3, 0, 2, 1))
    for n in ("hy_w1", "hy_w2", "hy_w3"):
        shared[n] = f(inp[n])
    hv = np.stack([np.asarray(inp["hy_b1"], np.float32), np.asarray(inp["hy_freq"], np.float32)[:, 0],
                   np.asarray(inp["hy_b2"], np.float32), np.asarray(inp["hy_freq"], np.float32)[:, 1]], axis=-1)
    shared["hy_vec"] = np.ascontiguousarray(hv.transpose(1, 0, 2))
    shared["hy_bias_fm"] = _fm(inp["hy_bias"], 2)
    for L_ in (256, 1024):
        t = np.linspace(0.0, 1.0, L_, dtype=np.float32)[:, None]
        w_ = (np.float32(2.0 * math.pi / L_) * np.arange(L_, dtype=np.float32))[:, None]
        bands = np.linspace(1e-4, 15, 16, dtype=np.float32)[None, :]
        feats = np.concatenate([t, np.cos(bands * w_), -np.sin(bands * w_)], axis=-1).astype(np.float32)
        shared["hy_featsT_%d" % L_] = np.ascontiguousarray(feats.T)
        max_decay = math.log(1e-2) / 0.3
        min_decay = math.log(1e-2) / 1.5
        deltas = np.abs(np.linspace(min_decay, max_decay, 256, dtype=np.float32))
        shared["hy_env_%d" % L_] = np.exp(-t * deltas).astype(np.float32)
        n_ = 2 * L_
        idx = np.arange(L_, dtype=np.float64)
        th = 2.0 * np.pi * np.outer(idx, idx + 0.5) / n_
        Cm, Sm = np.cos(th), np.sin(th)
        F = np.stack([Cm.reshape(L_, L_ // 128, 128), Sm.reshape(L_, L_ // 128, 128)], axis=2)
        shared["dftF_%d" % L_] = np.ascontiguousarray(F).astype(ml_dtypes.bfloat16)
        Iv = np.stack([Cm.T, Sm.T], axis=1) * (2.0 / n_)
        shared["dftI_%d" % L_] = np.ascontiguousarray(Iv).astype(ml_dtypes.bfloat16)
    maps = []
    for i in range(8):
        m = dict(shared)
        m["x_ctx"] = f(inp["x_prompt"][4 * i:4 * i + 4]).reshape(1024, 1024)
        m["x_lat"] = f(inp["x_sample"][i])
        cv = np.stack([np.asarray(inp["c_ctx"], np.float32), np.asarray(inp["c"][i], np.float32)], axis=0)
        m["cv"] = _fm(cv, 8)
        m["st_lru_fm"] = _fm(inp["state_lru"][i], 2)
        m["st_ssd"] = f(inp["state_ssd"][i])
        m["st_gdn"] = f(inp["state_gdn"][i])
        maps.append(m)
    return maps


def kernel(**inputs):
    maps = make_inputs(inputs)
    nc, _k = build()
    res = run_bass_kernel_spmd(nc, maps, core_ids=list(range(8)))
    R = res.results
    y_prompt = np.concatenate([np.asarray(R[i]["y_ctx"], np.float32).reshape(4, 256, 1024) for i in range(8)], axis=0)
    y_sample = np.stack([np.asarray(R[i]["y_lat"], np.float32) for i in range(8)], axis=0)
    ns_lru = np.concatenate([np.asarray(R[i]["ns_lru"], np.float32).reshape(4, 2, 2, 256) for i in range(8)], axis=0)
    ns_ssd = np.concatenate([np.asarray(R[i]["ns_ssd"], np.float32) for i in range(8)], axis=0)
    ns_gdn = np.concatenate([np.asarray(R[i]["ns_gdn"], np.float32) for i in range(8)], axis=0)
    return (y_prompt, y_sample, ns_lru, ns_ssd, ns_gdn)
```
